# Optimizing a Trainium2 kernel written in Bass

```python
import jax, jax.numpy as jnp
from jax import lax
import numpy as np

D_MODEL = 2048
BATCH = 2
SEQ = 16384
DEPTH = 2
DEC_BATCH = 8
DEC_SEQ = 16
PAST_LEN = 4096

CHUNK = 64
N_MIXERS = 2
N_RET_LAYERS = (DEPTH + 1) // 2
N_RWKV_LAYERS = DEPTH // 2

RET_HEADS = 8
RET_DK = D_MODEL // RET_HEADS
RET_DV = 2 * RET_DK
RET_QK = RET_HEADS * RET_DK
RET_VDIM = RET_HEADS * RET_DV
ROPE_BASE = 10000.0

RWKV_HEAD = 64
RWKV_HEADS = D_MODEL // RWKV_HEAD
LORA_DECAY = 96
LORA_AAA = 96
LORA_GATE = 256

D_FF = 5632
CONV_W = 3

NORM_EPS = 1e-6
RET_GN_EPS = 1e-5
RWKV_GN_EPS = 64e-5

kernel_name = "retnet_rwkv7_convffn_adaln_stream_step"


def rms_norm(x, eps=NORM_EPS):
    xf = x.astype(jnp.float32)
    return (xf * lax.rsqrt(jnp.mean(xf * xf, -1, keepdims=True) + eps)).astype(x.dtype)


def head_norm(x, gain, eps):
    xf = x.astype(jnp.float32)
    mu = jnp.mean(xf, -1, keepdims=True)
    var = jnp.mean(jnp.square(xf - mu), -1, keepdims=True)
    y = ((xf - mu) * lax.rsqrt(var + eps)).reshape(x.shape[:-2] + (-1,))
    return y * gain.astype(jnp.float32)


def rotary(x, pos):
    half = x.shape[-1] // 2
    inv = ROPE_BASE ** (-jnp.arange(half, dtype=jnp.float32) / half)
    ang = pos.astype(jnp.float32)[:, None] * inv[None, :]
    cos = jnp.cos(ang)[None, :, None, :]
    sin = jnp.sin(ang)[None, :, None, :]
    x1 = x[..., :half].astype(jnp.float32)
    x2 = x[..., half:].astype(jnp.float32)
    return jnp.concatenate([x1 * cos - x2 * sin, x1 * sin + x2 * cos], -1).astype(x.dtype)


def retention_mixer(h, state, pos0, w_in, w_out, gn_gain):
    B, T, _ = h.shape
    proj = h @ w_in
    q, k, v, g = jnp.split(proj, [RET_QK, 2 * RET_QK, 2 * RET_QK + RET_VDIM], axis=-1)
    pos = pos0 + jnp.arange(T)
    q = rotary(q.reshape(B, T, RET_HEADS, RET_DK), pos)
    k = rotary(k.reshape(B, T, RET_HEADS, RET_DK), pos) * (RET_DK ** -0.5)
    v = v.reshape(B, T, RET_HEADS, RET_DV)
    L = min(CHUNK, T)
    n_chunks = T // L
    log_g = jnp.log1p(-(2.0 ** (-5.0 - jnp.arange(RET_HEADS, dtype=jnp.float32))))
    idx = jnp.arange(L, dtype=jnp.float32)
    diff = idx[:, None] - idx[None, :]
    intra = jnp.where(diff >= 0, jnp.exp(log_g[:, None, None] * jnp.maximum(diff, 0.0)), 0.0)
    cross = jnp.exp(log_g[:, None] * (idx[None, :] + 1.0))[None, :, :, None]
    into = jnp.exp(log_g[:, None] * (L - 1.0 - idx[None, :]))[None, :, :, None]
    chunk_decay = jnp.exp(log_g * L)[None, :, None, None]

    def to_chunks(t):
        return t.reshape(B, n_chunks, L, RET_HEADS, -1).transpose(1, 0, 3, 2, 4)

    def step(S, inp):
        qi, ki, vi = inp
        scores = jnp.einsum('bhld,bhmd->bhlm', qi, ki) * intra
        o = (jnp.einsum('bhlm,bhmv->bhlv', scores, vi)
             + jnp.einsum('bhld,bhdv->bhlv', qi, S) * cross)
        S = S * chunk_decay + jnp.einsum('bhmd,bhmv->bhdv', ki * into, vi)
        return S, o.astype(jnp.float32)

    S, o = lax.scan(step, state.astype(jnp.float32), (to_chunks(q), to_chunks(k), to_chunks(v)))
    o = o.transpose(1, 0, 3, 2, 4).reshape(B, T, RET_HEADS, RET_DV)
    y = head_norm(o, gn_gain, RET_GN_EPS) * jax.nn.silu(g.astype(jnp.float32))
    return (y.astype(h.dtype) @ w_out).astype(h.dtype), S


def rwkv7_mixer(h, shift_state, wkv_state, mu, w_r, w_k, w_v, w_o, w0, w1, w2,
                a0, a1, a2, g1, g2, k_k, k_a, r_k, gn_gain):
    B, T, _ = h.shape
    H, N = RWKV_HEADS, RWKV_HEAD
    prev = jnp.concatenate([shift_state[:, None, :].astype(h.dtype), h[:, :-1]], axis=1)
    xx = prev - h
    xr, xw, xk, xv, xa, xg = [h + xx * mu[n] for n in range(6)]
    r = xr @ w_r
    k = xk @ w_k
    v = xv @ w_v
    w_log = -jax.nn.softplus(-(w0 + jnp.tanh(xw @ w1) @ w2).astype(jnp.float32)) - 0.5
    decay = jnp.exp(-jnp.exp(w_log))
    a = jax.nn.sigmoid((a0 + (xa @ a1) @ a2).astype(jnp.float32))
    g = jax.nn.sigmoid(xg @ g1) @ g2
    kk = (k * k_k).astype(jnp.float32).reshape(B, T, H, N)
    kk = kk / jnp.maximum(jnp.sqrt(jnp.sum(kk * kk, -1, keepdims=True)), 1e-12)
    k = k.astype(jnp.float32) * (1.0 + (a - 1.0) * k_a.astype(jnp.float32))

    def heads(t):
        return t.astype(jnp.float32).reshape(B, T, H, N)

    r4, k4, v4, a4, d4 = heads(r), heads(k), heads(v), heads(a), heads(decay)
    b4 = kk * a4

    def tm(t):
        return t.transpose(1, 0, 2, 3)

    def step(S, inp):
        r_t, d_t, k_t, v_t, kk_t, b_t = inp
        sa = jnp.einsum('bhij,bhj->bhi', S, -kk_t)
        S = S * d_t[:, :, None, :] + sa[..., None] * b_t[:, :, None, :] + v_t[..., None] * k_t[:, :, None, :]
        return S, jnp.einsum('bhij,bhj->bhi', S, r_t)

    S, o = lax.scan(step, wkv_state.astype(jnp.float32),
                    (tm(r4), tm(d4), tm(k4), tm(v4), tm(kk), tm(b4)))
    o = o.transpose(1, 0, 2, 3)
    y = head_norm(o, gn_gain, RWKV_GN_EPS)
    bonus = jnp.sum(r4 * k4 * r_k.astype(jnp.float32), -1, keepdims=True) * v4
    y = (y + bonus.reshape(B, T, D_MODEL)) * g.astype(jnp.float32)
    return (y.astype(h.dtype) @ w_o).astype(h.dtype), S, h[:, -1]


def conv_ffn(h, conv_state, w_gate, w_up, conv_w, conv_b, w_down):
    T = h.shape[1]
    u = h @ w_gate
    up = h @ w_up
    ext = jnp.concatenate([conv_state.astype(u.dtype), u], axis=1)
    conv = conv_b + sum(ext[:, j:j + T] * conv_w[j] for j in range(CONV_W))
    y = (jax.nn.silu(conv) * up) @ w_down
    return y.astype(h.dtype), ext[:, -(CONV_W - 1):]


def run_group(x, c, pos0, st_ret, st_wkv, st_shift, st_conv, p):
    new_ret, new_wkv, new_shift, new_conv = [], [], [], []
    for i in range(DEPTH):
        mod = (jax.nn.silu(c) @ p["ada_w"][i] + p["ada_b"][i]).reshape(c.shape[0], 1, 6, D_MODEL)
        shift_m, scale_m, gate_m, shift_f, scale_f, gate_f = [mod[:, :, n] for n in range(6)]
        h = rms_norm(x) * (1.0 + scale_m) + shift_m
        j = i // N_MIXERS
        if i % N_MIXERS == 0:
            y, s_ret = retention_mixer(h, st_ret[j], pos0, p["ret_w_in"][j], p["ret_w_out"][j],
                                       p["ret_gn_gain"][j])
            new_ret.append(s_ret)
        else:
            y, s_wkv, s_shift = rwkv7_mixer(
                h, st_shift[j], st_wkv[j], p["rwkv_mu"][j], p["rwkv_w_r"][j], p["rwkv_w_k"][j],
                p["rwkv_w_v"][j], p["rwkv_w_o"][j], p["rwkv_w0"][j], p["rwkv_w1"][j], p["rwkv_w2"][j],
                p["rwkv_a0"][j], p["rwkv_a1"][j], p["rwkv_a2"][j], p["rwkv_g1"][j], p["rwkv_g2"][j],
                p["rwkv_k_k"][j], p["rwkv_k_a"][j], p["rwkv_r_k"][j], p["rwkv_gn_gain"][j])
            new_wkv.append(s_wkv)
            new_shift.append(s_shift)
        x = x + gate_m * y
        h = rms_norm(x) * (1.0 + scale_f) + shift_f
        y, s_conv = conv_ffn(h, st_conv[i], p["ffn_w_gate"][i], p["ffn_w_up"][i], p["ffn_conv_w"][i],
                             p["ffn_conv_b"][i], p["ffn_w_down"][i])
        new_conv.append(s_conv)
        x = x + gate_f * y
    out = rms_norm(x) * p["final_gain"]
    return out, jnp.stack(new_ret), jnp.stack(new_wkv), jnp.stack(new_shift), jnp.stack(new_conv)


def setup_inputs(seed: int = 0) -> dict:
    key = jax.random.key(seed)
    ks = iter(jax.random.split(key, 48))
    f32 = jnp.float32

    def nrm(shape, scale):
        return jax.random.normal(next(ks), shape, f32) * scale

    def uni(shape, lo, hi):
        return jax.random.uniform(next(ks), shape, f32, lo, hi)

    D, F = D_MODEL, D_FF
    R, W = N_RET_LAYERS, N_RWKV_LAYERS
    return {
        "x_prompt": nrm((BATCH, SEQ, D), 1.0),
        "x_sample": nrm((DEC_BATCH, DEC_SEQ, D), 1.0),
        "c_prompt": nrm((BATCH, D), 1.0),
        "c_sample": nrm((DEC_BATCH, D), 1.0),
        "state_ret": nrm((R, DEC_BATCH, RET_HEADS, RET_DK, RET_DV), 0.05),
        "state_rwkv_wkv": nrm((W, DEC_BATCH, RWKV_HEADS, RWKV_HEAD, RWKV_HEAD), 0.1),
        "state_rwkv_shift": nrm((W, DEC_BATCH, D), 1.0),
        "state_ffn_conv": nrm((DEPTH, DEC_BATCH, CONV_W - 1, F), 1.0),
        "ada_w": nrm((DEPTH, D, 6 * D), 0.5 * D ** -0.5),
        "ada_b": nrm((DEPTH, 6 * D), 0.02),
        "ret_w_in": nrm((R, D, 2 * RET_QK + 2 * RET_VDIM), D ** -0.5),
        "ret_w_out": nrm((R, RET_VDIM, D), RET_VDIM ** -0.5),
        "ret_gn_gain": 1.0 + nrm((R, RET_VDIM), 0.02),
        "rwkv_mu": uni((W, 6, D), 0.0, 1.0),
        "rwkv_w_r": nrm((W, D, D), D ** -0.5),
        "rwkv_w_k": nrm((W, D, D), D ** -0.5),
        "rwkv_w_v": nrm((W, D, D), D ** -0.5),
        "rwkv_w_o": nrm((W, D, D), D ** -0.5),
        "rwkv_w0": uni((W, D), -6.5, -1.5),
        "rwkv_w1": nrm((W, D, LORA_DECAY), D ** -0.5),
        "rwkv_w2": nrm((W, LORA_DECAY, D), 0.1 * LORA_DECAY ** -0.5),
        "rwkv_a0": nrm((W, D), 0.1),
        "rwkv_a1": nrm((W, D, LORA_AAA), D ** -0.5),
        "rwkv_a2": nrm((W, LORA_AAA, D), 0.3 * LORA_AAA ** -0.5),
        "rwkv_g1": nrm((W, D, LORA_GATE), D ** -0.5),
        "rwkv_g2": nrm((W, LORA_GATE, D), LORA_GATE ** -0.5),
        "rwkv_k_k": 0.85 + nrm((W, D), 0.02),
        "rwkv_k_a": 1.0 + nrm((W, D), 0.02),
        "rwkv_r_k": nrm((W, RWKV_HEADS, RWKV_HEAD), 0.1),
        "rwkv_gn_gain": 1.0 + nrm((W, D), 0.02),
        "ffn_w_gate": nrm((DEPTH, D, F), D ** -0.5),
        "ffn_w_up": nrm((DEPTH, D, F), D ** -0.5),
        "ffn_conv_w": nrm((DEPTH, CONV_W, F), CONV_W ** -0.5),
        "ffn_conv_b": nrm((DEPTH, F), 0.02),
        "ffn_w_down": nrm((DEPTH, F, D), F ** -0.5),
        "final_gain": 1.0 + nrm((D,), 0.02),
    }


def reference(x_prompt, x_sample, c_prompt, c_sample, state_ret, state_rwkv_wkv, state_rwkv_shift,
              state_ffn_conv, ada_w, ada_b, ret_w_in, ret_w_out, ret_gn_gain, rwkv_mu, rwkv_w_r,
              rwkv_w_k, rwkv_w_v, rwkv_w_o, rwkv_w0, rwkv_w1, rwkv_w2, rwkv_a0, rwkv_a1, rwkv_a2,
              rwkv_g1, rwkv_g2, rwkv_k_k, rwkv_k_a, rwkv_r_k, rwkv_gn_gain, ffn_w_gate, ffn_w_up,
              ffn_conv_w, ffn_conv_b, ffn_w_down, final_gain):
    p = dict(ada_w=ada_w, ada_b=ada_b, ret_w_in=ret_w_in, ret_w_out=ret_w_out, ret_gn_gain=ret_gn_gain,
             rwkv_mu=rwkv_mu, rwkv_w_r=rwkv_w_r, rwkv_w_k=rwkv_w_k, rwkv_w_v=rwkv_w_v, rwkv_w_o=rwkv_w_o,
             rwkv_w0=rwkv_w0, rwkv_w1=rwkv_w1, rwkv_w2=rwkv_w2, rwkv_a0=rwkv_a0, rwkv_a1=rwkv_a1,
             rwkv_a2=rwkv_a2, rwkv_g1=rwkv_g1, rwkv_g2=rwkv_g2, rwkv_k_k=rwkv_k_k, rwkv_k_a=rwkv_k_a,
             rwkv_r_k=rwkv_r_k, rwkv_gn_gain=rwkv_gn_gain, ffn_w_gate=ffn_w_gate, ffn_w_up=ffn_w_up,
             ffn_conv_w=ffn_conv_w, ffn_conv_b=ffn_conv_b, ffn_w_down=ffn_w_down, final_gain=final_gain)
    z_ret = jnp.zeros((N_RET_LAYERS, BATCH, RET_HEADS, RET_DK, RET_DV), jnp.float32)
    z_wkv = jnp.zeros((N_RWKV_LAYERS, BATCH, RWKV_HEADS, RWKV_HEAD, RWKV_HEAD), jnp.float32)
    z_shift = jnp.zeros((N_RWKV_LAYERS, BATCH, D_MODEL), x_prompt.dtype)
    z_conv = jnp.zeros((DEPTH, BATCH, CONV_W - 1, D_FF), x_prompt.dtype)
    y_prompt, p_ret, p_wkv, p_shift, p_conv = run_group(
        x_prompt, c_prompt, 0, z_ret, z_wkv, z_shift, z_conv, p)
    y_sample, s_ret, s_wkv, s_shift, s_conv = run_group(
        x_sample, c_sample, PAST_LEN, state_ret, state_rwkv_wkv, state_rwkv_shift, state_ffn_conv, p)
    return (y_prompt, y_sample, p_ret, p_wkv, p_shift, p_conv, s_ret, s_wkv, s_shift, s_conv)
```

```python
import contextlib
import math
import numpy as np
import concourse.bass as bass
import concourse.mybir as mybir
from concourse.bass_utils import run_bass_kernel_spmd

F32 = mybir.dt.float32
BF16 = mybir.dt.bfloat16
ALU = mybir.AluOpType
AF = mybir.ActivationFunctionType
AX = mybir.AxisListType

D = 2048
KC = 16
SEQ = 16384
NT = 256
NTILE = SEQ // NT
TS = 16
PAST = 4096
RH, DK, DV = 8, 256, 512
FF = 5632
FC = 44
HN = 64
GRAN = 512
SAME_ENGINE_SYNC = True
NDMA_SEM = 12
STOP_AFTER = 99
import os
RW_STAGE = int(os.environ.get('RW_STAGE', '9'))
NCORES = 2
NSP = 8 // NCORES
NROW = 1 + NSP


class V:
    def __init__(self, ap, keys):
        self.ap = ap
        self.keys = keys

    def __getitem__(self, idx):
        return V(self.ap[idx], self.keys)


class Sched:
    ENGS = ("pe", "act", "dve", "pool", "sp")

    def __init__(self, nc, stack):
        self.nc = nc
        self.stack = stack
        self.sem = {e: stack.enter_context(nc.semaphore("s_" + e)) for e in ("pe", "act", "dve", "pool")}
        self.dsem = {q: [stack.enter_context(nc.semaphore("d_%s%d" % (q, i))) for i in range(NDMA_SEM)]
                     for q in ("sp", "pool", "act")}
        self.dcnt = {q: 0 for q in ("sp", "pool", "act")}
        self.cnt = {e: 0 for e in ("pe", "act", "dve", "pool")}
        self.waited = {e: {} for e in self.ENGS}
        self.ops = {e: [] for e in self.ENGS}
        self.lastw = {}
        self.readers = {}
        self.nops = 0

    @staticmethod
    def _keys(vs):
        out = []
        for v in vs:
            if isinstance(v, V):
                out.extend(v.keys)
            else:
                out.append(v)
        return out

    def _deps(self, rk, wk):
        deps = []
        lw, rd = self.lastw, self.readers
        for k in rk:
            w = lw.get(k)
            if w is not None:
                deps.append(w)
        for k in wk:
            w = lw.get(k)
            if w is not None:
                deps.append(w)
            r = rd.get(k)
            if r:
                deps.extend(r.values())
        return deps

    def _mark(self, eng, tok, rk, wk):
        rd = self.readers
        for k in rk:
            d = rd.get(k)
            if d is None:
                rd[k] = {eng: tok}
            else:
                d[eng] = tok
        for k in wk:
            self.lastw[k] = tok
            rd[k] = None

    def _waits(self, eng, deps, is_dma):
        waits = []
        wd = self.waited[eng]
        for (sem, val, seng) in deps:
            if seng == eng and not is_dma:
                if eng == "pe" or not SAME_ENGINE_SYNC:
                    continue
            if wd.get(id(sem), 0) >= val:
                continue
            wd[id(sem)] = val
            waits.append((sem, val))
        return waits

    def op(self, eng, fn, reads=(), writes=()):
        rk, wk = self._keys(reads), self._keys(writes)
        waits = self._waits(eng, self._deps(rk, wk), False)
        self.cnt[eng] += 1
        tok = (self.sem[eng], self.cnt[eng], eng)
        self.ops[eng].append((waits, fn, self.sem[eng], 1))
        self._mark(eng, tok, rk, wk)
        self.nops += 1

    def dma(self, q, fn, reads=(), writes=()):
        rk, wk = self._keys(reads), self._keys(writes)
        deps = self._deps(rk, wk)
        m = self.dcnt[q]
        self.dcnt[q] += 1
        slot, rnd = m % NDMA_SEM, m // NDMA_SEM
        sem = self.dsem[q][slot]
        if rnd > 0:
            deps.append((sem, 16 * rnd, "dma"))
        deps = [(s, v, "x") if e == q else (s, v, e) for (s, v, e) in deps]
        waits = self._waits(q, deps, True)
        tok = (sem, 16 * (rnd + 1), "dma_" + q)
        self.ops[q].append((waits, fn, sem, 16))
        self._mark("dma_" + q + str(slot), tok, rk, wk)
        self.nops += 1

    def state(self):
        import copy
        return dict(cnt=dict(self.cnt), dcnt=dict(self.dcnt), waited={e: dict(d) for e, d in self.waited.items()},
                    lastw=dict(self.lastw), readers={k: (dict(v) if v else v) for k, v in self.readers.items()},
                    lens={e: len(v) for e, v in self.ops.items()}, nops=self.nops)

    def restore(self, st):
        self.cnt, self.dcnt = dict(st["cnt"]), dict(st["dcnt"])
        self.waited = {e: dict(d) for e, d in st["waited"].items()}
        self.lastw = dict(st["lastw"])
        self.readers = {k: (dict(v) if v else v) for k, v in st["readers"].items()}
        for e in self.ENGS:
            del self.ops[e][st["lens"][e]:]
        self.nops = st["nops"]

    def deltas(self, st0, st1):
        d = {}
        for e in ("pe", "act", "dve", "pool"):
            d[id(self.sem[e])] = st1["cnt"][e] - st0["cnt"][e]
        for q in ("sp", "pool", "act"):
            m = st1["dcnt"][q] - st0["dcnt"][q]
            assert m % NDMA_SEM == 0, (q, m)
            for sm in self.dsem[q]:
                d[id(sm)] = 16 * (m // NDMA_SEM)
        return d

    def norm_tile(self, st0, st1, k, delta):
        out = {}
        for e in self.ENGS:
            out[e] = [tuple((id(s_), v - k * delta[id(s_)]) for (s_, v) in w[0])
                      for w in self.ops[e][st0["lens"][e]:st1["lens"][e]]]
        return out

    def set_loop(self, st0, st1, k, n_end, delta):
        self.loop = dict(lo=st0["lens"], hi=st1["lens"], k=k, n=n_end, delta=delta)
        extra_iters = n_end - 1 - k
        for e in ("pe", "act", "dve", "pool"):
            self.cnt[e] += extra_iters * (st1["cnt"][e] - st0["cnt"][e])
        for q in ("sp", "pool", "act"):
            self.dcnt[q] += extra_iters * (st1["dcnt"][q] - st0["dcnt"][q])

    loop = None
    loop_var = None

    def emit(self, block):
        ops = self.ops
        self.ops = {e: [] for e in self.ENGS}
        loop = self.loop
        self.loop = None
        sched = self

        def body(ename, lst, extra):
            def run(eng, sub, base, scratch):
                for (waits, fn, sem, inc) in sub:
                    for (s, v) in waits:
                        dl = loop["delta"][id(s)] if base is not None else 0
                        if dl:
                            eng.reg_add(scratch, base[dl], v)
                            eng.wait_ge(s, scratch)
                        else:
                            eng.wait_ge(s, v)
                    fn(eng).then_inc(sem, inc)

            def _f(eng):
                if loop is None:
                    run(eng, lst, None, None)
                else:
                    lo, hi = loop["lo"][ename], loop["hi"][ename]
                    run(eng, lst[:lo], None, None)
                    if hi > lo:
                        with contextlib.ExitStack() as rs:
                            dls = sorted({loop["delta"][id(s_)] for (w, _, _, _) in lst[lo:hi] for (s_, _) in w} - {0})
                            base = {d_: rs.enter_context(eng.register("lb_%s_%d" % (ename, j))) for j, d_ in enumerate(dls)}
                            scratch = rs.enter_context(eng.register("ls_%s" % ename))
                            for d_ in dls:
                                eng.reg_mov(base[d_], 0)
                            with eng.Fori(loop["k"], loop["n"]) as i:
                                sched.loop_var = i
                                run(eng, lst[lo:hi], base, scratch)
                                sched.loop_var = None
                                for d_ in dls:
                                    eng.reg_add(base[d_], base[d_], d_)
                    run(eng, lst[hi:], None, None)
                for (s, v) in extra:
                    eng.wait_ge(s, v)
            return _f

        extra = {e: [] for e in self.ENGS}
        for q in ("sp", "pool", "act"):
            m = self.dcnt[q]
            for slot in range(min(m, NDMA_SEM)):
                n = (m - slot + NDMA_SEM - 1) // NDMA_SEM
                extra[q].append((self.dsem[q][slot], 16 * n))
        block.tensor(body("pe", ops["pe"], extra["pe"]))
        block.scalar(body("act", ops["act"], extra["act"]))
        block.vector(body("dve", ops["dve"], extra["dve"]))
        block.gpsimd(body("pool", ops["pool"], extra["pool"]))
        block.sync(body("sp", ops["sp"], extra["sp"]))
        self.lastw = {}
        self.readers = {}


class Arena:
    def __init__(self, nc, stack, nbytes):
        self.nbytes = nbytes
        self.t = stack.enter_context(nc.sbuf_tensor("arena", [128, nbytes // 4], F32))
        self.top = 0

    def alloc(self, nbytes):
        lo = self.top
        self.top = lo + ((nbytes + GRAN - 1) // GRAN) * GRAN
        assert self.top <= self.nbytes, ("arena overflow", self.top)
        return lo

    def view(self, lo, dt, shape, np_=128):
        esz = 2 if dt == BF16 else 4
        n = int(np.prod(shape))
        nb = n * esz
        ap = self.t[0:np_, lo // 4:(lo + nb + 3) // 4]
        if dt == BF16:
            ap = ap.bitcast(BF16)
        if len(shape) == 2:
            ap = ap.rearrange("p (a b) -> p a b", b=shape[1])
        elif len(shape) == 3:
            ap = ap.rearrange("p (a b c) -> p a b c", b=shape[1], c=shape[2])
        keys = [("A", g) for g in range(lo // GRAN, (lo + nb - 1) // GRAN + 1)]
        return V(ap, keys)

    def tile(self, dt, shape, np_=128):
        esz = 2 if dt == BF16 else 4
        lo = self.alloc(int(np.prod(shape)) * esz)
        return self.view(lo, dt, shape, np_), lo


WSPEC = {
    "ret_in": (16, 256, 48),
    "ret_out": (32, 128, 16),
    "gu0": (16, 256, 44), "gu1": (16, 256, 44),
    "dn0": (44, 128, 16), "dn1": (44, 128, 16),
    "rkv": (32, 128, 48),
    "lora1": (32, 128, 4),
    "w_o": (16, 256, 8),
}


def _pieces(w, fw):
    K, F = w.shape
    return np.ascontiguousarray(w.reshape(K // 128, 128, F // fw, fw).transpose(2, 1, 0, 3)).reshape(F // fw, 128, (K // 128) * fw)


def _fm(v):
    return np.ascontiguousarray(v.reshape(-1, 128).T)


def build_program():
    nc = bass.Bass("TRN2", target_bir_lowering=False)
    dr = lambda n, sh, dt=F32, kind="ExternalInput": nc.dram_tensor(n, sh, dt, kind=kind)
    x_in = dr("x_in", [NTILE, 128, KC * NT])
    xs_in = dr("xs_in", [NSP, 128, KC * TS])
    cvec = dr("cvec", [128, KC * NROW])
    ada_w = dr("ada_w", [2 * 96, 128, KC * 128])
    win = {k: dr("w_" + k, [n, 128, kc * fw]) for k, (kc, fw, n) in WSPEC.items()}
    l2w = dr("l2w", [128, 3 * D])
    g2w = dr("g2w", [128, 2 * D])
    prm = dr("prm", [128, NPRM])
    cst = dr("cst", [128, NCST])
    rope = dr("rope", [NTILE, 128, 2 * NT])
    rope_s = dr("rope_s", [128, 2 * TS])
    st_ret = dr("st_ret", [NSP, 128, RH * 2 * DV])
    st_wkv = dr("st_wkv", [NSP, 128, 16 * 128])
    st_shift = dr("st_shift", [NSP, 128, KC])
    st_conv = dr("st_conv", [NSP, 128, 2 * FC * 2])
    y_out = dr("y_out", [NTILE, 128, KC * NT], kind="ExternalOutput")
    ys_out = dr("ys_out", [NSP, 128, KC * TS], kind="ExternalOutput")
    o_ret = dr("o_ret", [NROW, 128, RH * 2 * DV], kind="ExternalOutput")
    o_wkv = dr("o_wkv", [NROW, 128, 16 * 128], kind="ExternalOutput")
    o_shift = dr("o_shift", [NROW, 128, KC], kind="ExternalOutput")
    o_conv = dr("o_conv", [NROW, 128, 2 * FC * 2], kind="ExternalOutput")
    wscr = {k: nc.dram_tensor("ws_" + k, [n, 128, kc * fw], BF16) for k, (kc, fw, n) in WSPEC.items()}
    l2scr = nc.dram_tensor("ws_l2", [128, 3 * D], BF16)
    g2scr = nc.dram_tensor("ws_g2", [128, 2 * D], BF16)

    with contextlib.ExitStack() as st:
        S = Sched(nc, st)
        AR = Arena(nc, st, 188 * 1024)
        PS = [V(st.enter_context(nc.psum_tensor("ps%d" % i, [128, 512], F32))[:], ["ps%d" % i]) for i in range(8)]
        psi = [0]

        def psum():
            psi[0] = (psi[0] + 1) % 8
            return PS[psi[0]]

        def mm(out, lhsT, rhs, start, stop):
            S.op("pe", lambda e: e.matmul(out.ap, lhsT=lhsT.ap, rhs=rhs.ap, start=start, stop=stop),
                 reads=[lhsT, rhs], writes=[out])

        def tr(out, in_, ident):
            S.op("pe", lambda e: e.transpose(out.ap, in_.ap, ident.ap), reads=[in_, ident], writes=[out])

        def act(out, in_, func, bias=0.0, scale=1.0, extra=()):
            rd = [in_] + [b for b in (bias, scale) if isinstance(b, V)] + list(extra)
            b_ = bias.ap if isinstance(bias, V) else bias
            s_ = scale.ap if isinstance(scale, V) else scale
            S.op("act", lambda e: e.activation(out=out.ap, in_=in_.ap, func=func, bias=b_, scale=s_), reads=rd, writes=[out])

        def tt(out, a, b, op, eng="dve"):
            S.op(eng, lambda e: e.tensor_tensor(out=out.ap, in0=a.ap, in1=b.ap, op=op), reads=[a, b], writes=[out])

        def ts(out, a, s1, s2, op0, op1=None, eng="dve"):
            rd = [a] + [s for s in (s1, s2) if isinstance(s, V)]
            a1 = s1.ap if isinstance(s1, V) else s1
            a2 = s2.ap if isinstance(s2, V) else s2
            if op1 is None:
                S.op(eng, lambda e: e.tensor_scalar(out=out.ap, in0=a.ap, scalar1=a1, scalar2=None, op0=op0), reads=rd, writes=[out])
            else:
                S.op(eng, lambda e: e.tensor_scalar(out=out.ap, in0=a.ap, scalar1=a1, scalar2=a2, op0=op0, op1=op1), reads=rd, writes=[out])

        def stt(out, a, s, b, op0, op1, eng="dve"):
            rd = [a, b] + ([s] if isinstance(s, V) else [])
            s_ = s.ap if isinstance(s, V) else s
            S.op(eng, lambda e: e.scalar_tensor_tensor(out=out.ap, in0=a.ap, scalar=s_, in1=b.ap, op0=op0, op1=op1), reads=rd, writes=[out])

        def cp(out, in_, eng="dve"):
            if eng == "act":
                S.op("act", lambda e: e.copy(out=out.ap, in_=in_.ap), reads=[in_], writes=[out])
            else:
                S.op(eng, lambda e: e.tensor_copy(out=out.ap, in_=in_.ap), reads=[in_], writes=[out])

        def memset(out, val, eng="pool"):
            S.op(eng, lambda e: e.memset(out.ap, val), writes=[out])

        def recip(out, in_):
            S.op("dve", lambda e: e.reciprocal(out=out.ap, in_=in_.ap), reads=[in_], writes=[out])

        def dma(q, out, in_, okeys=None, ikeys=None):
            oa = out.ap if isinstance(out, V) else out
            ia = in_.ap if isinstance(in_, V) else in_
            rd = [in_] if isinstance(in_, V) else list(ikeys or [])
            wr = [out] if isinstance(out, V) else list(okeys or [])
            S.dma(q, lambda e: e.dma_start(out=oa, in_=ia), reads=rd, writes=wr)

        prm_t, _ = AR.tile(F32, (NPRM,))
        cst_t, _ = AR.tile(F32, (NCST,))
        mod_t, _ = AR.tile(F32, (2, 96, NROW))
        sret_t, _ = AR.tile(F32, (RH * 2, DV))
        sretb_t, _ = AR.tile(BF16, (RH * 2, DV))
        swkv_t, _ = AR.tile(F32, (16, 128))
        x_t, _ = AR.tile(F32, (KC, NT))
        shift_t, _ = AR.tile(F32, (KC, 1))
        conv_t, _ = AR.tile(F32, (2, FC, 2))
        WB = [AR.tile(BF16, (6144,)) for _ in range(2)]
        wbi = [0]
        base_top = AR.top

        P = lambda name: V(prm_t.ap[:, POFF[name][0]:POFF[name][0] + POFF[name][1]], prm_t.keys)
        C = lambda name: V(cst_t.ap[:, COFF[name][0]:COFF[name][0] + COFF[name][1]], cst_t.keys)
        ident = C("ident")
        ones = C("ones")
        bones = C("bones")

        def wslot():
            wbi[0] = (wbi[0] + 1) % 2
            return WB[wbi[0]]

        def wload(name, piece, dt=BF16, src=None):
            kcn, fw, _ = WSPEC[name] if name in WSPEC else (KC, 128, 0)
            (wv, lo) = wslot()
            v = AR.view(lo, dt, (kcn, fw))
            if src is None:
                dma("sp", v, wscr[name][piece].rearrange("p (a b) -> p a b", b=fw), ikeys=[("ws", name, piece)])
            else:
                dma("sp", v, src.rearrange("p (a b) -> p a b", b=fw))
            return v

        with nc.Block() as block:
            dma("sp", prm_t, prm.ap())
            dma("sp", cst_t, cst.ap())

            m0 = AR.top
            stg = [AR.tile(F32, (6144,)) for _ in range(2)]
            cengs = ["dve", "pool", "act"]
            ci = 0
            mu_v = P("mu")
            for name, (kcn, fw, npc) in WSPEC.items():
                for pc in range(npc):
                    nh = 2 if (kcn * fw * 4 > 24576 or name in ("rkv", "lora1")) else 1
                    hk = kcn // nh
                    for hh in range(nh):
                        (sv, slo) = stg[ci % 2]
                        (bv, blo) = WB[ci % 2]
                        s3 = AR.view(slo, F32, (hk, fw))
                        b3 = AR.view(blo, BF16, (hk, fw))
                        src = win[name][pc].rearrange("p (a b) -> p a b", b=fw)[:, hh * hk:(hh + 1) * hk, :]
                        dst = wscr[name][pc].rearrange("p (a b) -> p a b", b=fw)[:, hh * hk:(hh + 1) * hk, :]
                        dma("sp", s3, src)
                        eng = cengs[ci % 3]
                        mus = None
                        if name == "rkv" and hh == 1:
                            mus = {0: 0, 1: 2, 2: 3}[pc % 3]
                        if name == "lora1" and hh == 1:
                            mus = {0: 1, 1: 4, 2: 5, 3: 5}[pc]
                        if mus is not None:
                            mub = V(mu_v.ap[:, mus * 16:(mus + 1) * 16].unsqueeze(2).to_broadcast([128, 16, fw]), mu_v.keys)
                            tt(b3, s3, mub, ALU.mult, eng="dve" if eng == "act" else eng)
                        else:
                            cp(b3, s3, eng=eng)
                        dma("pool", dst, b3, okeys=[("ws", name, pc)])
                        ci += 1
            for (src_t, dst_t, ncol) in ((l2w, l2scr, 3 * D), (g2w, g2scr, 2 * D)):
                for j in range(ncol // 2048):
                    (sv, slo) = stg[ci % 2]
                    (bv, blo) = WB[ci % 2]
                    s2 = AR.view(slo, F32, (2048,))
                    b2 = AR.view(blo, BF16, (2048,))
                    dma("sp", s2, src_t.ap()[:, j * 2048:(j + 1) * 2048])
                    cp(b2, s2, eng=cengs[ci % 3])
                    dma("pool", dst_t.ap()[:, j * 2048:(j + 1) * 2048], b2, okeys=[("wsl", dst_t.name, j)])
                    ci += 1
            AR.top = m0

            m0 = AR.top
            cv, _ = AR.tile(F32, (KC, NROW))
            dma("sp", cv, cvec.ap().rearrange("p (a b) -> p a b", b=NROW))
            act(cv, cv, AF.Silu)
            for l in range(2):
                for oc in range(96):
                    w = wload("ada", 0, dt=F32, src=ada_w[l * 96 + oc])
                    ps = psum()
                    for kc in range(KC):
                        mm(ps[:, 0:NROW], w[:, kc, :], cv[:, kc, :], kc == 0, kc == KC - 1)
                    bcol = V(prm_t.ap[:, POFF["ada_b"][0] + l * 96 + oc:POFF["ada_b"][0] + l * 96 + oc + 1], prm_t.keys)
                    n = oc // 16
                    if n in (1, 4):
                        ts(mod_t[:, l, oc, :], ps[:, 0:NROW], bcol, 1.0, ALU.add, ALU.add)
                    else:
                        ts(mod_t[:, l, oc, :], ps[:, 0:NROW], bcol, None, ALU.add)
            AR.top = m0

            def modv(l, n, kc, row):
                return mod_t[:, l, n * 16 + kc, row:row + 1]

            def norm_mod(nt, l, nsh, nsc, row, out_bf, last=None):
                m = AR.top
                sq, _ = AR.tile(F32, (nt,))
                rstd, _ = AR.tile(F32, (nt,))
                tmp, _ = AR.tile(F32, (nt,))
                ps = psum()
                for kc in range(KC):
                    act(sq, x_t[:, kc, 0:nt], AF.Square)
                    mm(ps[:, 0:nt], ones, sq, kc == 0, kc == KC - 1)
                act(rstd, ps[:, 0:nt], AF.Sqrt, bias=C("eps6")[:, 0:1], scale=1.0 / D)
                recip(rstd, rstd)
                for kc in range(KC):
                    tt(tmp, x_t[:, kc, 0:nt], rstd, ALU.mult)
                    act(out_bf[:, kc, 0:nt], tmp, AF.Identity, bias=modv(l, nsh, kc, row), scale=modv(l, nsc, kc, row))
                    if last is not None:
                        act(last[:, kc, :], tmp[:, nt - 1:nt], AF.Identity, bias=modv(l, nsh, kc, row), scale=modv(l, nsc, kc, row))
                AR.top = m
                return rstd

            def retention(nt, row, is_sample):
                L = min(128, nt)
                nblk = nt // L
                m = AR.top
                h, _ = AR.tile(BF16, (KC, nt))
                norm_mod(nt, 0, 0, 1, row, h)
                y, _ = AR.tile(BF16, (32, nt))
                qf, _ = AR.tile(F32, (2, nt))
                kf, _ = AR.tile(F32, (2, nt))
                r1, _ = AR.tile(F32, (nt,))
                r2, _ = AR.tile(F32, (nt,))
                qT, _ = AR.tile(BF16, (2, nt))
                qcT, _ = AR.tile(BF16, (2, nt))
                kT, _ = AR.tile(BF16, (2, nt))
                vT, _ = AR.tile(BF16, (4, nt))
                Vt, _ = AR.tile(BF16, (nblk, DV))
                Kt, _ = AR.tile(BF16, (nblk, DK))
                g, _ = AR.tile(F32, (4, nt))
                oh, _ = AR.tile(F32, (4, nt))
                osq, _ = AR.tile(F32, (nt,))
                mean, _ = AR.tile(F32, (nt,))
                rstd, _ = AR.tile(F32, (nt,))
                sT, _ = AR.tile(BF16, (L,))
                cos = ropet[:, 0, 0:nt]
                sin = ropet[:, 1, 0:nt]
                sfx = "s" if is_sample else ""
                maskT = C("rmask" + sfx)
                cross = C("rcross" + sfx)
                into = C("rinto" + sfx)
                identb = identb_t
                for hd in range(RH):
                    gam_L = (1.0 - 2.0 ** (-5.0 - hd)) ** L
                    for pi in range(6):
                        w = wload("ret_in", hd * 6 + pi)
                        for oc2 in range(2):
                            oc = pi * 2 + oc2
                            ps = psum()
                            for kc in range(KC):
                                mm(ps[:, 0:nt], w[:, kc, oc2 * 128:(oc2 + 1) * 128], h[:, kc, :], kc == 0, kc == KC - 1)
                            if oc < 2:
                                cp(qf[:, oc, :], ps[:, 0:nt], eng="act")
                            elif oc < 4:
                                act(kf[:, oc - 2, :], ps[:, 0:nt], AF.Copy, scale=1.0 / 16.0)
                            elif oc < 8:
                                cp(vT[:, oc - 4, :], ps[:, 0:nt], eng="act")
                            else:
                                act(g[:, oc - 8, :], ps[:, 0:nt], AF.Silu)
                    for (src, dst) in ((qf, qT), (kf, kT)):
                        tt(r1, src[:, 0, :], cos, ALU.mult)
                        tt(r2, src[:, 1, :], sin, ALU.mult)
                        tt(dst[:, 0, :], r1, r2, ALU.subtract)
                        tt(r1, src[:, 0, :], sin, ALU.mult, eng="pool")
                        tt(r2, src[:, 1, :], cos, ALU.mult, eng="pool")
                        tt(dst[:, 1, :], r1, r2, ALU.add, eng="pool")
                    for b in range(nblk):
                        cr = cross[:, hd * L:(hd + 1) * L]
                        for dc in range(2):
                            tt(qcT[:, dc, b * L:(b + 1) * L], qT[:, dc, b * L:(b + 1) * L], cr, ALU.mult)
                    for b in range(nblk):
                        ps = psum()
                        psb = V(ps.ap[:, 0:256].bitcast(BF16), ps.keys)
                        for vc in range(4):
                            tr(psb[0:L, vc * 128:(vc + 1) * 128], vT[:, vc, b * L:(b + 1) * L], identb)
                        cp(Vt[0:L, b, :], psb[0:L, :], eng="act")
                        ps = psum()
                        psb = V(ps.ap[:, 0:256].bitcast(BF16), ps.keys)
                        for dc in range(2):
                            tr(psb[0:L, dc * 128:(dc + 1) * 128], kT[:, dc, b * L:(b + 1) * L], identb)
                        ts(Kt[0:L, b, :], psb[0:L, 0:256], into[0:L, hd:hd + 1], None, ALU.mult)
                    for b in range(nblk):
                        cs = slice(b * L, (b + 1) * L)
                        ps = psum()
                        for dc in range(2):
                            mm(ps[0:L, 0:L], kT[:, dc, cs], qT[:, dc, cs], dc == 0, dc == 1)
                        tt(sT[0:L, :], ps[0:L, 0:L], maskT[0:L, hd * L:(hd + 1) * L], ALU.mult)
                        ps = psum()
                        for vc in range(4):
                            o = ps[:, vc * L:(vc + 1) * L]
                            mm(o, Vt[0:L, b, vc * 128:(vc + 1) * 128], sT[0:L, :], True, False)
                            for dc in range(2):
                                mm(o, sretb_t[:, hd * 2 + dc, vc * 128:(vc + 1) * 128], qcT[:, dc, cs], False, dc == 1)
                        for vc in range(4):
                            cp(oh[:, vc, cs], ps[:, vc * L:(vc + 1) * L], eng="act" if vc % 2 else "dve")
                        for dc in range(2):
                            ps = psum()
                            mm(ps[:, 0:DV], Kt[0:L, b, dc * 128:(dc + 1) * 128], Vt[0:L, b, :], True, True)
                            stt(sret_t[:, hd * 2 + dc, :], sret_t[:, hd * 2 + dc, :], gam_L, ps[:, 0:DV], ALU.mult, ALU.add)
                            cp(sretb_t[:, hd * 2 + dc, :], sret_t[:, hd * 2 + dc, :], eng="pool")
                    ps1 = psum()
                    for vc in range(4):
                        mm(ps1[:, 0:nt], ones, oh[:, vc, :], vc == 0, vc == 3)
                    ps2 = psum()
                    for vc in range(4):
                        act(osq, oh[:, vc, :], AF.Square)
                        mm(ps2[:, 0:nt], ones, osq, vc == 0, vc == 3)
                    act(mean, ps1[:, 0:nt], AF.Copy, scale=1.0 / DV)
                    tt(osq, mean, mean, ALU.mult)
                    stt(rstd, ps2[:, 0:nt], 1.0 / DV, osq, ALU.mult, ALU.subtract)
                    act(rstd, rstd, AF.Sqrt, bias=C("eps5")[:, 0:1], scale=1.0)
                    recip(rstd, rstd)
                    gg = P("ret_gn")
                    for vc in range(4):
                        tt(osq, oh[:, vc, :], mean, ALU.subtract)
                        tt(osq, osq, rstd, ALU.mult)
                        stt(y[:, hd * 4 + vc, :], osq, gg[:, hd * 4 + vc:hd * 4 + vc + 1], g[:, vc, :], ALU.mult, ALU.mult)
                for pc in range(16):
                    w = wload("ret_out", pc)
                    ps = psum()
                    for kc in range(32):
                        mm(ps[:, 0:nt], w[:, kc, :], y[:, kc, :], kc == 0, kc == 31)
                    stt(x_t[:, pc, 0:nt], ps[:, 0:nt], modv(0, 2, pc, row), x_t[:, pc, 0:nt], ALU.mult, ALU.add)
                AR.top = m

            def ffn(nt, l, row):
                m = AR.top
                h, _ = AR.tile(BF16, (KC, nt))
                norm_mod(nt, l, 3, 4, row, h)
                a, _ = AR.tile(BF16, (FC, nt))
                ue, _ = AR.tile(F32, (nt + 2,))
                cv1, _ = AR.tile(F32, (nt,))
                cv2, _ = AR.tile(F32, (nt,))
                cw = P("conv_w")
                cb = P("conv_b")
                for fp in range(22):
                    wg = wload("gu%d" % l, 2 * fp)
                    wu = wload("gu%d" % l, 2 * fp + 1)
                    for oc2 in range(2):
                        fc = fp * 2 + oc2
                        psg = psum()
                        for kc in range(KC):
                            mm(psg[:, 0:nt], wg[:, kc, oc2 * 128:(oc2 + 1) * 128], h[:, kc, :], kc == 0, kc == KC - 1)
                        psu = psum()
                        for kc in range(KC):
                            mm(psu[:, 0:nt], wu[:, kc, oc2 * 128:(oc2 + 1) * 128], h[:, kc, :], kc == 0, kc == KC - 1)
                        cp(ue[:, 0:2], conv_t[:, l, fc, :], eng="pool")
                        cp(ue[:, 2:nt + 2], psg[:, 0:nt], eng="act")
                        cp(conv_t[:, l, fc, :], ue[:, nt:nt + 2], eng="pool")
                        wj = lambda j: cw[:, (l * 3 + j) * FC + fc:(l * 3 + j) * FC + fc + 1]
                        ts(cv1, ue[:, 0:nt], wj(0), cb[:, l * FC + fc:l * FC + fc + 1], ALU.mult, ALU.add)
                        stt(cv2, ue[:, 1:nt + 1], wj(1), cv1, ALU.mult, ALU.add)
                        stt(cv1, ue[:, 2:nt + 2], wj(2), cv2, ALU.mult, ALU.add)
                        act(cv2, cv1, AF.Silu)
                        tt(a[:, fc, :], cv2, psu[:, 0:nt], ALU.mult)
                for pc in range(16):
                    w = wload("dn%d" % l, pc)
                    ps = psum()
                    for kc in range(FC):
                        mm(ps[:, 0:nt], w[:, kc, :], a[:, kc, :], kc == 0, kc == FC - 1)
                    stt(x_t[:, pc, 0:nt], ps[:, 0:nt], modv(l, 5, pc, row), x_t[:, pc, 0:nt], ALU.mult, ALU.add)
                AR.top = m

            ropet, _ = AR.tile(F32, (2, NT))
            identb_t, _ = AR.tile(BF16, (128,))
            cp(identb_t, ident)
            base2 = AR.top


            negw0, _ = AR.tile(F32, (16,))
            ts(negw0, P("w0"), -1.0, None, ALU.mult)
            _mo = COFF["mstrict"][0]

            def rwkv(nt, row):
                Cn = min(128, nt)
                nch = nt // Cn
                nlev = int(round(math.log2(Cn)))
                m = AR.top
                h, _ = AR.tile(BF16, (KC, nt))
                xx, _ = AR.tile(BF16, (KC, nt))
                newsh, _ = AR.tile(F32, (KC, 1))
                norm_mod(nt, 1, 0, 1, row, h, last=newsh)
                tt(xx[:, :, 0:1], shift_t, h[:, :, 0:1], ALU.subtract)
                if nt > 1:
                    tt(xx[:, :, 1:nt], h[:, :, 0:nt - 1], h[:, :, 1:nt], ALU.subtract)
                cp(shift_t, newsh, eng="pool")
                y, _ = AR.tile(BF16, (KC, nt))
                lm, _ = AR.tile(BF16, (4, nt))
                l2, _ = AR.tile(BF16, (4, 128))
                F = lambda *sh: AR.tile(F32, sh)[0]
                r_t, k0_t, v_t, e2, asig, g_t, kk, k_t, b_t, bonus, tA, tB = (F(nt) for _ in range(12))
                e2T = F(128)
                cs_, gam, ginv, gprev = F(Cn), F(Cn), F(Cn), F(Cn)
                AR2, BK = F(2, Cn), F(2, Cn)
                AR2f = lambda hs_: V(AR2.ap[hs_].rearrange("p a b -> p (a b)"), AR2.keys)
                Bh, Kh = F(Cn), F(Cn)
                TM = F(4, 128)
                Gb = [F(2 * Cn) for _ in range(2)]
                Gk = [F(2 * Cn) for _ in range(2)]
                g3 = lambda t_: V(t_.ap[0:Cn].rearrange("p (a b) -> p a b", b=Cn), t_.keys)
                Lp = [F(Cn) for _ in range(2)]
                Pp = [F(Cn) for _ in range(2)]
                Xp = [F(128) for _ in range(2)]
                WT, UL, Wfm, UT, Osb, Osq, Yn = F(128), F(128), F(Cn), F(128), F(128), F(128), F(128)
                st1, st2, st3 = F(2), F(2), F(2)
                mask2 = V(cst_t.ap[0:Cn, _mo:_mo + 256].rearrange("p (a b) -> p a b", b=128)[:, :, 0:Cn], cst_t.keys)
                maskT = V(cst_t.ap[0:Cn, COFF["mstrictT"][0]:COFF["mstrictT"][0] + Cn], cst_t.keys)
                mincl = V(cst_t.ap[0:Cn, COFF["mincl"][0]:COFF["mincl"][0] + Cn], cst_t.keys)

                def proj(piece):
                    w = wload("rkv" if piece >= 0 else "lora1", piece if piece >= 0 else -piece - 1)
                    ps = psum()
                    for kc in range(32):
                        rhs = h[:, kc, :] if kc < 16 else xx[:, kc - 16, :]
                        mm(ps[:, 0:nt], w[:, kc, :], rhs, kc == 0, kc == 31)
                    return ps
                ps = proj(-1)
                act(lm[:, 0, :], ps[:, 0:nt], AF.Tanh)
                ps = proj(-2)
                cp(lm[:, 1, :], ps[:, 0:nt], eng="act")
                ps = proj(-3)
                act(lm[:, 2, :], ps[:, 0:nt], AF.Sigmoid)
                ps = proj(-4)
                act(lm[:, 3, :], ps[:, 0:nt], AF.Sigmoid)
                pcol = lambda n, p: V(prm_t.ap[:, POFF[n][0] + p:POFF[n][0] + p + 1], prm_t.keys)
                for p in range(16):
                    psl = slice(p * 128, (p + 1) * 128)
                    dma("sp", l2[:, 0, :], l2scr.ap()[:, p * 128:(p + 1) * 128], ikeys=[("wsl", l2scr.name, 0)])
                    dma("sp", l2[:, 1, :], l2scr.ap()[:, D + p * 128:D + (p + 1) * 128], ikeys=[("wsl", l2scr.name, 1)])
                    dma("sp", l2[:, 2:4, :], g2scr.ap().rearrange("p (a b) -> p a b", b=D)[:, :, p * 128:(p + 1) * 128],
                        ikeys=[("wsl", g2scr.name, 0), ("wsl", g2scr.name, 1)])
                    ps = proj(3 * p + 0)
                    cp(r_t, ps[:, 0:nt], eng="act")
                    ps = proj(3 * p + 1)
                    cp(k0_t, ps[:, 0:nt], eng="act")
                    ps = proj(3 * p + 2)
                    cp(v_t, ps[:, 0:nt], eng="act")
                    ps = psum()
                    mm(ps[:, 0:nt], l2[:, 0, :], lm[:, 0, :], True, True)
                    act(e2, ps[:, 0:nt], AF.Exp, bias=negw0[:, p:p + 1], scale=-1.0)
                    act(e2, e2, AF.Ln, bias=1.0)
                    act(e2, e2, AF.Exp, bias=C("mhalf")[:, 0:1], scale=-1.0)
                    ps = psum()
                    mm(ps[:, 0:nt], l2[:, 1, :], lm[:, 1, :], True, True)
                    act(asig, ps[:, 0:nt], AF.Sigmoid, bias=pcol("a0", p))
                    ps = psum()
                    mm(ps[:, 0:nt], l2[:, 2, :], lm[:, 2, :], True, False)
                    mm(ps[:, 0:nt], l2[:, 3, :], lm[:, 3, :], False, True)
                    cp(g_t, ps[:, 0:nt], eng="act")
                    ts(kk, k0_t, pcol("k_k", p), None, ALU.mult)
                    act(tA, kk, AF.Square)
                    ps = psum()
                    mm(ps[:, 0:nt], bones, tA, True, True)
                    act(tA, ps[:, 0:nt], AF.Sqrt)
                    ts(tA, tA, 1e-12, None, ALU.max)
                    recip(tA, tA)
                    tt(kk, kk, tA, ALU.mult)
                    ts(tA, asig, pcol("k_a", p), pcol("k_a", p), ALU.mult, ALU.subtract)
                    stt(k_t, tA, 1.0, k0_t, ALU.add, ALU.mult)
                    tt(b_t, kk, asig, ALU.mult)
                    stt(tA, r_t, pcol("r_k", p), k_t, ALU.mult, ALU.mult)
                    ps = psum()
                    mm(ps[:, 0:nt], bones, tA, True, True)
                    tt(bonus, ps[:, 0:nt], v_t, ALU.mult)
                    for c in range(nch if RW_STAGE >= 2 else 0):
                        cs = slice(c * Cn, (c + 1) * Cn)
                        ps = psum()
                        tr(ps[0:Cn, 0:128], e2[:, cs], ident)
                        cp(e2T[0:Cn, :], ps[0:Cn, 0:128])
                        ps = psum()
                        mm(ps[:, 0:Cn], e2T[0:Cn, :], mincl, True, True)
                        cp(cs_, ps[:, 0:Cn])
                        act(gam, cs_, AF.Exp, scale=-1.0)
                        act(ginv, cs_, AF.Exp)
                        tt(gprev, cs_, e2[:, cs], ALU.subtract)
                        act(gprev, gprev, AF.Exp, scale=-1.0)
                        stt(AR2[:, 0, :], kk[:, cs], -1.0, gprev, ALU.mult, ALU.mult)
                        tt(AR2[:, 1, :], r_t[:, cs], gam, ALU.mult)
                        tt(BK[:, 0, :], b_t[:, cs], ginv, ALU.mult)
                        tt(BK[:, 1, :], k_t[:, cs], ginv, ALU.mult)
                        gC = gam[:, Cn - 1:Cn]
                        ts(Bh, BK[:, 0, :], gC, None, ALU.mult)
                        ts(Kh, BK[:, 1, :], gC, None, ALU.mult)
                        ps = psum()
                        tr(ps[0:Cn, 0:128], v_t[:, cs], ident)
                        tr(ps[0:Cn, 128:256], Bh, ident)
                        tr(ps[0:Cn, 256:384], Kh, ident)
                        tr(ps[0:Cn, 384:512], AR2[:, 0, :], ident)
                        cp(TM[0:Cn, 0:2, :], V(ps.ap[0:Cn, 0:256].rearrange("p (a b) -> p a b", b=128), ps.keys), eng="act")
                        cp(TM[0:Cn, 2:4, :], V(ps.ap[0:Cn, 256:512].rearrange("p (a b) -> p a b", b=128), ps.keys))
                        Vtm, Bhtm, Khtm, Attm = TM[0:Cn, 0, :], TM[0:Cn, 1, :], TM[0:Cn, 2, :], TM[0:Cn, 3, :]
                        if RW_STAGE < 3:
                            continue
                        for hh in range(2):
                            hs = slice(64 * hh, 64 * hh + 64)
                            ar_h = AR2f(hs)
                            ps = psum()
                            mm(ps[0:Cn, 0:2 * Cn], BK[hs, 0, :], ar_h, True, True)
                            tt(g3(Gb[hh]), V(ps.ap[0:Cn, 0:2 * Cn].rearrange("p (a b) -> p a b", b=Cn), ps.keys), mask2, ALU.mult)
                            ps = psum()
                            mm(ps[0:Cn, 0:2 * Cn], BK[hs, 1, :], ar_h, True, True)
                            tt(g3(Gk[hh]), V(ps.ap[0:Cn, 0:2 * Cn].rearrange("p (a b) -> p a b", b=Cn), ps.keys), mask2, ALU.mult)
                            ps = psum()
                            mm(ps[0:Cn, 0:Cn], AR2[hs, 0, :], BK[hs, 0, :], True, True)
                            tt(Lp[0][0:Cn], ps[0:Cn, 0:Cn], maskT, ALU.mult)
                            Pc = Gb[hh][0:Cn, 0:Cn]
                            Lc = Lp[0][0:Cn]
                            hc = slice(64 * hh, 64 * hh + 64)
                            ps = psum()
                            mm(ps[0:Cn, 0:64], Gk[hh][0:Cn, 0:Cn], Vtm[:, hc], True, True)
                            X = Xp[0]
                            cp(X[0:Cn, 0:64], Attm[:, hc], eng="pool")
                            cp(X[0:Cn, 64:128], ps[0:Cn, 0:64], eng="act")
                            for lv in range(nlev):
                                ps = psum()
                                mm(ps[0:Cn, 0:128], Pc, X[0:Cn], True, True)
                                if lv == nlev - 1:
                                    tt(WT[0:Cn, hc], X[0:Cn, 0:64], ps[0:Cn, 0:64], ALU.add)
                                    tt(UL[0:Cn, hc], X[0:Cn, 64:128], ps[0:Cn, 64:128], ALU.add)
                                else:
                                    Xn = Xp[(lv + 1) % 2]
                                    tt(Xn[0:Cn], X[0:Cn], ps[0:Cn, 0:128], ALU.add)
                                    X = Xn
                                    ps1 = psum()
                                    mm(ps1[0:Cn, 0:Cn], Lc, Pc, True, True)
                                    ps2 = psum()
                                    mm(ps2[0:Cn, 0:Cn], Pc, Lc, True, True)
                                    Pn = Pp[lv % 2][0:Cn]
                                    Ln = Lp[(lv + 1) % 2][0:Cn]
                                    cp(Pn, ps1[0:Cn, 0:Cn], eng="act")
                                    cp(Ln, ps2[0:Cn, 0:Cn])
                                    Pc, Lc = Pn, Ln
                        ps = psum()
                        tr(ps[:, 0:Cn], WT[0:Cn], ident[0:Cn, 0:Cn])
                        cp(Wfm, ps[:, 0:Cn], eng="act")
                        if RW_STAGE < 4:
                            continue
                        for hh in range(2):
                            hs = slice(64 * hh, 64 * hh + 64)
                            ps = psum()
                            mm(ps[0:Cn, 0:64], Wfm[hs, :], swkv_t[hs, p, hs], True, True)
                            tt(UT[0:Cn, hs], ps[0:Cn, 0:64], UL[0:Cn, hs], ALU.add)
                        for hh in range(2):
                            hs = slice(64 * hh, 64 * hh + 64)
                            psa = psum()
                            mm(psa[0:Cn, 0:64], AR2[hs, 1, :], swkv_t[hs, p, hs], True, True)
                            cp(Osb[0:Cn, hs], psa[0:Cn, 0:64], eng="act")
                        ps = psum()
                        for hh in range(2):
                            hs = slice(64 * hh, 64 * hh + 64)
                            mm(ps[0:Cn, hs], Gb[hh][0:Cn, Cn:2 * Cn], UT[0:Cn, hs], True, False)
                            mm(ps[0:Cn, hs], Gk[hh][0:Cn, Cn:2 * Cn], Vtm[:, hs], False, True)
                        tt(Osb[0:Cn], Osb[0:Cn], ps[0:Cn, 0:128], ALU.add)
                        ps = psum()
                        mm(ps[:, 0:128], Bhtm, UT[0:Cn], True, False)
                        mm(ps[:, 0:128], Khtm, Vtm, False, True)
                        stt(swkv_t[:, p, :], swkv_t[:, p, :], gC, ps[:, 0:128], ALU.mult, ALU.add)
                        if RW_STAGE < 5:
                            continue
                        O3 = V(Osb.ap[0:Cn].rearrange("p (a b) -> p a b", b=64), Osb.keys)
                        Q3 = V(Osq.ap[0:Cn].rearrange("p (a b) -> p a b", b=64), Osq.keys)
                        Y3 = V(Yn.ap[0:Cn].rearrange("p (a b) -> p a b", b=64), Yn.keys)
                        S.op("dve", lambda e, o=st1, i=O3: e.tensor_reduce(out=o.ap[0:Cn], in_=i.ap, axis=AX.X, op=ALU.add), reads=[Osb], writes=[st1])
                        act(Osq[0:Cn], Osb[0:Cn], AF.Square)
                        S.op("dve", lambda e, o=st2, i=Q3: e.tensor_reduce(out=o.ap[0:Cn], in_=i.ap, axis=AX.X, op=ALU.add), reads=[Osq], writes=[st2])
                        ts(st1[0:Cn], st1[0:Cn], 1.0 / 64, None, ALU.mult)
                        tt(st3[0:Cn], st1[0:Cn], st1[0:Cn], ALU.mult)
                        stt(st2[0:Cn], st2[0:Cn], 1.0 / 64, st3[0:Cn], ALU.mult, ALU.subtract)
                        act(st2[0:Cn], st2[0:Cn], AF.Sqrt, bias=C("epsw")[0:Cn, 0:1])
                        recip(st2[0:Cn], st2[0:Cn])
                        bc = lambda t_: V(t_.ap[0:Cn].unsqueeze(2).to_broadcast([Cn, 2, 64]), t_.keys)
                        tt(Y3, O3, bc(st1), ALU.subtract)
                        tt(Y3, Y3, bc(st2), ALU.mult)
                        ps = psum()
                        tr(ps[:, 0:Cn], Yn[0:Cn], ident[0:Cn, 0:Cn])
                        stt(tA[:, 0:Cn], ps[:, 0:Cn], pcol("rw_gn", p), bonus[:, cs], ALU.mult, ALU.add)
                        tt(y[:, p, cs], tA[:, 0:Cn], g_t[:, cs], ALU.mult)
                for pc in range(8):
                    w = wload("w_o", pc)
                    for oc2 in range(2):
                        oc = pc * 2 + oc2
                        ps = psum()
                        for kc in range(KC):
                            mm(ps[:, 0:nt], w[:, kc, oc2 * 128:(oc2 + 1) * 128], y[:, kc, :], kc == 0, kc == KC - 1)
                        stt(x_t[:, oc, 0:nt], ps[:, 0:nt], modv(1, 2, oc, row), x_t[:, oc, 0:nt], ALU.mult, ALU.add)
                AR.top = m

            def final_out(nt, dst):
                m = AR.top
                sq, _ = AR.tile(F32, (nt,))
                rstd, _ = AR.tile(F32, (nt,))
                o, _ = AR.tile(F32, (KC, nt))
                ps = psum()
                for kc in range(KC):
                    act(sq, x_t[:, kc, 0:nt], AF.Square)
                    mm(ps[:, 0:nt], ones, sq, kc == 0, kc == KC - 1)
                act(rstd, ps[:, 0:nt], AF.Sqrt, bias=C("eps6")[:, 0:1], scale=1.0 / D)
                recip(rstd, rstd)
                fin = P("fin")
                for kc in range(KC):
                    stt(o[:, kc, :], x_t[:, kc, 0:nt], fin[:, kc:kc + 1], rstd, ALU.mult, ALU.mult)
                S.dma("pool", lambda e: e.dma_start(out=dst(), in_=o.ap), reads=S._keys([o]))
                AR.top = m

            dmy = nc.dram_tensor("dmy", [2, 64], F32)

            def pad_dmas(st0):
                for q in ("sp", "pool"):
                    m = S.dcnt[q] - st0["dcnt"][q]
                    for _ in range((-m) % NDMA_SEM):
                        S.dma(q, lambda e: e.dma_start(out=dmy.ap()[0:1, 0:16], in_=dmy.ap()[1:2, 0:16]))

            def tidx(ti):
                return lambda: (S.loop_var if S.loop_var is not None else ti)

            def init_states(is_sample, si):
                if is_sample:
                    dma("sp", sret_t, st_ret[si].rearrange("p (a b) -> p a b", b=DV))
                    dma("sp", swkv_t, st_wkv[si].rearrange("p (a b) -> p a b", b=128))
                    dma("sp", shift_t, st_shift[si].rearrange("p (a b) -> p a b", b=1))
                    dma("sp", conv_t, st_conv[si].rearrange("p (a b c) -> p a b c", b=FC, c=2))
                else:
                    memset(sret_t, 0.0)
                    memset(swkv_t, 0.0)
                    memset(shift_t, 0.0)
                    memset(conv_t, 0.0)
                cp(sretb_t, sret_t, eng="pool")

            def run_tile(is_sample, si, ti):
                nt = TS if is_sample else NT
                row = 1 + si if is_sample else 0
                psi[0] = 0
                wbi[0] = 0
                if is_sample:
                    dma("sp", x_t[:, :, 0:nt], xs_in[si].rearrange("p (a b) -> p a b", b=nt))
                    dma("sp", ropet[:, :, 0:nt], rope_s.ap().rearrange("p (a b) -> p a b", b=nt))
                else:
                    tv = tidx(ti)
                    S.dma("sp", lambda e: e.dma_start(out=x_t.ap, in_=x_in.ap()[tv()].rearrange("p (a b) -> p a b", b=NT)), writes=S._keys([x_t]))
                    S.dma("sp", lambda e: e.dma_start(out=ropet.ap, in_=rope.ap()[tv()].rearrange("p (a b) -> p a b", b=NT)), writes=S._keys([ropet]))
                retention(nt, row, is_sample)
                ffn(nt, 0, row)
                if STOP_AFTER >= 3:
                    rwkv(nt, row)
                if STOP_AFTER >= 4:
                    ffn(nt, 1, row)
                if is_sample:
                    final_out(nt, lambda: ys_out[si].rearrange("p (a b) -> p a b", b=nt))
                else:
                    tv = tidx(ti)
                    final_out(nt, lambda: y_out.ap()[tv()].rearrange("p (a b) -> p a b", b=NT))

            def write_states(oi):
                dma("pool", o_ret[oi].rearrange("p (a b) -> p a b", b=DV), sret_t)
                dma("pool", o_wkv[oi].rearrange("p (a b) -> p a b", b=128), swkv_t)
                dma("pool", o_shift[oi].rearrange("p (a b) -> p a b", b=1), shift_t)
                dma("pool", o_conv[oi].rearrange("p (a b c) -> p a b c", b=FC, c=2), conv_t)

            for si in range(NSP):
                init_states(True, si)
                run_tile(True, si, 0)
                write_states(1 + si)
            init_states(False, 0)
            n_run = NTILE_RUN
            if n_run <= 4:
                for ti in range(n_run):
                    st0 = S.state()
                    run_tile(False, 0, ti)
                    pad_dmas(st0)
            else:
                k = 0
                prev = None
                while True:
                    st0 = S.state()
                    run_tile(False, 0, k)
                    pad_dmas(st0)
                    st1 = S.state()
                    delta = S.deltas(st0, st1)
                    cur = S.norm_tile(st0, st1, k, delta)
                    if prev is not None and cur == prev[0] and delta == prev[1]:
                        S.restore(st0)
                        S.set_loop(prev[2], st0, k - 1, n_run, delta)
                        print("steady state at tile", k - 1)
                        break
                    prev = (cur, delta, st0)
                    k += 1
                    assert k < 8, "no steady state"
            S.emit(block)
        with nc.Block() as block2:
            write_states(0)
            S.emit(block2)
        print("ops recorded:", S.nops, "arena top", AR.top)
    return nc


POFF = {}
COFF = {}
_o = 0
for _n, _w in (("ada_b", 192), ("ret_gn", 32), ("mu", 96), ("w0", 16), ("a0", 16), ("k_k", 16), ("k_a", 16),
               ("r_k", 16), ("rw_gn", 16), ("conv_w", 2 * 3 * FC), ("conv_b", 2 * FC), ("fin", 16)):
    POFF[_n] = (_o, _w)
    _o += _w
NPRM = _o
_o = 0
for _n, _w in (("ident", 128), ("ones", 128), ("bones", 128), ("eps6", 1), ("eps5", 1), ("epsw", 1), ("mhalf", 1),
               ("rmask", 8 * 128), ("rcross", 8 * 128), ("rinto", 8),
               ("rmasks", 8 * 16), ("rcrosss", 8 * 16), ("rintos", 8),
               ("mstrict", 128), ("mincl", 128), ("mstrictT", 128)):
    COFF[_n] = (_o, _w)
    _o += _w
NCST = _o
NTILE_RUN = NTILE


def _consts():
    c = np.zeros((128, NCST), np.float32)

    def put(n, a):
        o, w = COFF[n]
        c[:a.shape[0], o:o + w] = a
    put("ident", np.eye(128, dtype=np.float32))
    put("ones", np.ones((128, 128), np.float32))
    bo = np.zeros((128, 128), np.float32)
    bo[:64, :64] = 1
    bo[64:, 64:] = 1
    put("bones", bo)
    put("eps6", np.full((128, 1), 1e-6, np.float32))
    put("eps5", np.full((128, 1), 1e-5, np.float32))
    put("epsw", np.full((128, 1), 64e-5, np.float32))
    put("mhalf", np.full((128, 1), -0.5, np.float32))
    gam = 1.0 - 2.0 ** (-5.0 - np.arange(8, dtype=np.float64))
    for sfx, L in (("", 128), ("s", 16)):
        idx = np.arange(L, dtype=np.float64)
        diff = idx[None, :] - idx[:, None]
        mk = np.where(diff >= 0, gam[:, None, None] ** np.maximum(diff, 0)[None], 0.0)
        put("rmask" + sfx, mk.transpose(1, 0, 2).reshape(L, 8 * L).astype(np.float32))
        cr = gam[:, None] ** (idx[None, :] + 1.0)
        put("rcross" + sfx, np.broadcast_to(cr.reshape(1, 8 * L), (128, 8 * L)).astype(np.float32))
        it = gam[:, None] ** (L - 1.0 - idx[None, :])
        put("rinto" + sfx, it.T.astype(np.float32))
    s = np.arange(128)
    put("mstrict", (s[:, None] < s[None, :]).astype(np.float32))
    put("mincl", (s[:, None] <= s[None, :]).astype(np.float32))
    put("mstrictT", (s[:, None] > s[None, :]).astype(np.float32))
    return c


def _rope(pos):
    half = 128
    inv = 10000.0 ** (-np.arange(half, dtype=np.float32) / half)
    ang = pos.astype(np.float32)[None, :] * inv[:, None]
    return np.cos(ang).astype(np.float32), np.sin(ang).astype(np.float32)


_NC_CACHE = {}


def kernel(**inp):
    f = lambda k: np.asarray(inp[k], np.float32)
    if "nc" not in _NC_CACHE:
        _NC_CACHE["nc"] = build_program()
    nc = _NC_CACHE["nc"]
    sh = {}
    aw = f("ada_w")
    sh["ada_w"] = np.concatenate([_pieces(aw[l], 128) for l in range(2)], 0)
    wi = f("ret_w_in")[0]
    cols = []
    for h in range(RH):
        cols += [wi[:, h * DK:(h + 1) * DK], wi[:, 2048 + h * DK:2048 + (h + 1) * DK],
                 wi[:, 4096 + h * DV:4096 + (h + 1) * DV], wi[:, 8192 + h * DV:8192 + (h + 1) * DV]]
    sh["w_ret_in"] = _pieces(np.concatenate(cols, 1), 256)
    sh["w_ret_out"] = _pieces(f("ret_w_out")[0], 128)
    for l in range(2):
        g_ = _pieces(f("ffn_w_gate")[l], 256)
        u_ = _pieces(f("ffn_w_up")[l], 256)
        gu = np.empty((44,) + g_.shape[1:], np.float32)
        gu[0::2] = g_
        gu[1::2] = u_
        sh["w_gu%d" % l] = gu
        sh["w_dn%d" % l] = _pieces(f("ffn_w_down")[l], 128)
    st2 = lambda w: np.concatenate([w, w], 0)
    r_, k_, v_ = (_pieces(st2(f(n)[0]), 128) for n in ("rwkv_w_r", "rwkv_w_k", "rwkv_w_v"))
    rkv = np.empty((48,) + r_.shape[1:], np.float32)
    rkv[0::3], rkv[1::3], rkv[2::3] = r_, k_, v_
    sh["w_rkv"] = rkv
    pad = lambda w: np.concatenate([w, np.zeros((w.shape[0], 128 - w.shape[1]), np.float32)], 1)
    sh["w_lora1"] = np.concatenate([_pieces(st2(pad(f("rwkv_w1")[0])), 128), _pieces(st2(pad(f("rwkv_a1")[0])), 128),
                                    _pieces(st2(f("rwkv_g1")[0]), 128)], 0)
    sh["w_w_o"] = _pieces(f("rwkv_w_o")[0], 256)
    l2 = np.zeros((128, 3 * D), np.float32)
    l2[:96, 0:D] = f("rwkv_w2")[0]
    l2[:96, D:2 * D] = f("rwkv_a2")[0]
    sh["l2w"] = l2
    g2 = f("rwkv_g2")[0]
    sh["g2w"] = np.concatenate([g2[0:128], g2[128:256]], 1)
    prm = np.zeros((128, NPRM), np.float32)

    def putp(n, a):
        o, w = POFF[n]
        prm[:, o:o + w] = a
    ab = f("ada_b")
    putp("ada_b", np.concatenate([_fm(ab[0]), _fm(ab[1])], 1))
    putp("ret_gn", _fm(f("ret_gn_gain")[0]))
    putp("mu", np.concatenate([_fm(f("rwkv_mu")[0][i]) for i in range(6)], 1))
    for n, k in (("w0", "rwkv_w0"), ("a0", "rwkv_a0"), ("k_k", "rwkv_k_k"), ("k_a", "rwkv_k_a"), ("rw_gn", "rwkv_gn_gain")):
        putp(n, _fm(f(k)[0]))
    putp("r_k", _fm(f("rwkv_r_k")[0].reshape(-1)))
    cw = f("ffn_conv_w")
    putp("conv_w", np.concatenate([_fm(cw[l, j]) for l in range(2) for j in range(3)], 1))
    cb = f("ffn_conv_b")
    putp("conv_b", np.concatenate([_fm(cb[l]) for l in range(2)], 1))
    putp("fin", _fm(f("final_gain")))
    sh["prm"] = prm
    sh["cst"] = _consts()
    cosp, sinp = _rope(np.arange(SEQ))
    rp = np.stack([cosp, sinp], 1)
    sh["rope"] = np.ascontiguousarray(rp.reshape(128, 2, NTILE, NT).transpose(2, 0, 1, 3)).reshape(NTILE, 128, 2 * NT)
    coss, sins = _rope(PAST + np.arange(TS))
    sh["rope_s"] = np.stack([coss, sins], 1).reshape(128, 2 * TS)
    xp, xs = f("x_prompt"), f("x_sample")
    cpv, csv = f("c_prompt"), f("c_sample")
    xt_cache = {}
    in_maps = []
    for c in range(NCORES):
        sq = c % 2
        sidx = [c * NSP + i for i in range(NSP)]
        if sq not in xt_cache:
            xT = xp[sq].T
            xt_cache[sq] = np.ascontiguousarray(xT.reshape(KC, 128, NTILE, NT).transpose(2, 1, 0, 3)).reshape(NTILE, 128, KC * NT)
        m = dict(sh)
        m["x_in"] = xt_cache[sq]
        m["xs_in"] = np.stack([np.ascontiguousarray(xs[j].T.reshape(KC, 128, TS).transpose(1, 0, 2)).reshape(128, KC * TS) for j in sidx], 0)
        m["cvec"] = np.stack([_fm(cpv[sq])] + [_fm(csv[j]) for j in sidx], 2).reshape(128, KC * NROW)
        l_ret, l_wkv, l_sh, l_cv = [], [], [], []
        for j in sidx:
            sr = f("state_ret")[0, j]
            l_ret.append(np.ascontiguousarray(sr.reshape(RH, 2, 128, DV).transpose(2, 0, 1, 3)).reshape(128, RH * 2 * DV))
            sw = f("state_rwkv_wkv")[0, j]
            t = np.zeros((16, 2, 64, 2, 64), np.float32)
            swT = sw.transpose(0, 2, 1).reshape(16, 2, 64, 64)
            for hh in range(2):
                t[:, hh, :, hh, :] = swT[:, hh]
            l_wkv.append(np.ascontiguousarray(t.transpose(1, 2, 0, 3, 4)).reshape(128, 16 * 128))
            l_sh.append(_fm(f("state_rwkv_shift")[0, j]))
            sc = f("state_ffn_conv")[:, j]
            l_cv.append(np.ascontiguousarray(sc.reshape(2, 2, FC, 128).transpose(3, 0, 2, 1)).reshape(128, 2 * FC * 2))
        m["st_ret"], m["st_wkv"], m["st_shift"], m["st_conv"] = (np.stack(l, 0) for l in (l_ret, l_wkv, l_sh, l_cv))
        in_maps.append(m)
    res = run_bass_kernel_spmd(nc, in_maps, core_ids=list(range(NCORES)))
    R = res.results
    def unx(a, nt, ntile):
        return a.reshape(ntile, 128, KC, nt).transpose(0, 3, 2, 1).reshape(ntile * nt, D)
    y_prompt = np.stack([unx(R[b]["y_out"], NT, NTILE) for b in range(2)], 0)
    y_sample = np.stack([unx(R[j // NSP]["ys_out"][j % NSP], TS, 1) for j in range(8)], 0)

    def un_ret(a):
        return a.reshape(128, RH, 2, DV).transpose(1, 2, 0, 3).reshape(RH, DK, DV)

    def un_wkv(a):
        t = a.reshape(2, 64, 16, 2, 64)
        o = np.stack([t[hh, :, :, hh, :] for hh in range(2)], 0)
        return o.transpose(2, 0, 3, 1).reshape(32, 64, 64)

    def un_conv(a):
        return a.reshape(128, 2, FC, 2).transpose(1, 3, 2, 0).reshape(2, 2, FF)

    def un_vec(a):
        return a.T.reshape(-1)
    outs_p = [np.stack([fn(R[b][k][0]) for b in range(2)], 0) for k, fn in
              (("o_ret", un_ret), ("o_wkv", un_wkv), ("o_shift", un_vec))]
    outs_s = [np.stack([fn(R[j // NSP][k][1 + j % NSP]) for j in range(8)], 0) for k, fn in
              (("o_ret", un_ret), ("o_wkv", un_wkv), ("o_shift", un_vec))]
    pc = np.stack([un_conv(R[b]["o_conv"][0]) for b in range(2)], 1)
    sc_ = np.stack([un_conv(R[j // NSP]["o_conv"][1 + j % NSP]) for j in range(8)], 1)
    return (y_prompt.astype(np.float32), y_sample.astype(np.float32),
            outs_p[0][None].astype(np.float32), outs_p[1][None].astype(np.float32), outs_p[2][None].astype(np.float32), pc.astype(np.float32),
            outs_s[0][None].astype(np.float32), outs_s[1][None].astype(np.float32), outs_s[2][None].astype(np.float32), sc_.astype(np.float32))
```

```python
import contextlib
import math
import numpy as np
import concourse.bass as bass
import concourse.mybir as mybir
from concourse.bass_utils import run_bass_kernel_spmd

F32 = mybir.dt.float32
BF16 = mybir.dt.bfloat16
ALU = mybir.AluOpType
AF = mybir.ActivationFunctionType
AX = mybir.AxisListType

D = 2048
KC = 16
SEQ = 16384
NT = 256
NTILE = SEQ // NT
TS = 16
PAST = 4096
RH, DK, DV = 8, 256, 512
FF = 5632
FC = 44
HN = 64
GRAN = 512
SAME_ENGINE_SYNC = True
NDMA_SEM = 12
STOP_AFTER = 99
import os
RW_STAGE = int(os.environ.get('RW_STAGE', '9'))
NCORES = 2
NSP = 8 // NCORES
NROW = 1 + NSP


class V:
    def __init__(self, ap, keys):
        self.ap = ap
        self.keys = keys

    def __getitem__(self, idx):
        return V(self.ap[idx], self.keys)


class Sched:
    ENGS = ("pe", "act", "dve", "pool", "sp")

    def __init__(self, nc, stack):
        self.nc = nc
        self.stack = stack
        self.sem = {e: stack.enter_context(nc.semaphore("s_" + e)) for e in ("pe", "act", "dve", "pool")}
        self.dsem = {q: [stack.enter_context(nc.semaphore("d_%s%d" % (q, i))) for i in range(NDMA_SEM)]
                     for q in ("sp", "pool", "act")}
        self.dcnt = {q: 0 for q in ("sp", "pool", "act")}
        self.cnt = {e: 0 for e in ("pe", "act", "dve", "pool")}
        self.waited = {e: {} for e in self.ENGS}
        self.ops = {e: [] for e in self.ENGS}
        self.lastw = {}
        self.readers = {}
        self.nops = 0

    @staticmethod
    def _keys(vs):
        out = []
        for v in vs:
            if isinstance(v, V):
                out.extend(v.keys)
            else:
                out.append(v)
        return out

    def _deps(self, rk, wk):
        deps = []
        lw, rd = self.lastw, self.readers
        for k in rk:
            w = lw.get(k)
            if w is not None:
                deps.append(w)
        for k in wk:
            w = lw.get(k)
            if w is not None:
                deps.append(w)
            r = rd.get(k)
            if r:
                deps.extend(r.values())
        return deps

    def _mark(self, eng, tok, rk, wk):
        rd = self.readers
        for k in rk:
            d = rd.get(k)
            if d is None:
                rd[k] = {eng: tok}
            else:
                d[eng] = tok
        for k in wk:
            self.lastw[k] = tok
            rd[k] = None

    def _waits(self, eng, deps, is_dma):
        waits = []
        wd = self.waited[eng]
        for (sem, val, seng) in deps:
            if seng == eng and not is_dma:
                if eng == "pe" or not SAME_ENGINE_SYNC:
                    continue
            if wd.get(id(sem), 0) >= val:
                continue
            wd[id(sem)] = val
            waits.append((sem, val))
        return waits

    def op(self, eng, fn, reads=(), writes=()):
        rk, wk = self._keys(reads), self._keys(writes)
        waits = self._waits(eng, self._deps(rk, wk), False)
        self.cnt[eng] += 1
        tok = (self.sem[eng], self.cnt[eng], eng)
        self.ops[eng].append((waits, fn, self.sem[eng], 1))
        self._mark(eng, tok, rk, wk)
        self.nops += 1

    def dma(self, q, fn, reads=(), writes=()):
        rk, wk = self._keys(reads), self._keys(writes)
        deps = self._deps(rk, wk)
        m = self.dcnt[q]
        self.dcnt[q] += 1
        slot, rnd = m % NDMA_SEM, m // NDMA_SEM
        sem = self.dsem[q][slot]
        if rnd > 0:
            deps.append((sem, 16 * rnd, "dma"))
        deps = [(s, v, "x") if e == q else (s, v, e) for (s, v, e) in deps]
        waits = self._waits(q, deps, True)
        tok = (sem, 16 * (rnd + 1), "dma_" + q)
        self.ops[q].append((waits, fn, sem, 16))
        self._mark("dma_" + q + str(slot), tok, rk, wk)
        self.nops += 1

    def state(self):
        import copy
        return dict(cnt=dict(self.cnt), dcnt=dict(self.dcnt), waited={e: dict(d) for e, d in self.waited.items()},
                    lastw=dict(self.lastw), readers={k: (dict(v) if v else v) for k, v in self.readers.items()},
                    lens={e: len(v) for e, v in self.ops.items()}, nops=self.nops)

    def restore(self, st):
        self.cnt, self.dcnt = dict(st["cnt"]), dict(st["dcnt"])
        self.waited = {e: dict(d) for e, d in st["waited"].items()}
        self.lastw = dict(st["lastw"])
        self.readers = {k: (dict(v) if v else v) for k, v in st["readers"].items()}
        for e in self.ENGS:
            del self.ops[e][st["lens"][e]:]
        self.nops = st["nops"]

    def deltas(self, st0, st1):
        d = {}
        for e in ("pe", "act", "dve", "pool"):
            d[id(self.sem[e])] = st1["cnt"][e] - st0["cnt"][e]
        for q in ("sp", "pool", "act"):
            m = st1["dcnt"][q] - st0["dcnt"][q]
            assert m % NDMA_SEM == 0, (q, m)
            for sm in self.dsem[q]:
                d[id(sm)] = 16 * (m // NDMA_SEM)
        return d

    def norm_tile(self, st0, st1, k, delta):
        out = {}
        for e in self.ENGS:
            out[e] = [tuple((id(s_), v - k * delta[id(s_)]) for (s_, v) in w[0])
                      for w in self.ops[e][st0["lens"][e]:st1["lens"][e]]]
        return out

    def set_loop(self, st0, st1, k, n_end, delta):
        self.loop = dict(lo=st0["lens"], hi=st1["lens"], k=k, n=n_end, delta=delta)
        extra_iters = n_end - 1 - k
        for e in ("pe", "act", "dve", "pool"):
            self.cnt[e] += extra_iters * (st1["cnt"][e] - st0["cnt"][e])
        for q in ("sp", "pool", "act"):
            self.dcnt[q] += extra_iters * (st1["dcnt"][q] - st0["dcnt"][q])

    loop = None
    loop_var = None

    def emit(self, block):
        ops = self.ops
        self.ops = {e: [] for e in self.ENGS}
        loop = self.loop
        self.loop = None
        sched = self

        def body(ename, lst, extra):
            def run(eng, sub, base, scratch):
                for (waits, fn, sem, inc) in sub:
                    for (s, v) in waits:
                        dl = loop["delta"][id(s)] if base is not None else 0
                        if dl:
                            eng.reg_add(scratch, base[dl], v)
                            eng.wait_ge(s, scratch)
                        else:
                            eng.wait_ge(s, v)
                    fn(eng).then_inc(sem, inc)

            def _f(eng):
                if loop is None:
                    run(eng, lst, None, None)
                else:
                    lo, hi = loop["lo"][ename], loop["hi"][ename]
                    run(eng, lst[:lo], None, None)
                    if hi > lo:
                        with contextlib.ExitStack() as rs:
                            dls = sorted({loop["delta"][id(s_)] for (w, _, _, _) in lst[lo:hi] for (s_, _) in w} - {0})
                            base = {d_: rs.enter_context(eng.register("lb_%s_%d" % (ename, j))) for j, d_ in enumerate(dls)}
                            scratch = rs.enter_context(eng.register("ls_%s" % ename))
                            for d_ in dls:
                                eng.reg_mov(base[d_], 0)
                            with eng.Fori(loop["k"], loop["n"]) as i:
                                sched.loop_var = i
                                run(eng, lst[lo:hi], base, scratch)
                                sched.loop_var = None
                                for d_ in dls:
                                    eng.reg_add(base[d_], base[d_], d_)
                    run(eng, lst[hi:], None, None)
                for (s, v) in extra:
                    eng.wait_ge(s, v)
            return _f

        extra = {e: [] for e in self.ENGS}
        for q in ("sp", "pool", "act"):
            m = self.dcnt[q]
            for slot in range(min(m, NDMA_SEM)):
                n = (m - slot + NDMA_SEM - 1) // NDMA_SEM
                extra[q].append((self.dsem[q][slot], 16 * n))
        block.tensor(body("pe", ops["pe"], extra["pe"]))
        block.scalar(body("act", ops["act"], extra["act"]))
        block.vector(body("dve", ops["dve"], extra["dve"]))
        block.gpsimd(body("pool", ops["pool"], extra["pool"]))
        block.sync(body("sp", ops["sp"], extra["sp"]))
        self.lastw = {}
        self.readers = {}


class Arena:
    def __init__(self, nc, stack, nbytes):
        self.nbytes = nbytes
        self.t = stack.enter_context(nc.sbuf_tensor("arena", [128, nbytes // 4], F32))
        self.top = 0

    def alloc(self, nbytes):
        lo = self.top
        self.top = lo + ((nbytes + GRAN - 1) // GRAN) * GRAN
        assert self.top <= self.nbytes, ("arena overflow", self.top)
        self.peak = max(getattr(self, 'peak', 0), self.top)
        return lo

    def view(self, lo, dt, shape, np_=128):
        esz = 2 if dt == BF16 else 4
        n = int(np.prod(shape))
        nb = n * esz
        ap = self.t[0:np_, lo // 4:(lo + nb + 3) // 4]
        if dt == BF16:
            ap = ap.bitcast(BF16)
        if len(shape) == 2:
            ap = ap.rearrange("p (a b) -> p a b", b=shape[1])
        elif len(shape) == 3:
            ap = ap.rearrange("p (a b c) -> p a b c", b=shape[1], c=shape[2])
        keys = [("A", g) for g in range(lo // GRAN, (lo + nb - 1) // GRAN + 1)]
        return V(ap, keys)

    def tile(self, dt, shape, np_=128):
        esz = 2 if dt == BF16 else 4
        lo = self.alloc(int(np.prod(shape)) * esz)
        return self.view(lo, dt, shape, np_), lo


WSPEC = {
    "ret_in": (16, 256, 48),
    "ret_out": (32, 128, 16),
    "gu0": (16, 256, 44), "gu1": (16, 256, 44),
    "dn0": (44, 128, 16), "dn1": (44, 128, 16),
    "rkv": (32, 128, 48),
    "lora1": (32, 128, 4),
    "w_o": (16, 256, 8),
}


def _pieces(w, fw):
    K, F = w.shape
    return np.ascontiguousarray(w.reshape(K // 128, 128, F // fw, fw).transpose(2, 1, 0, 3)).reshape(F // fw, 128, (K // 128) * fw)


def _fm(v):
    return np.ascontiguousarray(v.reshape(-1, 128).T)


def build_program():
    nc = bass.Bass("TRN2", target_bir_lowering=False)
    dr = lambda n, sh, dt=F32, kind="ExternalInput": nc.dram_tensor(n, sh, dt, kind=kind)
    x_in = dr("x_in", [NTILE, 128, KC * NT])
    xs_in = dr("xs_in", [NSP, 128, KC * TS])
    cvec = dr("cvec", [128, KC * NROW])
    ada_w = dr("ada_w", [2 * 96, 128, KC * 128])
    win = {k: dr("w_" + k, [n, 128, kc * fw]) for k, (kc, fw, n) in WSPEC.items()}
    l2w = dr("l2w", [128, 3 * D])
    g2w = dr("g2w", [128, 2 * D])
    prm = dr("prm", [128, NPRM])
    cst = dr("cst", [128, NCST])
    rope = dr("rope", [NTILE, 128, 2 * NT])
    rope_s = dr("rope_s", [128, 2 * TS])
    st_ret = dr("st_ret", [NSP, 128, RH * 2 * DV])
    st_wkv = dr("st_wkv", [NSP, 128, 16 * 128])
    st_shift = dr("st_shift", [NSP, 128, KC])
    st_conv = dr("st_conv", [NSP, 128, 2 * FC * 2])
    y_out = dr("y_out", [NTILE, 128, KC * NT], kind="ExternalOutput")
    ys_out = dr("ys_out", [NSP, 128, KC * TS], kind="ExternalOutput")
    o_ret = dr("o_ret", [NROW, 128, RH * 2 * DV], kind="ExternalOutput")
    o_wkv = dr("o_wkv", [NROW, 128, 16 * 128], kind="ExternalOutput")
    o_shift = dr("o_shift", [NROW, 128, KC], kind="ExternalOutput")
    o_conv = dr("o_conv", [NROW, 128, 2 * FC * 2], kind="ExternalOutput")
    wscr = {k: nc.dram_tensor("ws_" + k, [n, 128, kc * fw], BF16) for k, (kc, fw, n) in WSPEC.items()}
    l2scr = nc.dram_tensor("ws_l2", [128, 3 * D], BF16)
    g2scr = nc.dram_tensor("ws_g2", [128, 2 * D], BF16)

    with contextlib.ExitStack() as st:
        S = Sched(nc, st)
        AR = Arena(nc, st, 188 * 1024)
        PS = [V(st.enter_context(nc.psum_tensor("ps%d" % i, [128, 512], F32))[:], ["ps%d" % i]) for i in range(8)]
        psi = [0]

        def psum():
            psi[0] = (psi[0] + 1) % 8
            return PS[psi[0]]

        def mm(out, lhsT, rhs, start, stop):
            S.op("pe", lambda e: e.matmul(out.ap, lhsT=lhsT.ap, rhs=rhs.ap, start=start, stop=stop),
                 reads=[lhsT, rhs], writes=[out])

        def tr(out, in_, ident):
            S.op("pe", lambda e: e.transpose(out.ap, in_.ap, ident.ap), reads=[in_, ident], writes=[out])

        def act(out, in_, func, bias=0.0, scale=1.0, extra=()):
            rd = [in_] + [b for b in (bias, scale) if isinstance(b, V)] + list(extra)
            b_ = bias.ap if isinstance(bias, V) else bias
            s_ = scale.ap if isinstance(scale, V) else scale
            S.op("act", lambda e: e.activation(out=out.ap, in_=in_.ap, func=func, bias=b_, scale=s_), reads=rd, writes=[out])

        def tt(out, a, b, op, eng="dve"):
            S.op(eng, lambda e: e.tensor_tensor(out=out.ap, in0=a.ap, in1=b.ap, op=op), reads=[a, b], writes=[out])

        def ts(out, a, s1, s2, op0, op1=None, eng="dve"):
            rd = [a] + [s for s in (s1, s2) if isinstance(s, V)]
            a1 = s1.ap if isinstance(s1, V) else s1
            a2 = s2.ap if isinstance(s2, V) else s2
            if op1 is None:
                S.op(eng, lambda e: e.tensor_scalar(out=out.ap, in0=a.ap, scalar1=a1, scalar2=None, op0=op0), reads=rd, writes=[out])
            else:
                S.op(eng, lambda e: e.tensor_scalar(out=out.ap, in0=a.ap, scalar1=a1, scalar2=a2, op0=op0, op1=op1), reads=rd, writes=[out])

        def stt(out, a, s, b, op0, op1, eng="dve"):
            rd = [a, b] + ([s] if isinstance(s, V) else [])
            s_ = s.ap if isinstance(s, V) else s
            S.op(eng, lambda e: e.scalar_tensor_tensor(out=out.ap, in0=a.ap, scalar=s_, in1=b.ap, op0=op0, op1=op1), reads=rd, writes=[out])

        def cp(out, in_, eng="dve"):
            if eng == "act":
                S.op("act", lambda e: e.copy(out=out.ap, in_=in_.ap), reads=[in_], writes=[out])
            else:
                S.op(eng, lambda e: e.tensor_copy(out=out.ap, in_=in_.ap), reads=[in_], writes=[out])

        def memset(out, val, eng="pool"):
            S.op(eng, lambda e: e.memset(out.ap, val), writes=[out])

        def recip(out, in_):
            S.op("dve", lambda e: e.reciprocal(out=out.ap, in_=in_.ap), reads=[in_], writes=[out])

        def dma(q, out, in_, okeys=None, ikeys=None):
            oa = out.ap if isinstance(out, V) else out
            ia = in_.ap if isinstance(in_, V) else in_
            rd = [in_] if isinstance(in_, V) else list(ikeys or [])
            wr = [out] if isinstance(out, V) else list(okeys or [])
            S.dma(q, lambda e: e.dma_start(out=oa, in_=ia), reads=rd, writes=wr)

        prm_t, _ = AR.tile(F32, (NPRM,))
        cst_t, _ = AR.tile(F32, (NCST,))
        mod_t, _ = AR.tile(F32, (2, 96, NROW))
        sret_t, _ = AR.tile(F32, (RH * 2, DV))
        sretb_t, _ = AR.tile(BF16, (RH * 2, DV))
        swkv_t, _ = AR.tile(F32, (16, 128))
        x_t, _ = AR.tile(F32, (KC, NT))
        shift_t, _ = AR.tile(F32, (KC, 1))
        conv_t, _ = AR.tile(F32, (2, FC, 2))
        WB = [AR.tile(BF16, (6144,)) for _ in range(2)]
        wbi = [0]
        base_top = AR.top

        P = lambda name: V(prm_t.ap[:, POFF[name][0]:POFF[name][0] + POFF[name][1]], prm_t.keys)
        C = lambda name: V(cst_t.ap[:, COFF[name][0]:COFF[name][0] + COFF[name][1]], cst_t.keys)
        ident = C("ident")
        ones = C("ones")
        bones = C("bones")

        def wslot():
            wbi[0] = (wbi[0] + 1) % 2
            return WB[wbi[0]]

        def wload(name, piece, dt=BF16, src=None):
            kcn, fw, _ = WSPEC[name] if name in WSPEC else (KC, 128, 0)
            (wv, lo) = wslot()
            v = AR.view(lo, dt, (kcn, fw))
            if src is None:
                dma("sp", v, wscr[name][piece].rearrange("p (a b) -> p a b", b=fw), ikeys=[("ws", name, piece)])
            else:
                dma("sp", v, src.rearrange("p (a b) -> p a b", b=fw))
            return v

        with nc.Block() as block:
            dma("sp", prm_t, prm.ap())
            dma("sp", cst_t, cst.ap())

            m0 = AR.top
            stg = [AR.tile(F32, (6144,)) for _ in range(2)]
            cengs = ["dve", "pool", "act"]
            ci = 0
            mu_v = P("mu")
            for name, (kcn, fw, npc) in WSPEC.items():
                for pc in range(npc):
                    nh = 2 if (kcn * fw * 4 > 24576 or name in ("rkv", "lora1")) else 1
                    hk = kcn // nh
                    for hh in range(nh):
                        (sv, slo) = stg[ci % 2]
                        (bv, blo) = WB[ci % 2]
                        s3 = AR.view(slo, F32, (hk, fw))
                        b3 = AR.view(blo, BF16, (hk, fw))
                        src = win[name][pc].rearrange("p (a b) -> p a b", b=fw)[:, hh * hk:(hh + 1) * hk, :]
                        dst = wscr[name][pc].rearrange("p (a b) -> p a b", b=fw)[:, hh * hk:(hh + 1) * hk, :]
                        dma("sp", s3, src)
                        eng = cengs[ci % 3]
                        mus = None
                        if name == "rkv" and hh == 1:
                            mus = {0: 0, 1: 2, 2: 3}[pc % 3]
                        if name == "lora1" and hh == 1:
                            mus = {0: 1, 1: 4, 2: 5, 3: 5}[pc]
                        if mus is not None:
                            mub = V(mu_v.ap[:, mus * 16:(mus + 1) * 16].unsqueeze(2).to_broadcast([128, 16, fw]), mu_v.keys)
                            tt(b3, s3, mub, ALU.mult, eng="dve" if eng == "act" else eng)
                        else:
                            cp(b3, s3, eng=eng)
                        dma("pool", dst, b3, okeys=[("ws", name, pc)])
                        ci += 1
            for (src_t, dst_t, ncol) in ((l2w, l2scr, 3 * D), (g2w, g2scr, 2 * D)):
                for j in range(ncol // 2048):
                    (sv, slo) = stg[ci % 2]
                    (bv, blo) = WB[ci % 2]
                    s2 = AR.view(slo, F32, (2048,))
                    b2 = AR.view(blo, BF16, (2048,))
                    dma("sp", s2, src_t.ap()[:, j * 2048:(j + 1) * 2048])
                    cp(b2, s2, eng=cengs[ci % 3])
                    dma("pool", dst_t.ap()[:, j * 2048:(j + 1) * 2048], b2, okeys=[("wsl", dst_t.name, j)])
                    ci += 1
            AR.top = m0

            m0 = AR.top
            cv, _ = AR.tile(F32, (KC, NROW))
            dma("sp", cv, cvec.ap().rearrange("p (a b) -> p a b", b=NROW))
            act(cv, cv, AF.Silu)
            for l in range(2):
                for oc in range(96):
                    w = wload("ada", 0, dt=F32, src=ada_w[l * 96 + oc])
                    ps = psum()
                    for kc in range(KC):
                        mm(ps[:, 0:NROW], w[:, kc, :], cv[:, kc, :], kc == 0, kc == KC - 1)
                    bcol = V(prm_t.ap[:, POFF["ada_b"][0] + l * 96 + oc:POFF["ada_b"][0] + l * 96 + oc + 1], prm_t.keys)
                    n = oc // 16
                    if n in (1, 4):
                        ts(mod_t[:, l, oc, :], ps[:, 0:NROW], bcol, 1.0, ALU.add, ALU.add)
                    else:
                        ts(mod_t[:, l, oc, :], ps[:, 0:NROW], bcol, None, ALU.add)
            AR.top = m0

            def modv(l, n, kc, row):
                return mod_t[:, l, n * 16 + kc, row:row + 1]

            def norm_mod(nt, l, nsh, nsc, row, out_bf, last=None):
                m = AR.top
                sqs = [AR.tile(F32, (nt,))[0] for _ in range(3)]
                rstd, _ = AR.tile(F32, (nt,))
                tmps = [AR.tile(F32, (nt,))[0] for _ in range(3)]
                ps = psum()
                for kc in range(KC):
                    sq = sqs[kc % 3]
                    act(sq, x_t[:, kc, 0:nt], AF.Square)
                    mm(ps[:, 0:nt], ones, sq, kc == 0, kc == KC - 1)
                act(rstd, ps[:, 0:nt], AF.Sqrt, bias=C("eps6")[:, 0:1], scale=1.0 / D)
                recip(rstd, rstd)
                for kc in range(KC):
                    tmp = tmps[kc % 3]
                    tt(tmp, x_t[:, kc, 0:nt], rstd, ALU.mult)
                    act(out_bf[:, kc, 0:nt], tmp, AF.Identity, bias=modv(l, nsh, kc, row), scale=modv(l, nsc, kc, row))
                    if last is not None:
                        act(last[:, kc, :], tmp[:, nt - 1:nt], AF.Identity, bias=modv(l, nsh, kc, row), scale=modv(l, nsc, kc, row))
                AR.top = m
                return rstd

            def retention(nt, row, is_sample):
                L = min(128, nt)
                nblk = nt // L
                m = AR.top
                h, _ = AR.tile(BF16, (KC, nt))
                norm_mod(nt, 0, 0, 1, row, h)
                y, _ = AR.tile(BF16, (32, nt))
                qf, _ = AR.tile(F32, (2, nt))
                kf, _ = AR.tile(F32, (2, nt))
                r1, _ = AR.tile(F32, (nt,))
                r2, _ = AR.tile(F32, (nt,))
                r3, _ = AR.tile(F32, (nt,))
                r4, _ = AR.tile(F32, (nt,))
                qT, _ = AR.tile(BF16, (2, nt))
                qcT, _ = AR.tile(BF16, (2, nt))
                kT, _ = AR.tile(BF16, (2, nt))
                vT, _ = AR.tile(BF16, (4, nt))
                Vt, _ = AR.tile(BF16, (nblk, DV))
                Kt, _ = AR.tile(BF16, (nblk, DK))
                g, _ = AR.tile(F32, (4, nt))
                oh, _ = AR.tile(F32, (4, nt))
                osq, _ = AR.tile(F32, (nt,))
                mean, _ = AR.tile(F32, (nt,))
                rstd, _ = AR.tile(F32, (nt,))
                sT, _ = AR.tile(BF16, (L,))
                cos = ropet[:, 0, 0:nt]
                sin = ropet[:, 1, 0:nt]
                sfx = "s" if is_sample else ""
                maskT = C("rmask" + sfx)
                cross = C("rcross" + sfx)
                into = C("rinto" + sfx)
                identb = identb_t
                for hd in range(RH):
                    gam_L = (1.0 - 2.0 ** (-5.0 - hd)) ** L
                    for pi in range(6):
                        w = wload("ret_in", hd * 6 + pi)
                        for oc2 in range(2):
                            oc = pi * 2 + oc2
                            ps = psum()
                            for kc in range(KC):
                                mm(ps[:, 0:nt], w[:, kc, oc2 * 128:(oc2 + 1) * 128], h[:, kc, :], kc == 0, kc == KC - 1)
                            if oc < 2:
                                cp(qf[:, oc, :], ps[:, 0:nt], eng="act")
                            elif oc < 4:
                                act(kf[:, oc - 2, :], ps[:, 0:nt], AF.Copy, scale=1.0 / 16.0)
                            elif oc < 8:
                                cp(vT[:, oc - 4, :], ps[:, 0:nt], eng="act")
                            else:
                                act(g[:, oc - 8, :], ps[:, 0:nt], AF.Silu)
                    for (src, dst) in ((qf, qT), (kf, kT)):
                        tt(r1, src[:, 0, :], cos, ALU.mult)
                        tt(r2, src[:, 1, :], sin, ALU.mult)
                        tt(dst[:, 0, :], r1, r2, ALU.subtract)
                        tt(r3, src[:, 0, :], sin, ALU.mult, eng="pool")
                        tt(r4, src[:, 1, :], cos, ALU.mult, eng="pool")
                        tt(dst[:, 1, :], r3, r4, ALU.add, eng="pool")
                    for b in range(nblk):
                        cr = cross[:, hd * L:(hd + 1) * L]
                        for dc in range(2):
                            tt(qcT[:, dc, b * L:(b + 1) * L], qT[:, dc, b * L:(b + 1) * L], cr, ALU.mult)
                    for b in range(nblk):
                        ps = psum()
                        psb = V(ps.ap[:, 0:256].bitcast(BF16), ps.keys)
                        for vc in range(4):
                            tr(psb[0:L, vc * 128:(vc + 1) * 128], vT[:, vc, b * L:(b + 1) * L], identb)
                        cp(Vt[0:L, b, :], psb[0:L, :], eng="act")
                        ps = psum()
                        psb = V(ps.ap[:, 0:256].bitcast(BF16), ps.keys)
                        for dc in range(2):
                            tr(psb[0:L, dc * 128:(dc + 1) * 128], kT[:, dc, b * L:(b + 1) * L], identb)
                        ts(Kt[0:L, b, :], psb[0:L, 0:256], into[0:L, hd:hd + 1], None, ALU.mult)
                    for b in range(nblk):
                        cs = slice(b * L, (b + 1) * L)
                        ps = psum()
                        for dc in range(2):
                            mm(ps[0:L, 0:L], kT[:, dc, cs], qT[:, dc, cs], dc == 0, dc == 1)
                        tt(sT[0:L, :], ps[0:L, 0:L], maskT[0:L, hd * L:(hd + 1) * L], ALU.mult)
                        ps = psum()
                        for vc in range(4):
                            o = ps[:, vc * L:(vc + 1) * L]
                            mm(o, Vt[0:L, b, vc * 128:(vc + 1) * 128], sT[0:L, :], True, False)
                            for dc in range(2):
                                mm(o, sretb_t[:, hd * 2 + dc, vc * 128:(vc + 1) * 128], qcT[:, dc, cs], False, dc == 1)
                        for vc in range(4):
                            cp(oh[:, vc, cs], ps[:, vc * L:(vc + 1) * L], eng="act" if vc % 2 else "dve")
                        for dc in range(2):
                            ps = psum()
                            mm(ps[:, 0:DV], Kt[0:L, b, dc * 128:(dc + 1) * 128], Vt[0:L, b, :], True, True)
                            stt(sret_t[:, hd * 2 + dc, :], sret_t[:, hd * 2 + dc, :], gam_L, ps[:, 0:DV], ALU.mult, ALU.add)
                            cp(sretb_t[:, hd * 2 + dc, :], sret_t[:, hd * 2 + dc, :], eng="pool")
                    ps1 = psum()
                    for vc in range(4):
                        mm(ps1[:, 0:nt], ones, oh[:, vc, :], vc == 0, vc == 3)
                    ps2 = psum()
                    for vc in range(4):
                        act(osq, oh[:, vc, :], AF.Square)
                        mm(ps2[:, 0:nt], ones, osq, vc == 0, vc == 3)
                    act(mean, ps1[:, 0:nt], AF.Copy, scale=1.0 / DV)
                    tt(osq, mean, mean, ALU.mult)
                    stt(rstd, ps2[:, 0:nt], 1.0 / DV, osq, ALU.mult, ALU.subtract)
                    act(rstd, rstd, AF.Sqrt, bias=C("eps5")[:, 0:1], scale=1.0)
                    recip(rstd, rstd)
                    gg = P("ret_gn")
                    for vc in range(4):
                        tt(osq, oh[:, vc, :], mean, ALU.subtract)
                        tt(osq, osq, rstd, ALU.mult)
                        stt(y[:, hd * 4 + vc, :], osq, gg[:, hd * 4 + vc:hd * 4 + vc + 1], g[:, vc, :], ALU.mult, ALU.mult)
                for pc in range(16):
                    w = wload("ret_out", pc)
                    ps = psum()
                    for kc in range(32):
                        mm(ps[:, 0:nt], w[:, kc, :], y[:, kc, :], kc == 0, kc == 31)
                    stt(x_t[:, pc, 0:nt], ps[:, 0:nt], modv(0, 2, pc, row), x_t[:, pc, 0:nt], ALU.mult, ALU.add)
                AR.top = m

            def ffn(nt, l, row):
                m = AR.top
                h, _ = AR.tile(BF16, (KC, nt))
                norm_mod(nt, l, 3, 4, row, h)
                a, _ = AR.tile(BF16, (FC, nt))
                ues = [AR.tile(F32, (nt + 2,))[0] for _ in range(2)]
                cvs = [[AR.tile(F32, (nt,))[0] for _ in range(4)] for _ in range(2)]
                cw = P("conv_w")
                cb = P("conv_b")
                for fp in range(22):
                    wg = wload("gu%d" % l, 2 * fp)
                    wu = wload("gu%d" % l, 2 * fp + 1)
                    for oc2 in range(2):
                        fc = fp * 2 + oc2
                        psg = psum()
                        for kc in range(KC):
                            mm(psg[:, 0:nt], wg[:, kc, oc2 * 128:(oc2 + 1) * 128], h[:, kc, :], kc == 0, kc == KC - 1)
                        psu = psum()
                        for kc in range(KC):
                            mm(psu[:, 0:nt], wu[:, kc, oc2 * 128:(oc2 + 1) * 128], h[:, kc, :], kc == 0, kc == KC - 1)
                        ue = ues[fc % 2]
                        cp(ue[:, 0:2], conv_t[:, l, fc, :], eng="pool")
                        cp(ue[:, 2:nt + 2], psg[:, 0:nt], eng="act")
                        cp(conv_t[:, l, fc, :], ue[:, nt:nt + 2], eng="pool")
                        wj = lambda j: cw[:, (l * 3 + j) * FC + fc:(l * 3 + j) * FC + fc + 1]
                        cA, cB, cC, cD = cvs[fc % 2]
                        ts(cA, ue[:, 0:nt], wj(0), cb[:, l * FC + fc:l * FC + fc + 1], ALU.mult, ALU.add)
                        stt(cB, ue[:, 1:nt + 1], wj(1), cA, ALU.mult, ALU.add)
                        stt(cC, ue[:, 2:nt + 2], wj(2), cB, ALU.mult, ALU.add)
                        act(cD, cC, AF.Silu)
                        tt(a[:, fc, :], cD, psu[:, 0:nt], ALU.mult)
                for pc in range(16):
                    w = wload("dn%d" % l, pc)
                    ps = psum()
                    for kc in range(FC):
                        mm(ps[:, 0:nt], w[:, kc, :], a[:, kc, :], kc == 0, kc == FC - 1)
                    stt(x_t[:, pc, 0:nt], ps[:, 0:nt], modv(l, 5, pc, row), x_t[:, pc, 0:nt], ALU.mult, ALU.add)
                AR.top = m

            ropet, _ = AR.tile(F32, (2, NT))
            identb_t, _ = AR.tile(BF16, (128,))
            cp(identb_t, ident)
            base2 = AR.top


            negw0, _ = AR.tile(F32, (16,))
            ts(negw0, P("w0"), -1.0, None, ALU.mult)
            _mo = COFF["mstrict"][0]

            def rwkv(nt, row):
                Cn = min(128, nt)
                nch = nt // Cn
                nlev = int(round(math.log2(Cn)))
                m = AR.top
                h, _ = AR.tile(BF16, (KC, nt))
                xx, _ = AR.tile(BF16, (KC, nt))
                newsh, _ = AR.tile(F32, (KC, 1))
                norm_mod(nt, 1, 0, 1, row, h, last=newsh)
                tt(xx[:, :, 0:1], shift_t, h[:, :, 0:1], ALU.subtract)
                if nt > 1:
                    tt(xx[:, :, 1:nt], h[:, :, 0:nt - 1], h[:, :, 1:nt], ALU.subtract)
                cp(shift_t, newsh, eng="pool")
                y, _ = AR.tile(BF16, (KC, nt))
                lm, _ = AR.tile(BF16, (4, nt))
                l2, _ = AR.tile(BF16, (4, 128))
                F = lambda *sh: AR.tile(F32, sh)[0]
                r_t, k0_t, v_t, e2, asig, g_t, kk, k_t, b_t, bonus, tA, tB = (F(nt) for _ in range(12))
                e2T = F(128)
                cs_, gam, ginv, gprev = F(Cn), F(Cn), F(Cn), F(Cn)
                AR2, BK = F(2, Cn), F(2, Cn)
                AR2f = lambda hs_: V(AR2.ap[hs_].rearrange("p a b -> p (a b)"), AR2.keys)
                Bh, Kh = F(Cn), F(Cn)
                TM = F(4, 128)
                Gb = [F(2 * Cn) for _ in range(2)]
                Gk = [F(2 * Cn) for _ in range(2)]
                g3 = lambda t_: V(t_.ap[0:Cn].rearrange("p (a b) -> p a b", b=Cn), t_.keys)
                Lp = [[F(Cn) for _ in range(2)] for _ in range(2)]
                Pp = [[F(Cn) for _ in range(2)] for _ in range(2)]
                Xp = [[F(128) for _ in range(2)] for _ in range(2)]
                WT, UL, Wfm, UT, Osb, Osq, Yn = F(128), F(128), F(Cn), F(128), F(128), F(128), F(128)
                st1, st2, st3 = F(2), F(2), F(2)
                mask2 = V(cst_t.ap[0:Cn, _mo:_mo + 256].rearrange("p (a b) -> p a b", b=128)[:, :, 0:Cn], cst_t.keys)
                maskT = V(cst_t.ap[0:Cn, COFF["mstrictT"][0]:COFF["mstrictT"][0] + Cn], cst_t.keys)
                mincl = V(cst_t.ap[0:Cn, COFF["mincl"][0]:COFF["mincl"][0] + Cn], cst_t.keys)

                def proj(piece):
                    w = wload("rkv" if piece >= 0 else "lora1", piece if piece >= 0 else -piece - 1)
                    ps = psum()
                    for kc in range(32):
                        rhs = h[:, kc, :] if kc < 16 else xx[:, kc - 16, :]
                        mm(ps[:, 0:nt], w[:, kc, :], rhs, kc == 0, kc == 31)
                    return ps
                ps = proj(-1)
                act(lm[:, 0, :], ps[:, 0:nt], AF.Tanh)
                ps = proj(-2)
                cp(lm[:, 1, :], ps[:, 0:nt], eng="act")
                ps = proj(-3)
                act(lm[:, 2, :], ps[:, 0:nt], AF.Sigmoid)
                ps = proj(-4)
                act(lm[:, 3, :], ps[:, 0:nt], AF.Sigmoid)
                pcol = lambda n, p: V(prm_t.ap[:, POFF[n][0] + p:POFF[n][0] + p + 1], prm_t.keys)
                for p in range(16):
                    psl = slice(p * 128, (p + 1) * 128)
                    dma("sp", l2[:, 0, :], l2scr.ap()[:, p * 128:(p + 1) * 128], ikeys=[("wsl", l2scr.name, 0)])
                    dma("sp", l2[:, 1, :], l2scr.ap()[:, D + p * 128:D + (p + 1) * 128], ikeys=[("wsl", l2scr.name, 1)])
                    dma("sp", l2[:, 2:4, :], g2scr.ap().rearrange("p (a b) -> p a b", b=D)[:, :, p * 128:(p + 1) * 128],
                        ikeys=[("wsl", g2scr.name, 0), ("wsl", g2scr.name, 1)])
                    ps = proj(3 * p + 0)
                    cp(r_t, ps[:, 0:nt], eng="act")
                    ps = proj(3 * p + 1)
                    cp(k0_t, ps[:, 0:nt], eng="act")
                    ps = proj(3 * p + 2)
                    cp(v_t, ps[:, 0:nt], eng="act")
                    ps = psum()
                    mm(ps[:, 0:nt], l2[:, 0, :], lm[:, 0, :], True, True)
                    act(e2, ps[:, 0:nt], AF.Exp, bias=negw0[:, p:p + 1], scale=-1.0)
                    act(e2, e2, AF.Ln, bias=1.0)
                    act(e2, e2, AF.Exp, bias=C("mhalf")[:, 0:1], scale=-1.0)
                    ps = psum()
                    mm(ps[:, 0:nt], l2[:, 1, :], lm[:, 1, :], True, True)
                    act(asig, ps[:, 0:nt], AF.Sigmoid, bias=pcol("a0", p))
                    ps = psum()
                    mm(ps[:, 0:nt], l2[:, 2, :], lm[:, 2, :], True, False)
                    mm(ps[:, 0:nt], l2[:, 3, :], lm[:, 3, :], False, True)
                    cp(g_t, ps[:, 0:nt], eng="act")
                    ts(kk, k0_t, pcol("k_k", p), None, ALU.mult)
                    act(tA, kk, AF.Square)
                    ps = psum()
                    mm(ps[:, 0:nt], bones, tA, True, True)
                    act(tA, ps[:, 0:nt], AF.Sqrt)
                    ts(tA, tA, 1e-12, None, ALU.max)
                    recip(tA, tA)
                    tt(kk, kk, tA, ALU.mult)
                    ts(tA, asig, pcol("k_a", p), pcol("k_a", p), ALU.mult, ALU.subtract)
                    stt(k_t, tA, 1.0, k0_t, ALU.add, ALU.mult)
                    tt(b_t, kk, asig, ALU.mult)
                    stt(tA, r_t, pcol("r_k", p), k_t, ALU.mult, ALU.mult)
                    ps = psum()
                    mm(ps[:, 0:nt], bones, tA, True, True)
                    tt(bonus, ps[:, 0:nt], v_t, ALU.mult)
                    for c in range(nch if RW_STAGE >= 2 else 0):
                        cs = slice(c * Cn, (c + 1) * Cn)
                        ps = psum()
                        tr(ps[0:Cn, 0:128], e2[:, cs], ident)
                        cp(e2T[0:Cn, :], ps[0:Cn, 0:128])
                        ps = psum()
                        mm(ps[:, 0:Cn], e2T[0:Cn, :], mincl, True, True)
                        cp(cs_, ps[:, 0:Cn])
                        act(gam, cs_, AF.Exp, scale=-1.0)
                        act(ginv, cs_, AF.Exp)
                        tt(gprev, cs_, e2[:, cs], ALU.subtract)
                        act(gprev, gprev, AF.Exp, scale=-1.0)
                        stt(AR2[:, 0, :], kk[:, cs], -1.0, gprev, ALU.mult, ALU.mult)
                        tt(AR2[:, 1, :], r_t[:, cs], gam, ALU.mult)
                        tt(BK[:, 0, :], b_t[:, cs], ginv, ALU.mult)
                        tt(BK[:, 1, :], k_t[:, cs], ginv, ALU.mult)
                        gC = gam[:, Cn - 1:Cn]
                        ts(Bh, BK[:, 0, :], gC, None, ALU.mult)
                        ts(Kh, BK[:, 1, :], gC, None, ALU.mult)
                        ps = psum()
                        tr(ps[0:Cn, 0:128], v_t[:, cs], ident)
                        tr(ps[0:Cn, 128:256], Bh, ident)
                        tr(ps[0:Cn, 256:384], Kh, ident)
                        tr(ps[0:Cn, 384:512], AR2[:, 0, :], ident)
                        cp(TM[0:Cn, 0:2, :], V(ps.ap[0:Cn, 0:256].rearrange("p (a b) -> p a b", b=128), ps.keys), eng="act")
                        cp(TM[0:Cn, 2:4, :], V(ps.ap[0:Cn, 256:512].rearrange("p (a b) -> p a b", b=128), ps.keys))
                        Vtm, Bhtm, Khtm, Attm = TM[0:Cn, 0, :], TM[0:Cn, 1, :], TM[0:Cn, 2, :], TM[0:Cn, 3, :]
                        if RW_STAGE < 3:
                            continue
                        hst = []
                        for hh in range(2):
                            hs = slice(64 * hh, 64 * hh + 64)
                            ar_h = AR2f(hs)
                            ps = psum()
                            mm(ps[0:Cn, 0:2 * Cn], BK[hs, 0, :], ar_h, True, True)
                            tt(g3(Gb[hh]), V(ps.ap[0:Cn, 0:2 * Cn].rearrange("p (a b) -> p a b", b=Cn), ps.keys), mask2, ALU.mult)
                            ps = psum()
                            mm(ps[0:Cn, 0:2 * Cn], BK[hs, 1, :], ar_h, True, True)
                            tt(g3(Gk[hh]), V(ps.ap[0:Cn, 0:2 * Cn].rearrange("p (a b) -> p a b", b=Cn), ps.keys), mask2, ALU.mult)
                            ps = psum()
                            mm(ps[0:Cn, 0:Cn], AR2[hs, 0, :], BK[hs, 0, :], True, True)
                            tt(Lp[hh][0][0:Cn], ps[0:Cn, 0:Cn], maskT, ALU.mult)
                            hc = hs
                            ps = psum()
                            mm(ps[0:Cn, 0:64], Gk[hh][0:Cn, 0:Cn], Vtm[:, hc], True, True)
                            X = Xp[hh][0]
                            cp(X[0:Cn, 0:64], Attm[:, hc], eng="pool")
                            cp(X[0:Cn, 64:128], ps[0:Cn, 0:64], eng="act")
                            hst.append([Gb[hh][0:Cn, 0:Cn], Lp[hh][0][0:Cn], X])
                        for lv in range(nlev):
                            for hh in range(2):
                                hc = slice(64 * hh, 64 * hh + 64)
                                Pc, Lc, X = hst[hh]
                                ps = psum()
                                mm(ps[0:Cn, 0:128], Pc, X[0:Cn], True, True)
                                if lv == nlev - 1:
                                    tt(WT[0:Cn, hc], X[0:Cn, 0:64], ps[0:Cn, 0:64], ALU.add)
                                    tt(UL[0:Cn, hc], X[0:Cn, 64:128], ps[0:Cn, 64:128], ALU.add)
                                else:
                                    Xn = Xp[hh][(lv + 1) % 2]
                                    tt(Xn[0:Cn], X[0:Cn], ps[0:Cn, 0:128], ALU.add)
                                    ps1 = psum()
                                    mm(ps1[0:Cn, 0:Cn], Lc, Pc, True, True)
                                    ps2 = psum()
                                    mm(ps2[0:Cn, 0:Cn], Pc, Lc, True, True)
                                    Pn = Pp[hh][lv % 2][0:Cn]
                                    Ln = Lp[hh][(lv + 1) % 2][0:Cn]
                                    cp(Pn, ps1[0:Cn, 0:Cn], eng="act")
                                    cp(Ln, ps2[0:Cn, 0:Cn])
                                    hst[hh] = [Pn, Ln, Xn]
                        ps = psum()
                        tr(ps[:, 0:Cn], WT[0:Cn], ident[0:Cn, 0:Cn])
                        cp(Wfm, ps[:, 0:Cn], eng="act")
                        if RW_STAGE < 4:
                            continue
                        for hh in range(2):
                            hs = slice(64 * hh, 64 * hh + 64)
                            ps = psum()
                            mm(ps[0:Cn, 0:64], Wfm[hs, :], swkv_t[hs, p, hs], True, True)
                            tt(UT[0:Cn, hs], ps[0:Cn, 0:64], UL[0:Cn, hs], ALU.add)
                        for hh in range(2):
                            hs = slice(64 * hh, 64 * hh + 64)
                            psa = psum()
                            mm(psa[0:Cn, 0:64], AR2[hs, 1, :], swkv_t[hs, p, hs], True, True)
                            cp(Osb[0:Cn, hs], psa[0:Cn, 0:64], eng="act")
                        ps = psum()
                        for hh in range(2):
                            hs = slice(64 * hh, 64 * hh + 64)
                            mm(ps[0:Cn, hs], Gb[hh][0:Cn, Cn:2 * Cn], UT[0:Cn, hs], True, False)
                            mm(ps[0:Cn, hs], Gk[hh][0:Cn, Cn:2 * Cn], Vtm[:, hs], False, True)
                        tt(Osb[0:Cn], Osb[0:Cn], ps[0:Cn, 0:128], ALU.add)
                        ps = psum()
                        mm(ps[:, 0:128], Bhtm, UT[0:Cn], True, False)
                        mm(ps[:, 0:128], Khtm, Vtm, False, True)
                        stt(swkv_t[:, p, :], swkv_t[:, p, :], gC, ps[:, 0:128], ALU.mult, ALU.add)
                        if RW_STAGE < 5:
                            continue
                        O3 = V(Osb.ap[0:Cn].rearrange("p (a b) -> p a b", b=64), Osb.keys)
                        Q3 = V(Osq.ap[0:Cn].rearrange("p (a b) -> p a b", b=64), Osq.keys)
                        Y3 = V(Yn.ap[0:Cn].rearrange("p (a b) -> p a b", b=64), Yn.keys)
                        S.op("dve", lambda e, o=st1, i=O3: e.tensor_reduce(out=o.ap[0:Cn], in_=i.ap, axis=AX.X, op=ALU.add), reads=[Osb], writes=[st1])
                        act(Osq[0:Cn], Osb[0:Cn], AF.Square)
                        S.op("dve", lambda e, o=st2, i=Q3: e.tensor_reduce(out=o.ap[0:Cn], in_=i.ap, axis=AX.X, op=ALU.add), reads=[Osq], writes=[st2])
                        ts(st1[0:Cn], st1[0:Cn], 1.0 / 64, None, ALU.mult)
                        tt(st3[0:Cn], st1[0:Cn], st1[0:Cn], ALU.mult)
                        stt(st2[0:Cn], st2[0:Cn], 1.0 / 64, st3[0:Cn], ALU.mult, ALU.subtract)
                        act(st2[0:Cn], st2[0:Cn], AF.Sqrt, bias=C("epsw")[0:Cn, 0:1])
                        recip(st2[0:Cn], st2[0:Cn])
                        bc = lambda t_: V(t_.ap[0:Cn].unsqueeze(2).to_broadcast([Cn, 2, 64]), t_.keys)
                        tt(Y3, O3, bc(st1), ALU.subtract)
                        tt(Y3, Y3, bc(st2), ALU.mult)
                        ps = psum()
                        tr(ps[:, 0:Cn], Yn[0:Cn], ident[0:Cn, 0:Cn])
                        stt(tA[:, 0:Cn], ps[:, 0:Cn], pcol("rw_gn", p), bonus[:, cs], ALU.mult, ALU.add)
                        tt(y[:, p, cs], tA[:, 0:Cn], g_t[:, cs], ALU.mult)
                for pc in range(8):
                    w = wload("w_o", pc)
                    for oc2 in range(2):
                        oc = pc * 2 + oc2
                        ps = psum()
                        for kc in range(KC):
                            mm(ps[:, 0:nt], w[:, kc, oc2 * 128:(oc2 + 1) * 128], y[:, kc, :], kc == 0, kc == KC - 1)
                        stt(x_t[:, oc, 0:nt], ps[:, 0:nt], modv(1, 2, oc, row), x_t[:, oc, 0:nt], ALU.mult, ALU.add)
                AR.top = m

            def final_out(nt, dst):
                m = AR.top
                sq, _ = AR.tile(F32, (nt,))
                rstd, _ = AR.tile(F32, (nt,))
                o, _ = AR.tile(F32, (KC, nt))
                ps = psum()
                for kc in range(KC):
                    act(sq, x_t[:, kc, 0:nt], AF.Square)
                    mm(ps[:, 0:nt], ones, sq, kc == 0, kc == KC - 1)
                act(rstd, ps[:, 0:nt], AF.Sqrt, bias=C("eps6")[:, 0:1], scale=1.0 / D)
                recip(rstd, rstd)
                fin = P("fin")
                for kc in range(KC):
                    stt(o[:, kc, :], x_t[:, kc, 0:nt], fin[:, kc:kc + 1], rstd, ALU.mult, ALU.mult)
                S.dma("pool", lambda e: e.dma_start(out=dst(), in_=o.ap), reads=S._keys([o]))
                AR.top = m

            dmy = nc.dram_tensor("dmy", [2, 64], F32)

            def pad_dmas(st0):
                for q in ("sp", "pool"):
                    m = S.dcnt[q] - st0["dcnt"][q]
                    for _ in range((-m) % NDMA_SEM):
                        S.dma(q, lambda e: e.dma_start(out=dmy.ap()[0:1, 0:16], in_=dmy.ap()[1:2, 0:16]))

            def tidx(ti):
                return lambda: (S.loop_var if S.loop_var is not None else ti)

            def init_states(is_sample, si):
                if is_sample:
                    dma("sp", sret_t, st_ret[si].rearrange("p (a b) -> p a b", b=DV))
                    dma("sp", swkv_t, st_wkv[si].rearrange("p (a b) -> p a b", b=128))
                    dma("sp", shift_t, st_shift[si].rearrange("p (a b) -> p a b", b=1))
                    dma("sp", conv_t, st_conv[si].rearrange("p (a b c) -> p a b c", b=FC, c=2))
                else:
                    memset(sret_t, 0.0)
                    memset(swkv_t, 0.0)
                    memset(shift_t, 0.0)
                    memset(conv_t, 0.0)
                cp(sretb_t, sret_t, eng="pool")

            def run_tile(is_sample, si, ti):
                nt = TS if is_sample else NT
                row = 1 + si if is_sample else 0
                psi[0] = 0
                wbi[0] = 0
                if is_sample:
                    dma("sp", x_t[:, :, 0:nt], xs_in[si].rearrange("p (a b) -> p a b", b=nt))
                    dma("sp", ropet[:, :, 0:nt], rope_s.ap().rearrange("p (a b) -> p a b", b=nt))
                else:
                    tv = tidx(ti)
                    S.dma("sp", lambda e: e.dma_start(out=x_t.ap, in_=x_in.ap()[tv()].rearrange("p (a b) -> p a b", b=NT)), writes=S._keys([x_t]))
                    S.dma("sp", lambda e: e.dma_start(out=ropet.ap, in_=rope.ap()[tv()].rearrange("p (a b) -> p a b", b=NT)), writes=S._keys([ropet]))
                retention(nt, row, is_sample)
                ffn(nt, 0, row)
                if STOP_AFTER >= 3:
                    rwkv(nt, row)
                if STOP_AFTER >= 4:
                    ffn(nt, 1, row)
                if is_sample:
                    final_out(nt, lambda: ys_out[si].rearrange("p (a b) -> p a b", b=nt))
                else:
                    tv = tidx(ti)
                    final_out(nt, lambda: y_out.ap()[tv()].rearrange("p (a b) -> p a b", b=NT))

            def write_states(oi):
                dma("pool", o_ret[oi].rearrange("p (a b) -> p a b", b=DV), sret_t)
                dma("pool", o_wkv[oi].rearrange("p (a b) -> p a b", b=128), swkv_t)
                dma("pool", o_shift[oi].rearrange("p (a b) -> p a b", b=1), shift_t)
                dma("pool", o_conv[oi].rearrange("p (a b c) -> p a b c", b=FC, c=2), conv_t)

            for si in range(NSP):
                init_states(True, si)
                run_tile(True, si, 0)
                write_states(1 + si)
            init_states(False, 0)
            n_run = NTILE_RUN
            if n_run <= 4:
                for ti in range(n_run):
                    st0 = S.state()
                    run_tile(False, 0, ti)
                    pad_dmas(st0)
            else:
                k = 0
                prev = None
                while True:
                    st0 = S.state()
                    run_tile(False, 0, k)
                    pad_dmas(st0)
                    st1 = S.state()
                    delta = S.deltas(st0, st1)
                    cur = S.norm_tile(st0, st1, k, delta)
                    if prev is not None and cur == prev[0] and delta == prev[1]:
                        S.restore(st0)
                        S.set_loop(prev[2], st0, k - 1, n_run, delta)
                        print("steady state at tile", k - 1)
                        break
                    prev = (cur, delta, st0)
                    k += 1
                    assert k < 8, "no steady state"
            S.emit(block)
        with nc.Block() as block2:
            write_states(0)
            S.emit(block2)
        print("ops recorded:", S.nops, "arena top", AR.top, "peak", AR.peak)
    return nc


POFF = {}
COFF = {}
_o = 0
for _n, _w in (("ada_b", 192), ("ret_gn", 32), ("mu", 96), ("w0", 16), ("a0", 16), ("k_k", 16), ("k_a", 16),
               ("r_k", 16), ("rw_gn", 16), ("conv_w", 2 * 3 * FC), ("conv_b", 2 * FC), ("fin", 16)):
    POFF[_n] = (_o, _w)
    _o += _w
NPRM = _o
_o = 0
for _n, _w in (("ident", 128), ("ones", 128), ("bones", 128), ("eps6", 1), ("eps5", 1), ("epsw", 1), ("mhalf", 1),
               ("rmask", 8 * 128), ("rcross", 8 * 128), ("rinto", 8),
               ("rmasks", 8 * 16), ("rcrosss", 8 * 16), ("rintos", 8),
               ("mstrict", 128), ("mincl", 128), ("mstrictT", 128)):
    COFF[_n] = (_o, _w)
    _o += _w
NCST = _o
NTILE_RUN = NTILE


def _consts():
    c = np.zeros((128, NCST), np.float32)

    def put(n, a):
        o, w = COFF[n]
        c[:a.shape[0], o:o + w] = a
    put("ident", np.eye(128, dtype=np.float32))
    put("ones", np.ones((128, 128), np.float32))
    bo = np.zeros((128, 128), np.float32)
    bo[:64, :64] = 1
    bo[64:, 64:] = 1
    put("bones", bo)
    put("eps6", np.full((128, 1), 1e-6, np.float32))
    put("eps5", np.full((128, 1), 1e-5, np.float32))
    put("epsw", np.full((128, 1), 64e-5, np.float32))
    put("mhalf", np.full((128, 1), -0.5, np.float32))
    gam = 1.0 - 2.0 ** (-5.0 - np.arange(8, dtype=np.float64))
    for sfx, L in (("", 128), ("s", 16)):
        idx = np.arange(L, dtype=np.float64)
        diff = idx[None, :] - idx[:, None]
        mk = np.where(diff >= 0, gam[:, None, None] ** np.maximum(diff, 0)[None], 0.0)
        put("rmask" + sfx, mk.transpose(1, 0, 2).reshape(L, 8 * L).astype(np.float32))
        cr = gam[:, None] ** (idx[None, :] + 1.0)
        put("rcross" + sfx, np.broadcast_to(cr.reshape(1, 8 * L), (128, 8 * L)).astype(np.float32))
        it = gam[:, None] ** (L - 1.0 - idx[None, :])
        put("rinto" + sfx, it.T.astype(np.float32))
    s = np.arange(128)
    put("mstrict", (s[:, None] < s[None, :]).astype(np.float32))
    put("mincl", (s[:, None] <= s[None, :]).astype(np.float32))
    put("mstrictT", (s[:, None] > s[None, :]).astype(np.float32))
    return c


def _rope(pos):
    half = 128
    inv = 10000.0 ** (-np.arange(half, dtype=np.float32) / half)
    ang = pos.astype(np.float32)[None, :] * inv[:, None]
    return np.cos(ang).astype(np.float32), np.sin(ang).astype(np.float32)


_NC_CACHE = {}


def kernel(**inp):
    f = lambda k: np.asarray(inp[k], np.float32)
    if "nc" not in _NC_CACHE:
        _NC_CACHE["nc"] = build_program()
    nc = _NC_CACHE["nc"]
    sh = {}
    aw = f("ada_w")
    sh["ada_w"] = np.concatenate([_pieces(aw[l], 128) for l in range(2)], 0)
    wi = f("ret_w_in")[0]
    cols = []
    for h in range(RH):
        cols += [wi[:, h * DK:(h + 1) * DK], wi[:, 2048 + h * DK:2048 + (h + 1) * DK],
                 wi[:, 4096 + h * DV:4096 + (h + 1) * DV], wi[:, 8192 + h * DV:8192 + (h + 1) * DV]]
    sh["w_ret_in"] = _pieces(np.concatenate(cols, 1), 256)
    sh["w_ret_out"] = _pieces(f("ret_w_out")[0], 128)
    for l in range(2):
        g_ = _pieces(f("ffn_w_gate")[l], 256)
        u_ = _pieces(f("ffn_w_up")[l], 256)
        gu = np.empty((44,) + g_.shape[1:], np.float32)
        gu[0::2] = g_
        gu[1::2] = u_
        sh["w_gu%d" % l] = gu
        sh["w_dn%d" % l] = _pieces(f("ffn_w_down")[l], 128)
    st2 = lambda w: np.concatenate([w, w], 0)
    r_, k_, v_ = (_pieces(st2(f(n)[0]), 128) for n in ("rwkv_w_r", "rwkv_w_k", "rwkv_w_v"))
    rkv = np.empty((48,) + r_.shape[1:], np.float32)
    rkv[0::3], rkv[1::3], rkv[2::3] = r_, k_, v_
    sh["w_rkv"] = rkv
    pad = lambda w: np.concatenate([w, np.zeros((w.shape[0], 128 - w.shape[1]), np.float32)], 1)
    sh["w_lora1"] = np.concatenate([_pieces(st2(pad(f("rwkv_w1")[0])), 128), _pieces(st2(pad(f("rwkv_a1")[0])), 128),
                                    _pieces(st2(f("rwkv_g1")[0]), 128)], 0)
    sh["w_w_o"] = _pieces(f("rwkv_w_o")[0], 256)
    l2 = np.zeros((128, 3 * D), np.float32)
    l2[:96, 0:D] = f("rwkv_w2")[0]
    l2[:96, D:2 * D] = f("rwkv_a2")[0]
    sh["l2w"] = l2
    g2 = f("rwkv_g2")[0]
    sh["g2w"] = np.concatenate([g2[0:128], g2[128:256]], 1)
    prm = np.zeros((128, NPRM), np.float32)

    def putp(n, a):
        o, w = POFF[n]
        prm[:, o:o + w] = a
    ab = f("ada_b")
    putp("ada_b", np.concatenate([_fm(ab[0]), _fm(ab[1])], 1))
    putp("ret_gn", _fm(f("ret_gn_gain")[0]))
    putp("mu", np.concatenate([_fm(f("rwkv_mu")[0][i]) for i in range(6)], 1))
    for n, k in (("w0", "rwkv_w0"), ("a0", "rwkv_a0"), ("k_k", "rwkv_k_k"), ("k_a", "rwkv_k_a"), ("rw_gn", "rwkv_gn_gain")):
        putp(n, _fm(f(k)[0]))
    putp("r_k", _fm(f("rwkv_r_k")[0].reshape(-1)))
    cw = f("ffn_conv_w")
    putp("conv_w", np.concatenate([_fm(cw[l, j]) for l in range(2) for j in range(3)], 1))
    cb = f("ffn_conv_b")
    putp("conv_b", np.concatenate([_fm(cb[l]) for l in range(2)], 1))
    putp("fin", _fm(f("final_gain")))
    sh["prm"] = prm
    sh["cst"] = _consts()
    cosp, sinp = _rope(np.arange(SEQ))
    rp = np.stack([cosp, sinp], 1)
    sh["rope"] = np.ascontiguousarray(rp.reshape(128, 2, NTILE, NT).transpose(2, 0, 1, 3)).reshape(NTILE, 128, 2 * NT)
    coss, sins = _rope(PAST + np.arange(TS))
    sh["rope_s"] = np.stack([coss, sins], 1).reshape(128, 2 * TS)
    xp, xs = f("x_prompt"), f("x_sample")
    cpv, csv = f("c_prompt"), f("c_sample")
    xt_cache = {}
    in_maps = []
    for c in range(NCORES):
        sq = c % 2
        sidx = [c * NSP + i for i in range(NSP)]
        if sq not in xt_cache:
            xT = xp[sq].T
            xt_cache[sq] = np.ascontiguousarray(xT.reshape(KC, 128, NTILE, NT).transpose(2, 1, 0, 3)).reshape(NTILE, 128, KC * NT)
        m = dict(sh)
        m["x_in"] = xt_cache[sq]
        m["xs_in"] = np.stack([np.ascontiguousarray(xs[j].T.reshape(KC, 128, TS).transpose(1, 0, 2)).reshape(128, KC * TS) for j in sidx], 0)
        m["cvec"] = np.stack([_fm(cpv[sq])] + [_fm(csv[j]) for j in sidx], 2).reshape(128, KC * NROW)
        l_ret, l_wkv, l_sh, l_cv = [], [], [], []
        for j in sidx:
            sr = f("state_ret")[0, j]
            l_ret.append(np.ascontiguousarray(sr.reshape(RH, 2, 128, DV).transpose(2, 0, 1, 3)).reshape(128, RH * 2 * DV))
            sw = f("state_rwkv_wkv")[0, j]
            t = np.zeros((16, 2, 64, 2, 64), np.float32)
            swT = sw.transpose(0, 2, 1).reshape(16, 2, 64, 64)
            for hh in range(2):
                t[:, hh, :, hh, :] = swT[:, hh]
            l_wkv.append(np.ascontiguousarray(t.transpose(1, 2, 0, 3, 4)).reshape(128, 16 * 128))
            l_sh.append(_fm(f("state_rwkv_shift")[0, j]))
            sc = f("state_ffn_conv")[:, j]
            l_cv.append(np.ascontiguousarray(sc.reshape(2, 2, FC, 128).transpose(3, 0, 2, 1)).reshape(128, 2 * FC * 2))
        m["st_ret"], m["st_wkv"], m["st_shift"], m["st_conv"] = (np.stack(l, 0) for l in (l_ret, l_wkv, l_sh, l_cv))
        in_maps.append(m)
    res = run_bass_kernel_spmd(nc, in_maps, core_ids=list(range(NCORES)))
    R = res.results
    def unx(a, nt, ntile):
        return a.reshape(ntile, 128, KC, nt).transpose(0, 3, 2, 1).reshape(ntile * nt, D)
    y_prompt = np.stack([unx(R[b]["y_out"], NT, NTILE) for b in range(2)], 0)
    y_sample = np.stack([unx(R[j // NSP]["ys_out"][j % NSP], TS, 1) for j in range(8)], 0)

    def un_ret(a):
        return a.reshape(128, RH, 2, DV).transpose(1, 2, 0, 3).reshape(RH, DK, DV)

    def un_wkv(a):
        t = a.reshape(2, 64, 16, 2, 64)
        o = np.stack([t[hh, :, :, hh, :] for hh in range(2)], 0)
        return o.transpose(2, 0, 3, 1).reshape(32, 64, 64)

    def un_conv(a):
        return a.reshape(128, 2, FC, 2).transpose(1, 3, 2, 0).reshape(2, 2, FF)

    def un_vec(a):
        return a.T.reshape(-1)
    outs_p = [np.stack([fn(R[b][k][0]) for b in range(2)], 0) for k, fn in
              (("o_ret", un_ret), ("o_wkv", un_wkv), ("o_shift", un_vec))]
    outs_s = [np.stack([fn(R[j // NSP][k][1 + j % NSP]) for j in range(8)], 0) for k, fn in
              (("o_ret", un_ret), ("o_wkv", un_wkv), ("o_shift", un_vec))]
    pc = np.stack([un_conv(R[b]["o_conv"][0]) for b in range(2)], 1)
    sc_ = np.stack([un_conv(R[j // NSP]["o_conv"][1 + j % NSP]) for j in range(8)], 1)
    return (y_prompt.astype(np.float32), y_sample.astype(np.float32),
            outs_p[0][None].astype(np.float32), outs_p[1][None].astype(np.float32), outs_p[2][None].astype(np.float32), pc.astype(np.float32),
            outs_s[0][None].astype(np.float32), outs_s[1][None].astype(np.float32), outs_s[2][None].astype(np.float32), sc_.astype(np.float32))
```

```python
import contextlib
import math
import numpy as np
import concourse.bass as bass
import concourse.mybir as mybir
from concourse.bass_utils import run_bass_kernel_spmd

F32 = mybir.dt.float32
BF16 = mybir.dt.bfloat16
ALU = mybir.AluOpType
AF = mybir.ActivationFunctionType
AX = mybir.AxisListType

D = 2048
KC = 16
SEQ = 16384
NT = 256
NTILE = SEQ // NT
TS = 16
PAST = 4096
RH, DK, DV = 8, 256, 512
FF = 5632
FC = 44
HN = 64
GRAN = 512
SAME_ENGINE_SYNC = True
NDMA_SEM = 12
STOP_AFTER = 99
import os
RW_STAGE = int(os.environ.get('RW_STAGE', '9'))
NCORES = 2
NSP = 8 // NCORES
NROW = 1 + NSP


class V:
    def __init__(self, ap, keys):
        self.ap = ap
        self.keys = keys

    def __getitem__(self, idx):
        return V(self.ap[idx], self.keys)


class Sched:
    ENGS = ("pe", "act", "dve", "pool", "sp")

    def __init__(self, nc, stack):
        self.nc = nc
        self.stack = stack
        self.sem = {e: stack.enter_context(nc.semaphore("s_" + e)) for e in ("pe", "act", "dve", "pool")}
        self.dsem = {q: [stack.enter_context(nc.semaphore("d_%s%d" % (q, i))) for i in range(NDMA_SEM)]
                     for q in ("sp", "pool", "act")}
        self.dcnt = {q: 0 for q in ("sp", "pool", "act")}
        self.cnt = {e: 0 for e in ("pe", "act", "dve", "pool")}
        self.waited = {e: {} for e in self.ENGS}
        self.ops = {e: [] for e in self.ENGS}
        self.lastw = {}
        self.readers = {}
        self.nops = 0

    @staticmethod
    def _keys(vs):
        out = []
        for v in vs:
            if isinstance(v, V):
                out.extend(v.keys)
            else:
                out.append(v)
        return out

    def _deps(self, rk, wk):
        deps = []
        lw, rd = self.lastw, self.readers
        for k in rk:
            w = lw.get(k)
            if w is not None:
                deps.append(w)
        for k in wk:
            w = lw.get(k)
            if w is not None:
                deps.append(w)
            r = rd.get(k)
            if r:
                deps.extend(r.values())
        return deps

    def _mark(self, eng, tok, rk, wk):
        rd = self.readers
        for k in rk:
            d = rd.get(k)
            if d is None:
                rd[k] = {eng: tok}
            else:
                d[eng] = tok
        for k in wk:
            self.lastw[k] = tok
            rd[k] = None

    def _waits(self, eng, deps, is_dma):
        waits = []
        wd = self.waited[eng]
        for (sem, val, seng) in deps:
            if seng == eng and not is_dma:
                if eng == "pe" or not SAME_ENGINE_SYNC:
                    continue
            if wd.get(id(sem), 0) >= val:
                continue
            wd[id(sem)] = val
            waits.append((sem, val))
        return waits

    def op(self, eng, fn, reads=(), writes=()):
        rk, wk = self._keys(reads), self._keys(writes)
        waits = self._waits(eng, self._deps(rk, wk), False)
        self.cnt[eng] += 1
        tok = (self.sem[eng], self.cnt[eng], eng)
        self.ops[eng].append((waits, fn, self.sem[eng], 1))
        self._mark(eng, tok, rk, wk)
        self.nops += 1

    def dma(self, q, fn, reads=(), writes=()):
        rk, wk = self._keys(reads), self._keys(writes)
        deps = self._deps(rk, wk)
        m = self.dcnt[q]
        self.dcnt[q] += 1
        slot, rnd = m % NDMA_SEM, m // NDMA_SEM
        sem = self.dsem[q][slot]
        if rnd > 0:
            deps.append((sem, 16 * rnd, "dma"))
        deps = [(s, v, "x") if e == q else (s, v, e) for (s, v, e) in deps]
        waits = self._waits(q, deps, True)
        tok = (sem, 16 * (rnd + 1), "dma_" + q)
        self.ops[q].append((waits, fn, sem, 16))
        self._mark("dma_" + q + str(slot), tok, rk, wk)
        self.nops += 1

    def state(self):
        import copy
        return dict(cnt=dict(self.cnt), dcnt=dict(self.dcnt), waited={e: dict(d) for e, d in self.waited.items()},
                    lastw=dict(self.lastw), readers={k: (dict(v) if v else v) for k, v in self.readers.items()},
                    lens={e: len(v) for e, v in self.ops.items()}, nops=self.nops)

    def restore(self, st):
        self.cnt, self.dcnt = dict(st["cnt"]), dict(st["dcnt"])
        self.waited = {e: dict(d) for e, d in st["waited"].items()}
        self.lastw = dict(st["lastw"])
        self.readers = {k: (dict(v) if v else v) for k, v in st["readers"].items()}
        for e in self.ENGS:
            del self.ops[e][st["lens"][e]:]
        self.nops = st["nops"]

    def deltas(self, st0, st1):
        d = {}
        for e in ("pe", "act", "dve", "pool"):
            d[id(self.sem[e])] = st1["cnt"][e] - st0["cnt"][e]
        for q in ("sp", "pool", "act"):
            m = st1["dcnt"][q] - st0["dcnt"][q]
            assert m % NDMA_SEM == 0, (q, m)
            for sm in self.dsem[q]:
                d[id(sm)] = 16 * (m // NDMA_SEM)
        return d

    def norm_tile(self, st0, st1, k, delta):
        out = {}
        for e in self.ENGS:
            out[e] = [tuple((id(s_), v - k * delta[id(s_)]) for (s_, v) in w[0])
                      for w in self.ops[e][st0["lens"][e]:st1["lens"][e]]]
        return out

    def set_loop(self, st0, st1, k, n_end, delta):
        self.loop = dict(lo=st0["lens"], hi=st1["lens"], k=k, n=n_end, delta=delta)
        extra_iters = n_end - 1 - k
        for e in ("pe", "act", "dve", "pool"):
            self.cnt[e] += extra_iters * (st1["cnt"][e] - st0["cnt"][e])
        for q in ("sp", "pool", "act"):
            self.dcnt[q] += extra_iters * (st1["dcnt"][q] - st0["dcnt"][q])

    loop = None
    loop_var = None

    def emit(self, block):
        ops = self.ops
        self.ops = {e: [] for e in self.ENGS}
        loop = self.loop
        self.loop = None
        sched = self

        def body(ename, lst, extra):
            def run(eng, sub, base, scratch):
                for (waits, fn, sem, inc) in sub:
                    for (s, v) in waits:
                        dl = loop["delta"][id(s)] if base is not None else 0
                        if dl:
                            eng.reg_add(scratch, base[dl], v)
                            eng.wait_ge(s, scratch)
                        else:
                            eng.wait_ge(s, v)
                    fn(eng).then_inc(sem, inc)

            def _f(eng):
                if loop is None:
                    run(eng, lst, None, None)
                else:
                    lo, hi = loop["lo"][ename], loop["hi"][ename]
                    run(eng, lst[:lo], None, None)
                    if hi > lo:
                        with contextlib.ExitStack() as rs:
                            dls = sorted({loop["delta"][id(s_)] for (w, _, _, _) in lst[lo:hi] for (s_, _) in w} - {0})
                            base = {d_: rs.enter_context(eng.register("lb_%s_%d" % (ename, j))) for j, d_ in enumerate(dls)}
                            scratch = rs.enter_context(eng.register("ls_%s" % ename))
                            for d_ in dls:
                                eng.reg_mov(base[d_], 0)
                            with eng.Fori(loop["k"], loop["n"]) as i:
                                sched.loop_var = i
                                run(eng, lst[lo:hi], base, scratch)
                                sched.loop_var = None
                                for d_ in dls:
                                    eng.reg_add(base[d_], base[d_], d_)
                    run(eng, lst[hi:], None, None)
                for (s, v) in extra:
                    eng.wait_ge(s, v)
            return _f

        extra = {e: [] for e in self.ENGS}
        for q in ("sp", "pool", "act"):
            m = self.dcnt[q]
            for slot in range(min(m, NDMA_SEM)):
                n = (m - slot + NDMA_SEM - 1) // NDMA_SEM
                extra[q].append((self.dsem[q][slot], 16 * n))
        block.tensor(body("pe", ops["pe"], extra["pe"]))
        block.scalar(body("act", ops["act"], extra["act"]))
        block.vector(body("dve", ops["dve"], extra["dve"]))
        block.gpsimd(body("pool", ops["pool"], extra["pool"]))
        block.sync(body("sp", ops["sp"], extra["sp"]))
        self.lastw = {}
        self.readers = {}


class Arena:
    def __init__(self, nc, stack, nbytes):
        self.nbytes = nbytes
        self.t = stack.enter_context(nc.sbuf_tensor("arena", [128, nbytes // 4], F32))
        self.top = 0

    def alloc(self, nbytes):
        lo = self.top
        self.top = lo + ((nbytes + GRAN - 1) // GRAN) * GRAN
        assert self.top <= self.nbytes, ("arena overflow", self.top)
        self.peak = max(getattr(self, 'peak', 0), self.top)
        return lo

    def view(self, lo, dt, shape, np_=128):
        esz = 2 if dt == BF16 else 4
        n = int(np.prod(shape))
        nb = n * esz
        ap = self.t[0:np_, lo // 4:(lo + nb + 3) // 4]
        if dt == BF16:
            ap = ap.bitcast(BF16)
        if len(shape) == 2:
            ap = ap.rearrange("p (a b) -> p a b", b=shape[1])
        elif len(shape) == 3:
            ap = ap.rearrange("p (a b c) -> p a b c", b=shape[1], c=shape[2])
        keys = [("A", g) for g in range(lo // GRAN, (lo + nb - 1) // GRAN + 1)]
        return V(ap, keys)

    def tile(self, dt, shape, np_=128):
        esz = 2 if dt == BF16 else 4
        lo = self.alloc(int(np.prod(shape)) * esz)
        return self.view(lo, dt, shape, np_), lo


WSPEC = {
    "ret_in": (16, 256, 48),
    "ret_out": (32, 128, 16),
    "gu0": (16, 256, 44), "gu1": (16, 256, 44),
    "dn0": (44, 128, 16), "dn1": (44, 128, 16),
    "rkv": (32, 128, 48),
    "lora1": (32, 128, 4),
    "w_o": (16, 256, 8),
}


def _pieces(w, fw):
    K, F = w.shape
    return np.ascontiguousarray(w.reshape(K // 128, 128, F // fw, fw).transpose(2, 1, 0, 3)).reshape(F // fw, 128, (K // 128) * fw)


def _fm(v):
    return np.ascontiguousarray(v.reshape(-1, 128).T)


def build_program():
    nc = bass.Bass("TRN2", target_bir_lowering=False)
    dr = lambda n, sh, dt=F32, kind="ExternalInput": nc.dram_tensor(n, sh, dt, kind=kind)
    x_in = dr("x_in", [NTILE, 128, KC * NT])
    xs_in = dr("xs_in", [NSP, 128, KC * TS])
    cvec = dr("cvec", [128, KC * NROW])
    ada_w = dr("ada_w", [2 * 96, 128, KC * 128])
    win = {k: dr("w_" + k, [n, 128, kc * fw]) for k, (kc, fw, n) in WSPEC.items()}
    l2w = dr("l2w", [128, 3 * D])
    g2w = dr("g2w", [128, 2 * D])
    prm = dr("prm", [128, NPRM])
    cst = dr("cst", [128, NCST])
    rope = dr("rope", [NTILE, 128, 2 * NT])
    rope_s = dr("rope_s", [128, 2 * TS])
    st_ret = dr("st_ret", [NSP, 128, RH * 2 * DV])
    st_wkv = dr("st_wkv", [NSP, 128, 16 * 128])
    st_shift = dr("st_shift", [NSP, 128, KC])
    st_conv = dr("st_conv", [NSP, 128, 2 * FC * 2])
    y_out = dr("y_out", [NTILE, 128, KC * NT], kind="ExternalOutput")
    ys_out = dr("ys_out", [NSP, 128, KC * TS], kind="ExternalOutput")
    o_ret = dr("o_ret", [NROW, 128, RH * 2 * DV], kind="ExternalOutput")
    o_wkv = dr("o_wkv", [NROW, 128, 16 * 128], kind="ExternalOutput")
    o_shift = dr("o_shift", [NROW, 128, KC], kind="ExternalOutput")
    o_conv = dr("o_conv", [NROW, 128, 2 * FC * 2], kind="ExternalOutput")
    wscr = {k: nc.dram_tensor("ws_" + k, [n, 128, kc * fw], BF16) for k, (kc, fw, n) in WSPEC.items()}
    l2scr = nc.dram_tensor("ws_l2", [128, 3 * D], BF16)
    g2scr = nc.dram_tensor("ws_g2", [128, 2 * D], BF16)

    with contextlib.ExitStack() as st:
        S = Sched(nc, st)
        AR = Arena(nc, st, 188 * 1024)
        PS = [V(st.enter_context(nc.psum_tensor("ps%d" % i, [128, 512], F32))[:], ["ps%d" % i]) for i in range(8)]
        psi = [0]

        def psum():
            psi[0] = (psi[0] + 1) % 8
            return PS[psi[0]]

        def mm(out, lhsT, rhs, start, stop):
            S.op("pe", lambda e: e.matmul(out.ap, lhsT=lhsT.ap, rhs=rhs.ap, start=start, stop=stop),
                 reads=[lhsT, rhs], writes=[out])

        def tr(out, in_, ident):
            S.op("pe", lambda e: e.transpose(out.ap, in_.ap, ident.ap), reads=[in_, ident], writes=[out])

        def act(out, in_, func, bias=0.0, scale=1.0, extra=()):
            rd = [in_] + [b for b in (bias, scale) if isinstance(b, V)] + list(extra)
            b_ = bias.ap if isinstance(bias, V) else bias
            s_ = scale.ap if isinstance(scale, V) else scale
            S.op("act", lambda e: e.activation(out=out.ap, in_=in_.ap, func=func, bias=b_, scale=s_), reads=rd, writes=[out])

        def tt(out, a, b, op, eng="dve"):
            S.op(eng, lambda e: e.tensor_tensor(out=out.ap, in0=a.ap, in1=b.ap, op=op), reads=[a, b], writes=[out])

        def ts(out, a, s1, s2, op0, op1=None, eng="dve"):
            rd = [a] + [s for s in (s1, s2) if isinstance(s, V)]
            a1 = s1.ap if isinstance(s1, V) else s1
            a2 = s2.ap if isinstance(s2, V) else s2
            if op1 is None:
                S.op(eng, lambda e: e.tensor_scalar(out=out.ap, in0=a.ap, scalar1=a1, scalar2=None, op0=op0), reads=rd, writes=[out])
            else:
                S.op(eng, lambda e: e.tensor_scalar(out=out.ap, in0=a.ap, scalar1=a1, scalar2=a2, op0=op0, op1=op1), reads=rd, writes=[out])

        def stt(out, a, s, b, op0, op1, eng="dve"):
            rd = [a, b] + ([s] if isinstance(s, V) else [])
            s_ = s.ap if isinstance(s, V) else s
            S.op(eng, lambda e: e.scalar_tensor_tensor(out=out.ap, in0=a.ap, scalar=s_, in1=b.ap, op0=op0, op1=op1), reads=rd, writes=[out])

        def cp(out, in_, eng="dve"):
            if eng == "act":
                S.op("act", lambda e: e.copy(out=out.ap, in_=in_.ap), reads=[in_], writes=[out])
            else:
                S.op(eng, lambda e: e.tensor_copy(out=out.ap, in_=in_.ap), reads=[in_], writes=[out])

        def memset(out, val, eng="pool"):
            S.op(eng, lambda e: e.memset(out.ap, val), writes=[out])

        def recip(out, in_):
            S.op("dve", lambda e: e.reciprocal(out=out.ap, in_=in_.ap), reads=[in_], writes=[out])

        def dma(q, out, in_, okeys=None, ikeys=None):
            oa = out.ap if isinstance(out, V) else out
            ia = in_.ap if isinstance(in_, V) else in_
            rd = [in_] if isinstance(in_, V) else list(ikeys or [])
            wr = [out] if isinstance(out, V) else list(okeys or [])
            S.dma(q, lambda e: e.dma_start(out=oa, in_=ia), reads=rd, writes=wr)

        prm_t, _ = AR.tile(F32, (NPRM,))
        cst_t, _ = AR.tile(F32, (NCST,))
        mod_t, _ = AR.tile(F32, (2, 96, NROW))
        sret_t, _ = AR.tile(F32, (RH * 2, DV))
        swkv_t, _ = AR.tile(F32, (16, 128))
        x_t, _ = AR.tile(F32, (KC, NT))
        shift_t, _ = AR.tile(F32, (KC, 1))
        conv_t, _ = AR.tile(F32, (2, FC, 2))
        WB = [AR.tile(BF16, (6144,)) for _ in range(3)]
        wbi = [0]
        base_top = AR.top

        P = lambda name: V(prm_t.ap[:, POFF[name][0]:POFF[name][0] + POFF[name][1]], prm_t.keys)
        C = lambda name: V(cst_t.ap[:, COFF[name][0]:COFF[name][0] + COFF[name][1]], cst_t.keys)
        ident = C("ident")
        ones = C("ones")
        bones = C("bones")

        def wslot():
            wbi[0] = (wbi[0] + 1) % 3
            return WB[wbi[0]]

        def wload(name, piece, dt=BF16, src=None):
            kcn, fw, _ = WSPEC[name] if name in WSPEC else (KC, 128, 0)
            (wv, lo) = wslot()
            v = AR.view(lo, dt, (kcn, fw))
            if src is None:
                dma("sp", v, wscr[name][piece].rearrange("p (a b) -> p a b", b=fw), ikeys=[("ws", name, piece)])
            else:
                dma("sp", v, src.rearrange("p (a b) -> p a b", b=fw))
            return v

        with nc.Block() as block:
            dma("sp", prm_t, prm.ap())
            dma("sp", cst_t, cst.ap())

            m0 = AR.top
            stg = [AR.tile(F32, (6144,)) for _ in range(2)]
            cengs = ["dve", "pool", "act"]
            ci = 0
            mu_v = P("mu")
            for name, (kcn, fw, npc) in WSPEC.items():
                for pc in range(npc):
                    nh = 2 if (kcn * fw * 4 > 24576 or name in ("rkv", "lora1")) else 1
                    hk = kcn // nh
                    for hh in range(nh):
                        (sv, slo) = stg[ci % 2]
                        (bv, blo) = WB[ci % 2]
                        s3 = AR.view(slo, F32, (hk, fw))
                        b3 = AR.view(blo, BF16, (hk, fw))
                        src = win[name][pc].rearrange("p (a b) -> p a b", b=fw)[:, hh * hk:(hh + 1) * hk, :]
                        dst = wscr[name][pc].rearrange("p (a b) -> p a b", b=fw)[:, hh * hk:(hh + 1) * hk, :]
                        dma("sp", s3, src)
                        eng = cengs[ci % 3]
                        mus = None
                        if name == "rkv" and hh == 1:
                            mus = {0: 0, 1: 2, 2: 3}[pc % 3]
                        if name == "lora1" and hh == 1:
                            mus = {0: 1, 1: 4, 2: 5, 3: 5}[pc]
                        if mus is not None:
                            mub = V(mu_v.ap[:, mus * 16:(mus + 1) * 16].unsqueeze(2).to_broadcast([128, 16, fw]), mu_v.keys)
                            tt(b3, s3, mub, ALU.mult, eng="dve" if eng == "act" else eng)
                        else:
                            cp(b3, s3, eng=eng)
                        dma("pool", dst, b3, okeys=[("ws", name, pc)])
                        ci += 1
            for (src_t, dst_t, ncol) in ((l2w, l2scr, 3 * D), (g2w, g2scr, 2 * D)):
                for j in range(ncol // 2048):
                    (sv, slo) = stg[ci % 2]
                    (bv, blo) = WB[ci % 2]
                    s2 = AR.view(slo, F32, (2048,))
                    b2 = AR.view(blo, BF16, (2048,))
                    dma("sp", s2, src_t.ap()[:, j * 2048:(j + 1) * 2048])
                    cp(b2, s2, eng=cengs[ci % 3])
                    dma("pool", dst_t.ap()[:, j * 2048:(j + 1) * 2048], b2, okeys=[("wsl", dst_t.name, j)])
                    ci += 1
            AR.top = m0

            m0 = AR.top
            cv, _ = AR.tile(F32, (KC, NROW))
            dma("sp", cv, cvec.ap().rearrange("p (a b) -> p a b", b=NROW))
            act(cv, cv, AF.Silu)
            for l in range(2):
                for oc in range(96):
                    w = wload("ada", 0, dt=F32, src=ada_w[l * 96 + oc])
                    ps = psum()
                    for kc in range(KC):
                        mm(ps[:, 0:NROW], w[:, kc, :], cv[:, kc, :], kc == 0, kc == KC - 1)
                    bcol = V(prm_t.ap[:, POFF["ada_b"][0] + l * 96 + oc:POFF["ada_b"][0] + l * 96 + oc + 1], prm_t.keys)
                    n = oc // 16
                    if n in (1, 4):
                        ts(mod_t[:, l, oc, :], ps[:, 0:NROW], bcol, 1.0, ALU.add, ALU.add)
                    else:
                        ts(mod_t[:, l, oc, :], ps[:, 0:NROW], bcol, None, ALU.add)
            AR.top = m0

            def modv(l, n, kc, row):
                return mod_t[:, l, n * 16 + kc, row:row + 1]

            def norm_mod(nt, l, nsh, nsc, row, out_bf, last=None):
                m = AR.top
                sqs = [AR.tile(F32, (nt,))[0] for _ in range(3)]
                rstd, _ = AR.tile(F32, (nt,))
                tmps = [AR.tile(F32, (nt,))[0] for _ in range(3)]
                ps = psum()
                for kc in range(KC):
                    sq = sqs[kc % 3]
                    act(sq, x_t[:, kc, 0:nt], AF.Square)
                    mm(ps[:, 0:nt], ones, sq, kc == 0, kc == KC - 1)
                act(rstd, ps[:, 0:nt], AF.Sqrt, bias=C("eps6")[:, 0:1], scale=1.0 / D)
                recip(rstd, rstd)
                for kc in range(KC):
                    tmp = tmps[kc % 3]
                    tt(tmp, x_t[:, kc, 0:nt], rstd, ALU.mult)
                    act(out_bf[:, kc, 0:nt], tmp, AF.Identity, bias=modv(l, nsh, kc, row), scale=modv(l, nsc, kc, row))
                    if last is not None:
                        act(last[:, kc, :], tmp[:, nt - 1:nt], AF.Identity, bias=modv(l, nsh, kc, row), scale=modv(l, nsc, kc, row))
                AR.top = m
                return rstd

            def retention(nt, row, is_sample):
                L = min(128, nt)
                nblk = nt // L
                m = AR.top
                h, _ = AR.tile(BF16, (KC, nt))
                norm_mod(nt, 0, 0, 1, row, h)
                y, _ = AR.tile(BF16, (32, nt))
                qf, _ = AR.tile(F32, (2, nt))
                kf, _ = AR.tile(F32, (2, nt))
                r1, _ = AR.tile(F32, (nt,))
                r2, _ = AR.tile(F32, (nt,))
                r3, _ = AR.tile(F32, (nt,))
                r4, _ = AR.tile(F32, (nt,))
                qT, _ = AR.tile(BF16, (2, nt))
                qcT, _ = AR.tile(F32, (2, nt))
                kT, _ = AR.tile(BF16, (2, nt))
                vT, _ = AR.tile(BF16, (4, nt))
                Vt, _ = AR.tile(BF16, (nblk, DV))
                Kt, _ = AR.tile(BF16, (nblk, DK))
                g, _ = AR.tile(F32, (4, nt))
                oh, _ = AR.tile(F32, (4, nt))
                osq, _ = AR.tile(F32, (nt,))
                mean, _ = AR.tile(F32, (nt,))
                rstd, _ = AR.tile(F32, (nt,))
                sT, _ = AR.tile(BF16, (L,))
                cos = ropet[:, 0, 0:nt]
                sin = ropet[:, 1, 0:nt]
                sfx = "s" if is_sample else ""
                maskT = C("rmask" + sfx)
                cross = C("rcross" + sfx)
                into = C("rinto" + sfx)
                identb = identb_t
                for hd in range(RH):
                    gam_L = (1.0 - 2.0 ** (-5.0 - hd)) ** L
                    for pi in range(6):
                        w = wload("ret_in", hd * 6 + pi)
                        for oc2 in range(2):
                            oc = pi * 2 + oc2
                            ps = psum()
                            for kc in range(KC):
                                mm(ps[:, 0:nt], w[:, kc, oc2 * 128:(oc2 + 1) * 128], h[:, kc, :], kc == 0, kc == KC - 1)
                            if oc < 2:
                                cp(qf[:, oc, :], ps[:, 0:nt], eng="act")
                            elif oc < 4:
                                act(kf[:, oc - 2, :], ps[:, 0:nt], AF.Copy, scale=1.0 / 16.0)
                            elif oc < 8:
                                cp(vT[:, oc - 4, :], ps[:, 0:nt], eng="act")
                            else:
                                act(g[:, oc - 8, :], ps[:, 0:nt], AF.Silu)
                    for (src, dst) in ((qf, qT), (kf, kT)):
                        tt(r1, src[:, 0, :], cos, ALU.mult)
                        tt(r2, src[:, 1, :], sin, ALU.mult)
                        tt(dst[:, 0, :], r1, r2, ALU.subtract)
                        tt(r3, src[:, 0, :], sin, ALU.mult, eng="pool")
                        tt(r4, src[:, 1, :], cos, ALU.mult, eng="pool")
                        tt(dst[:, 1, :], r3, r4, ALU.add, eng="pool")
                    for b in range(nblk):
                        cr = cross[:, hd * L:(hd + 1) * L]
                        for dc in range(2):
                            tt(qcT[:, dc, b * L:(b + 1) * L], qT[:, dc, b * L:(b + 1) * L], cr, ALU.mult)
                    for b in range(nblk):
                        ps = psum()
                        psb = V(ps.ap[:, 0:256].bitcast(BF16), ps.keys)
                        for vc in range(4):
                            tr(psb[0:L, vc * 128:(vc + 1) * 128], vT[:, vc, b * L:(b + 1) * L], identb)
                        cp(Vt[0:L, b, :], psb[0:L, :], eng="act")
                        ps = psum()
                        psb = V(ps.ap[:, 0:256].bitcast(BF16), ps.keys)
                        for dc in range(2):
                            tr(psb[0:L, dc * 128:(dc + 1) * 128], kT[:, dc, b * L:(b + 1) * L], identb)
                        ts(Kt[0:L, b, :], psb[0:L, 0:256], into[0:L, hd:hd + 1], None, ALU.mult)
                    for b in range(nblk):
                        cs = slice(b * L, (b + 1) * L)
                        ps = psum()
                        for dc in range(2):
                            mm(ps[0:L, 0:L], kT[:, dc, cs], qT[:, dc, cs], dc == 0, dc == 1)
                        tt(sT[0:L, :], ps[0:L, 0:L], maskT[0:L, hd * L:(hd + 1) * L], ALU.mult)
                        ps = psum()
                        for vc in range(4):
                            mm(ps[:, vc * L:(vc + 1) * L], Vt[0:L, b, vc * 128:(vc + 1) * 128], sT[0:L, :], True, True)
                        psb2 = psum()
                        for vc in range(4):
                            for dc in range(2):
                                mm(psb2[:, vc * L:(vc + 1) * L], sret_t[:, hd * 2 + dc, vc * 128:(vc + 1) * 128], qcT[:, dc, cs], dc == 0, dc == 1)
                        v3 = lambda p_: V(p_.ap[:, 0:4 * L].rearrange("p (a b) -> p a b", b=L), p_.keys)
                        cp(oh[:, :, cs], v3(ps), eng="act")
                        tt(oh[:, :, cs], oh[:, :, cs], v3(psb2), ALU.add)
                        for dc in range(2):
                            ps = psum()
                            mm(ps[:, 0:DV], Kt[0:L, b, dc * 128:(dc + 1) * 128], Vt[0:L, b, :], True, True)
                            stt(sret_t[:, hd * 2 + dc, :], sret_t[:, hd * 2 + dc, :], gam_L, ps[:, 0:DV], ALU.mult, ALU.add)
                    ps1 = psum()
                    for vc in range(4):
                        mm(ps1[:, 0:nt], ones, oh[:, vc, :], vc == 0, vc == 3)
                    ps2 = psum()
                    for vc in range(4):
                        act(osq, oh[:, vc, :], AF.Square)
                        mm(ps2[:, 0:nt], ones, osq, vc == 0, vc == 3)
                    act(mean, ps1[:, 0:nt], AF.Copy, scale=1.0 / DV)
                    tt(osq, mean, mean, ALU.mult)
                    stt(rstd, ps2[:, 0:nt], 1.0 / DV, osq, ALU.mult, ALU.subtract)
                    act(rstd, rstd, AF.Sqrt, bias=C("eps5")[:, 0:1], scale=1.0)
                    recip(rstd, rstd)
                    gg = P("ret_gn")
                    for vc in range(4):
                        tt(osq, oh[:, vc, :], mean, ALU.subtract)
                        tt(osq, osq, rstd, ALU.mult)
                        stt(y[:, hd * 4 + vc, :], osq, gg[:, hd * 4 + vc:hd * 4 + vc + 1], g[:, vc, :], ALU.mult, ALU.mult)
                for pc in range(16):
                    w = wload("ret_out", pc)
                    ps = psum()
                    for kc in range(32):
                        mm(ps[:, 0:nt], w[:, kc, :], y[:, kc, :], kc == 0, kc == 31)
                    stt(x_t[:, pc, 0:nt], ps[:, 0:nt], modv(0, 2, pc, row), x_t[:, pc, 0:nt], ALU.mult, ALU.add)
                AR.top = m

            def ffn(nt, l, row):
                m = AR.top
                h, _ = AR.tile(BF16, (KC, nt))
                norm_mod(nt, l, 3, 4, row, h)
                a, _ = AR.tile(BF16, (FC, nt))
                ues = [AR.tile(F32, (nt + 2,))[0] for _ in range(2)]
                cvs = [[AR.tile(F32, (nt,))[0] for _ in range(4)] for _ in range(2)]
                cw = P("conv_w")
                cb = P("conv_b")
                for fp in range(22):
                    wg = wload("gu%d" % l, 2 * fp)
                    wu = wload("gu%d" % l, 2 * fp + 1)
                    for oc2 in range(2):
                        fc = fp * 2 + oc2
                        psg = psum()
                        for kc in range(KC):
                            mm(psg[:, 0:nt], wg[:, kc, oc2 * 128:(oc2 + 1) * 128], h[:, kc, :], kc == 0, kc == KC - 1)
                        psu = psum()
                        for kc in range(KC):
                            mm(psu[:, 0:nt], wu[:, kc, oc2 * 128:(oc2 + 1) * 128], h[:, kc, :], kc == 0, kc == KC - 1)
                        ue = ues[fc % 2]
                        cp(ue[:, 0:2], conv_t[:, l, fc, :], eng="pool")
                        cp(ue[:, 2:nt + 2], psg[:, 0:nt], eng="act")
                        cp(conv_t[:, l, fc, :], ue[:, nt:nt + 2], eng="pool")
                        wj = lambda j: cw[:, (l * 3 + j) * FC + fc:(l * 3 + j) * FC + fc + 1]
                        cA, cB, cC, cD = cvs[fc % 2]
                        ts(cA, ue[:, 0:nt], wj(0), cb[:, l * FC + fc:l * FC + fc + 1], ALU.mult, ALU.add)
                        stt(cB, ue[:, 1:nt + 1], wj(1), cA, ALU.mult, ALU.add)
                        stt(cC, ue[:, 2:nt + 2], wj(2), cB, ALU.mult, ALU.add)
                        act(cD, cC, AF.Silu)
                        tt(a[:, fc, :], cD, psu[:, 0:nt], ALU.mult)
                for pc in range(16):
                    w = wload("dn%d" % l, pc)
                    ps = psum()
                    for kc in range(FC):
                        mm(ps[:, 0:nt], w[:, kc, :], a[:, kc, :], kc == 0, kc == FC - 1)
                    stt(x_t[:, pc, 0:nt], ps[:, 0:nt], modv(l, 5, pc, row), x_t[:, pc, 0:nt], ALU.mult, ALU.add)
                AR.top = m

            ropet, _ = AR.tile(F32, (2, NT))
            identb_t, _ = AR.tile(BF16, (128,))
            cp(identb_t, ident)
            base2 = AR.top


            negw0, _ = AR.tile(F32, (16,))
            ts(negw0, P("w0"), -1.0, None, ALU.mult)
            _mo = COFF["mstrict"][0]

            def rwkv(nt, row):
                Cn = min(128, nt)
                nch = nt // Cn
                nlev = int(round(math.log2(Cn)))
                m = AR.top
                h, _ = AR.tile(BF16, (KC, nt))
                xx, _ = AR.tile(BF16, (KC, nt))
                newsh, _ = AR.tile(F32, (KC, 1))
                norm_mod(nt, 1, 0, 1, row, h, last=newsh)
                tt(xx[:, :, 0:1], shift_t, h[:, :, 0:1], ALU.subtract)
                if nt > 1:
                    tt(xx[:, :, 1:nt], h[:, :, 0:nt - 1], h[:, :, 1:nt], ALU.subtract)
                cp(shift_t, newsh, eng="pool")
                y, _ = AR.tile(BF16, (KC, nt))
                lm, _ = AR.tile(BF16, (4, nt))
                l2, _ = AR.tile(BF16, (4, 128))
                F = lambda *sh: AR.tile(F32, sh)[0]
                r_t, k0_t, v_t, e2, asig, g_t, kk, k_t, b_t, bonus, tA, tB = (F(nt) for _ in range(12))
                e2T = F(128)
                cs_, gam, ginv, gprev = F(Cn), F(Cn), F(Cn), F(Cn)
                AR2, BK = F(2, Cn), F(2, Cn)
                AR2f = lambda hs_: V(AR2.ap[hs_].rearrange("p a b -> p (a b)"), AR2.keys)
                Bh, Kh = F(Cn), F(Cn)
                TM = F(4, 128)
                Gb = [F(2 * Cn) for _ in range(2)]
                Gk = [F(2 * Cn) for _ in range(2)]
                g3 = lambda t_: V(t_.ap[0:Cn].rearrange("p (a b) -> p a b", b=Cn), t_.keys)
                Lp = [[F(Cn) for _ in range(2)] for _ in range(2)]
                Pp = [[F(Cn) for _ in range(2)] for _ in range(2)]
                Xp = [[F(128) for _ in range(2)] for _ in range(2)]
                WT, UL, Wfm, UT, Osb, Osq, Yn = F(128), F(128), F(Cn), F(128), F(128), F(128), F(128)
                st1, st2, st3 = F(2), F(2), F(2)
                mask2 = V(cst_t.ap[0:Cn, _mo:_mo + 256].rearrange("p (a b) -> p a b", b=128)[:, :, 0:Cn], cst_t.keys)
                maskT = V(cst_t.ap[0:Cn, COFF["mstrictT"][0]:COFF["mstrictT"][0] + Cn], cst_t.keys)
                mincl = V(cst_t.ap[0:Cn, COFF["mincl"][0]:COFF["mincl"][0] + Cn], cst_t.keys)

                def proj(piece):
                    w = wload("rkv" if piece >= 0 else "lora1", piece if piece >= 0 else -piece - 1)
                    ps = psum()
                    for kc in range(32):
                        rhs = h[:, kc, :] if kc < 16 else xx[:, kc - 16, :]
                        mm(ps[:, 0:nt], w[:, kc, :], rhs, kc == 0, kc == 31)
                    return ps
                ps = proj(-1)
                act(lm[:, 0, :], ps[:, 0:nt], AF.Tanh)
                ps = proj(-2)
                cp(lm[:, 1, :], ps[:, 0:nt], eng="act")
                ps = proj(-3)
                act(lm[:, 2, :], ps[:, 0:nt], AF.Sigmoid)
                ps = proj(-4)
                act(lm[:, 3, :], ps[:, 0:nt], AF.Sigmoid)
                pcol = lambda n, p: V(prm_t.ap[:, POFF[n][0] + p:POFF[n][0] + p + 1], prm_t.keys)
                for p in range(16):
                    psl = slice(p * 128, (p + 1) * 128)
                    dma("sp", l2[:, 0, :], l2scr.ap()[:, p * 128:(p + 1) * 128], ikeys=[("wsl", l2scr.name, 0)])
                    dma("sp", l2[:, 1, :], l2scr.ap()[:, D + p * 128:D + (p + 1) * 128], ikeys=[("wsl", l2scr.name, 1)])
                    dma("sp", l2[:, 2:4, :], g2scr.ap().rearrange("p (a b) -> p a b", b=D)[:, :, p * 128:(p + 1) * 128],
                        ikeys=[("wsl", g2scr.name, 0), ("wsl", g2scr.name, 1)])
                    ps = proj(3 * p + 0)
                    cp(r_t, ps[:, 0:nt], eng="act")
                    ps = proj(3 * p + 1)
                    cp(k0_t, ps[:, 0:nt], eng="act")
                    ps = proj(3 * p + 2)
                    cp(v_t, ps[:, 0:nt], eng="act")
                    ps = psum()
                    mm(ps[:, 0:nt], l2[:, 0, :], lm[:, 0, :], True, True)
                    act(e2, ps[:, 0:nt], AF.Exp, bias=negw0[:, p:p + 1], scale=-1.0)
                    act(e2, e2, AF.Ln, bias=1.0)
                    act(e2, e2, AF.Exp, bias=C("mhalf")[:, 0:1], scale=-1.0)
                    ps = psum()
                    mm(ps[:, 0:nt], l2[:, 1, :], lm[:, 1, :], True, True)
                    act(asig, ps[:, 0:nt], AF.Sigmoid, bias=pcol("a0", p))
                    ps = psum()
                    mm(ps[:, 0:nt], l2[:, 2, :], lm[:, 2, :], True, False)
                    mm(ps[:, 0:nt], l2[:, 3, :], lm[:, 3, :], False, True)
                    cp(g_t, ps[:, 0:nt], eng="act")
                    ts(kk, k0_t, pcol("k_k", p), None, ALU.mult)
                    act(tA, kk, AF.Square)
                    ps = psum()
                    mm(ps[:, 0:nt], bones, tA, True, True)
                    act(tA, ps[:, 0:nt], AF.Sqrt)
                    ts(tA, tA, 1e-12, None, ALU.max)
                    recip(tA, tA)
                    tt(kk, kk, tA, ALU.mult)
                    ts(tA, asig, pcol("k_a", p), pcol("k_a", p), ALU.mult, ALU.subtract)
                    stt(k_t, tA, 1.0, k0_t, ALU.add, ALU.mult)
                    tt(b_t, kk, asig, ALU.mult)
                    stt(tA, r_t, pcol("r_k", p), k_t, ALU.mult, ALU.mult)
                    ps = psum()
                    mm(ps[:, 0:nt], bones, tA, True, True)
                    tt(bonus, ps[:, 0:nt], v_t, ALU.mult)
                    for c in range(nch if RW_STAGE >= 2 else 0):
                        cs = slice(c * Cn, (c + 1) * Cn)
                        ps = psum()
                        tr(ps[0:Cn, 0:128], e2[:, cs], ident)
                        cp(e2T[0:Cn, :], ps[0:Cn, 0:128])
                        ps = psum()
                        mm(ps[:, 0:Cn], e2T[0:Cn, :], mincl, True, True)
                        cp(cs_, ps[:, 0:Cn])
                        act(gam, cs_, AF.Exp, scale=-1.0)
                        act(ginv, cs_, AF.Exp)
                        tt(gprev, cs_, e2[:, cs], ALU.subtract)
                        act(gprev, gprev, AF.Exp, scale=-1.0)
                        stt(AR2[:, 0, :], kk[:, cs], -1.0, gprev, ALU.mult, ALU.mult)
                        tt(AR2[:, 1, :], r_t[:, cs], gam, ALU.mult)
                        tt(BK[:, 0, :], b_t[:, cs], ginv, ALU.mult)
                        tt(BK[:, 1, :], k_t[:, cs], ginv, ALU.mult)
                        gC = gam[:, Cn - 1:Cn]
                        ts(Bh, BK[:, 0, :], gC, None, ALU.mult)
                        ts(Kh, BK[:, 1, :], gC, None, ALU.mult)
                        ps = psum()
                        tr(ps[0:Cn, 0:128], v_t[:, cs], ident)
                        tr(ps[0:Cn, 128:256], Bh, ident)
                        tr(ps[0:Cn, 256:384], Kh, ident)
                        tr(ps[0:Cn, 384:512], AR2[:, 0, :], ident)
                        cp(TM[0:Cn, 0:2, :], V(ps.ap[0:Cn, 0:256].rearrange("p (a b) -> p a b", b=128), ps.keys), eng="act")
                        cp(TM[0:Cn, 2:4, :], V(ps.ap[0:Cn, 256:512].rearrange("p (a b) -> p a b", b=128), ps.keys))
                        Vtm, Bhtm, Khtm, Attm = TM[0:Cn, 0, :], TM[0:Cn, 1, :], TM[0:Cn, 2, :], TM[0:Cn, 3, :]
                        if RW_STAGE < 3:
                            continue
                        hst = []
                        for hh in range(2):
                            hs = slice(64 * hh, 64 * hh + 64)
                            ar_h = AR2f(hs)
                            ps = psum()
                            mm(ps[0:Cn, 0:2 * Cn], BK[hs, 0, :], ar_h, True, True)
                            tt(g3(Gb[hh]), V(ps.ap[0:Cn, 0:2 * Cn].rearrange("p (a b) -> p a b", b=Cn), ps.keys), mask2, ALU.mult)
                            ps = psum()
                            mm(ps[0:Cn, 0:2 * Cn], BK[hs, 1, :], ar_h, True, True)
                            tt(g3(Gk[hh]), V(ps.ap[0:Cn, 0:2 * Cn].rearrange("p (a b) -> p a b", b=Cn), ps.keys), mask2, ALU.mult)
                            ps = psum()
                            mm(ps[0:Cn, 0:Cn], AR2[hs, 0, :], BK[hs, 0, :], True, True)
                            tt(Lp[hh][0][0:Cn], ps[0:Cn, 0:Cn], maskT, ALU.mult)
                            hc = hs
                            ps = psum()
                            mm(ps[0:Cn, 0:64], Gk[hh][0:Cn, 0:Cn], Vtm[:, hc], True, True)
                            X = Xp[hh][0]
                            cp(X[0:Cn, 0:64], Attm[:, hc], eng="pool")
                            cp(X[0:Cn, 64:128], ps[0:Cn, 0:64], eng="act")
                            hst.append([Gb[hh][0:Cn, 0:Cn], Lp[hh][0][0:Cn], X])
                        for lv in range(nlev):
                            for hh in range(2):
                                hc = slice(64 * hh, 64 * hh + 64)
                                Pc, Lc, X = hst[hh]
                                ps = psum()
                                mm(ps[0:Cn, 0:128], Pc, X[0:Cn], True, True)
                                if lv == nlev - 1:
                                    tt(WT[0:Cn, hc], X[0:Cn, 0:64], ps[0:Cn, 0:64], ALU.add)
                                    tt(UL[0:Cn, hc], X[0:Cn, 64:128], ps[0:Cn, 64:128], ALU.add)
                                else:
                                    Xn = Xp[hh][(lv + 1) % 2]
                                    tt(Xn[0:Cn], X[0:Cn], ps[0:Cn, 0:128], ALU.add)
                                    ps1 = psum()
                                    mm(ps1[0:Cn, 0:Cn], Lc, Pc, True, True)
                                    ps2 = psum()
                                    mm(ps2[0:Cn, 0:Cn], Pc, Lc, True, True)
                                    Pn = Pp[hh][lv % 2][0:Cn]
                                    Ln = Lp[hh][(lv + 1) % 2][0:Cn]
                                    cp(Pn, ps1[0:Cn, 0:Cn], eng="act")
                                    cp(Ln, ps2[0:Cn, 0:Cn])
                                    hst[hh] = [Pn, Ln, Xn]
                        ps = psum()
                        tr(ps[:, 0:Cn], WT[0:Cn], ident[0:Cn, 0:Cn])
                        cp(Wfm, ps[:, 0:Cn], eng="act")
                        if RW_STAGE < 4:
                            continue
                        for hh in range(2):
                            hs = slice(64 * hh, 64 * hh + 64)
                            ps = psum()
                            mm(ps[0:Cn, 0:64], Wfm[hs, :], swkv_t[hs, p, hs], True, True)
                            tt(UT[0:Cn, hs], ps[0:Cn, 0:64], UL[0:Cn, hs], ALU.add)
                        for hh in range(2):
                            hs = slice(64 * hh, 64 * hh + 64)
                            psa = psum()
                            mm(psa[0:Cn, 0:64], AR2[hs, 1, :], swkv_t[hs, p, hs], True, True)
                            cp(Osb[0:Cn, hs], psa[0:Cn, 0:64], eng="act")
                        ps = psum()
                        for hh in range(2):
                            hs = slice(64 * hh, 64 * hh + 64)
                            mm(ps[0:Cn, hs], Gb[hh][0:Cn, Cn:2 * Cn], UT[0:Cn, hs], True, False)
                            mm(ps[0:Cn, hs], Gk[hh][0:Cn, Cn:2 * Cn], Vtm[:, hs], False, True)
                        tt(Osb[0:Cn], Osb[0:Cn], ps[0:Cn, 0:128], ALU.add)
                        ps = psum()
                        mm(ps[:, 0:128], Bhtm, UT[0:Cn], True, False)
                        mm(ps[:, 0:128], Khtm, Vtm, False, True)
                        stt(swkv_t[:, p, :], swkv_t[:, p, :], gC, ps[:, 0:128], ALU.mult, ALU.add)
                        if RW_STAGE < 5:
                            continue
                        O3 = V(Osb.ap[0:Cn].rearrange("p (a b) -> p a b", b=64), Osb.keys)
                        Q3 = V(Osq.ap[0:Cn].rearrange("p (a b) -> p a b", b=64), Osq.keys)
                        Y3 = V(Yn.ap[0:Cn].rearrange("p (a b) -> p a b", b=64), Yn.keys)
                        S.op("dve", lambda e, o=st1, i=O3: e.tensor_reduce(out=o.ap[0:Cn], in_=i.ap, axis=AX.X, op=ALU.add), reads=[Osb], writes=[st1])
                        act(Osq[0:Cn], Osb[0:Cn], AF.Square)
                        S.op("dve", lambda e, o=st2, i=Q3: e.tensor_reduce(out=o.ap[0:Cn], in_=i.ap, axis=AX.X, op=ALU.add), reads=[Osq], writes=[st2])
                        ts(st1[0:Cn], st1[0:Cn], 1.0 / 64, None, ALU.mult)
                        tt(st3[0:Cn], st1[0:Cn], st1[0:Cn], ALU.mult)
                        stt(st2[0:Cn], st2[0:Cn], 1.0 / 64, st3[0:Cn], ALU.mult, ALU.subtract)
                        act(st2[0:Cn], st2[0:Cn], AF.Sqrt, bias=C("epsw")[0:Cn, 0:1])
                        recip(st2[0:Cn], st2[0:Cn])
                        bc = lambda t_: V(t_.ap[0:Cn].unsqueeze(2).to_broadcast([Cn, 2, 64]), t_.keys)
                        tt(Y3, O3, bc(st1), ALU.subtract)
                        tt(Y3, Y3, bc(st2), ALU.mult)
                        ps = psum()
                        tr(ps[:, 0:Cn], Yn[0:Cn], ident[0:Cn, 0:Cn])
                        stt(tA[:, 0:Cn], ps[:, 0:Cn], pcol("rw_gn", p), bonus[:, cs], ALU.mult, ALU.add)
                        tt(y[:, p, cs], tA[:, 0:Cn], g_t[:, cs], ALU.mult)
                for pc in range(8):
                    w = wload("w_o", pc)
                    for oc2 in range(2):
                        oc = pc * 2 + oc2
                        ps = psum()
                        for kc in range(KC):
                            mm(ps[:, 0:nt], w[:, kc, oc2 * 128:(oc2 + 1) * 128], y[:, kc, :], kc == 0, kc == KC - 1)
                        stt(x_t[:, oc, 0:nt], ps[:, 0:nt], modv(1, 2, oc, row), x_t[:, oc, 0:nt], ALU.mult, ALU.add)
                AR.top = m

            def final_out(nt, dst):
                m = AR.top
                sq, _ = AR.tile(F32, (nt,))
                rstd, _ = AR.tile(F32, (nt,))
                o, _ = AR.tile(F32, (KC, nt))
                ps = psum()
                for kc in range(KC):
                    act(sq, x_t[:, kc, 0:nt], AF.Square)
                    mm(ps[:, 0:nt], ones, sq, kc == 0, kc == KC - 1)
                act(rstd, ps[:, 0:nt], AF.Sqrt, bias=C("eps6")[:, 0:1], scale=1.0 / D)
                recip(rstd, rstd)
                fin = P("fin")
                for kc in range(KC):
                    stt(o[:, kc, :], x_t[:, kc, 0:nt], fin[:, kc:kc + 1], rstd, ALU.mult, ALU.mult)
                S.dma("pool", lambda e: e.dma_start(out=dst(), in_=o.ap), reads=S._keys([o]))
                AR.top = m

            dmy = nc.dram_tensor("dmy", [2, 64], F32)

            def pad_dmas(st0):
                for q in ("sp", "pool"):
                    m = S.dcnt[q] - st0["dcnt"][q]
                    for _ in range((-m) % NDMA_SEM):
                        S.dma(q, lambda e: e.dma_start(out=dmy.ap()[0:1, 0:16], in_=dmy.ap()[1:2, 0:16]))

            def tidx(ti):
                return lambda: (S.loop_var if S.loop_var is not None else ti)

            def init_states(is_sample, si):
                if is_sample:
                    dma("sp", sret_t, st_ret[si].rearrange("p (a b) -> p a b", b=DV))
                    dma("sp", swkv_t, st_wkv[si].rearrange("p (a b) -> p a b", b=128))
                    dma("sp", shift_t, st_shift[si].rearrange("p (a b) -> p a b", b=1))
                    dma("sp", conv_t, st_conv[si].rearrange("p (a b c) -> p a b c", b=FC, c=2))
                else:
                    memset(sret_t, 0.0)
                    memset(swkv_t, 0.0)
                    memset(shift_t, 0.0)
                    memset(conv_t, 0.0)

            def run_tile(is_sample, si, ti):
                nt = TS if is_sample else NT
                row = 1 + si if is_sample else 0
                psi[0] = 0
                wbi[0] = 0
                if is_sample:
                    dma("sp", x_t[:, :, 0:nt], xs_in[si].rearrange("p (a b) -> p a b", b=nt))
                    dma("sp", ropet[:, :, 0:nt], rope_s.ap().rearrange("p (a b) -> p a b", b=nt))
                else:
                    tv = tidx(ti)
                    S.dma("sp", lambda e: e.dma_start(out=x_t.ap, in_=x_in.ap()[tv()].rearrange("p (a b) -> p a b", b=NT)), writes=S._keys([x_t]))
                    S.dma("sp", lambda e: e.dma_start(out=ropet.ap, in_=rope.ap()[tv()].rearrange("p (a b) -> p a b", b=NT)), writes=S._keys([ropet]))
                retention(nt, row, is_sample)
                ffn(nt, 0, row)
                if STOP_AFTER >= 3:
                    rwkv(nt, row)
                if STOP_AFTER >= 4:
                    ffn(nt, 1, row)
                if is_sample:
                    final_out(nt, lambda: ys_out[si].rearrange("p (a b) -> p a b", b=nt))
                else:
                    tv = tidx(ti)
                    final_out(nt, lambda: y_out.ap()[tv()].rearrange("p (a b) -> p a b", b=NT))

            def write_states(oi):
                dma("pool", o_ret[oi].rearrange("p (a b) -> p a b", b=DV), sret_t)
                dma("pool", o_wkv[oi].rearrange("p (a b) -> p a b", b=128), swkv_t)
                dma("pool", o_shift[oi].rearrange("p (a b) -> p a b", b=1), shift_t)
                dma("pool", o_conv[oi].rearrange("p (a b c) -> p a b c", b=FC, c=2), conv_t)

            for si in range(NSP):
                init_states(True, si)
                run_tile(True, si, 0)
                write_states(1 + si)
            init_states(False, 0)
            n_run = NTILE_RUN
            if n_run <= 4:
                for ti in range(n_run):
                    st0 = S.state()
                    run_tile(False, 0, ti)
                    pad_dmas(st0)
            else:
                k = 0
                prev = None
                while True:
                    st0 = S.state()
                    run_tile(False, 0, k)
                    pad_dmas(st0)
                    st1 = S.state()
                    delta = S.deltas(st0, st1)
                    cur = S.norm_tile(st0, st1, k, delta)
                    if prev is not None and cur == prev[0] and delta == prev[1]:
                        S.restore(st0)
                        S.set_loop(prev[2], st0, k - 1, n_run, delta)
                        print("steady state at tile", k - 1)
                        break
                    prev = (cur, delta, st0)
                    k += 1
                    assert k < 8, "no steady state"
            S.emit(block)
        with nc.Block() as block2:
            write_states(0)
            S.emit(block2)
        print("ops recorded:", S.nops, "arena top", AR.top, "peak", AR.peak)
    return nc


POFF = {}
COFF = {}
_o = 0
for _n, _w in (("ada_b", 192), ("ret_gn", 32), ("mu", 96), ("w0", 16), ("a0", 16), ("k_k", 16), ("k_a", 16),
               ("r_k", 16), ("rw_gn", 16), ("conv_w", 2 * 3 * FC), ("conv_b", 2 * FC), ("fin", 16)):
    POFF[_n] = (_o, _w)
    _o += _w
NPRM = _o
_o = 0
for _n, _w in (("ident", 128), ("ones", 128), ("bones", 128), ("eps6", 1), ("eps5", 1), ("epsw", 1), ("mhalf", 1),
               ("rmask", 8 * 128), ("rcross", 8 * 128), ("rinto", 8),
               ("rmasks", 8 * 16), ("rcrosss", 8 * 16), ("rintos", 8),
               ("mstrict", 128), ("mincl", 128), ("mstrictT", 128)):
    COFF[_n] = (_o, _w)
    _o += _w
NCST = _o
NTILE_RUN = NTILE


def _consts():
    c = np.zeros((128, NCST), np.float32)

    def put(n, a):
        o, w = COFF[n]
        c[:a.shape[0], o:o + w] = a
    put("ident", np.eye(128, dtype=np.float32))
    put("ones", np.ones((128, 128), np.float32))
    bo = np.zeros((128, 128), np.float32)
    bo[:64, :64] = 1
    bo[64:, 64:] = 1
    put("bones", bo)
    put("eps6", np.full((128, 1), 1e-6, np.float32))
    put("eps5", np.full((128, 1), 1e-5, np.float32))
    put("epsw", np.full((128, 1), 64e-5, np.float32))
    put("mhalf", np.full((128, 1), -0.5, np.float32))
    gam = 1.0 - 2.0 ** (-5.0 - np.arange(8, dtype=np.float64))
    for sfx, L in (("", 128), ("s", 16)):
        idx = np.arange(L, dtype=np.float64)
        diff = idx[None, :] - idx[:, None]
        mk = np.where(diff >= 0, gam[:, None, None] ** np.maximum(diff, 0)[None], 0.0)
        put("rmask" + sfx, mk.transpose(1, 0, 2).reshape(L, 8 * L).astype(np.float32))
        cr = gam[:, None] ** (idx[None, :] + 1.0)
        put("rcross" + sfx, np.broadcast_to(cr.reshape(1, 8 * L), (128, 8 * L)).astype(np.float32))
        it = gam[:, None] ** (L - 1.0 - idx[None, :])
        put("rinto" + sfx, it.T.astype(np.float32))
    s = np.arange(128)
    put("mstrict", (s[:, None] < s[None, :]).astype(np.float32))
    put("mincl", (s[:, None] <= s[None, :]).astype(np.float32))
    put("mstrictT", (s[:, None] > s[None, :]).astype(np.float32))
    return c


def _rope(pos):
    half = 128
    inv = 10000.0 ** (-np.arange(half, dtype=np.float32) / half)
    ang = pos.astype(np.float32)[None, :] * inv[:, None]
    return np.cos(ang).astype(np.float32), np.sin(ang).astype(np.float32)


_NC_CACHE = {}


def kernel(**inp):
    f = lambda k: np.asarray(inp[k], np.float32)
    if "nc" not in _NC_CACHE:
        _NC_CACHE["nc"] = build_program()
    nc = _NC_CACHE["nc"]
    sh = {}
    aw = f("ada_w")
    sh["ada_w"] = np.concatenate([_pieces(aw[l], 128) for l in range(2)], 0)
    wi = f("ret_w_in")[0]
    cols = []
    for h in range(RH):
        cols += [wi[:, h * DK:(h + 1) * DK], wi[:, 2048 + h * DK:2048 + (h + 1) * DK],
                 wi[:, 4096 + h * DV:4096 + (h + 1) * DV], wi[:, 8192 + h * DV:8192 + (h + 1) * DV]]
    sh["w_ret_in"] = _pieces(np.concatenate(cols, 1), 256)
    sh["w_ret_out"] = _pieces(f("ret_w_out")[0], 128)
    for l in range(2):
        g_ = _pieces(f("ffn_w_gate")[l], 256)
        u_ = _pieces(f("ffn_w_up")[l], 256)
        gu = np.empty((44,) + g_.shape[1:], np.float32)
        gu[0::2] = g_
        gu[1::2] = u_
        sh["w_gu%d" % l] = gu
        sh["w_dn%d" % l] = _pieces(f("ffn_w_down")[l], 128)
    st2 = lambda w: np.concatenate([w, w], 0)
    r_, k_, v_ = (_pieces(st2(f(n)[0]), 128) for n in ("rwkv_w_r", "rwkv_w_k", "rwkv_w_v"))
    rkv = np.empty((48,) + r_.shape[1:], np.float32)
    rkv[0::3], rkv[1::3], rkv[2::3] = r_, k_, v_
    sh["w_rkv"] = rkv
    pad = lambda w: np.concatenate([w, np.zeros((w.shape[0], 128 - w.shape[1]), np.float32)], 1)
    sh["w_lora1"] = np.concatenate([_pieces(st2(pad(f("rwkv_w1")[0])), 128), _pieces(st2(pad(f("rwkv_a1")[0])), 128),
                                    _pieces(st2(f("rwkv_g1")[0]), 128)], 0)
    sh["w_w_o"] = _pieces(f("rwkv_w_o")[0], 256)
    l2 = np.zeros((128, 3 * D), np.float32)
    l2[:96, 0:D] = f("rwkv_w2")[0]
    l2[:96, D:2 * D] = f("rwkv_a2")[0]
    sh["l2w"] = l2
    g2 = f("rwkv_g2")[0]
    sh["g2w"] = np.concatenate([g2[0:128], g2[128:256]], 1)
    prm = np.zeros((128, NPRM), np.float32)

    def putp(n, a):
        o, w = POFF[n]
        prm[:, o:o + w] = a
    ab = f("ada_b")
    putp("ada_b", np.concatenate([_fm(ab[0]), _fm(ab[1])], 1))
    putp("ret_gn", _fm(f("ret_gn_gain")[0]))
    putp("mu", np.concatenate([_fm(f("rwkv_mu")[0][i]) for i in range(6)], 1))
    for n, k in (("w0", "rwkv_w0"), ("a0", "rwkv_a0"), ("k_k", "rwkv_k_k"), ("k_a", "rwkv_k_a"), ("rw_gn", "rwkv_gn_gain")):
        putp(n, _fm(f(k)[0]))
    putp("r_k", _fm(f("rwkv_r_k")[0].reshape(-1)))
    cw = f("ffn_conv_w")
    putp("conv_w", np.concatenate([_fm(cw[l, j]) for l in range(2) for j in range(3)], 1))
    cb = f("ffn_conv_b")
    putp("conv_b", np.concatenate([_fm(cb[l]) for l in range(2)], 1))
    putp("fin", _fm(f("final_gain")))
    sh["prm"] = prm
    sh["cst"] = _consts()
    cosp, sinp = _rope(np.arange(SEQ))
    rp = np.stack([cosp, sinp], 1)
    sh["rope"] = np.ascontiguousarray(rp.reshape(128, 2, NTILE, NT).transpose(2, 0, 1, 3)).reshape(NTILE, 128, 2 * NT)
    coss, sins = _rope(PAST + np.arange(TS))
    sh["rope_s"] = np.stack([coss, sins], 1).reshape(128, 2 * TS)
    xp, xs = f("x_prompt"), f("x_sample")
    cpv, csv = f("c_prompt"), f("c_sample")
    xt_cache = {}
    in_maps = []
    for c in range(NCORES):
        sq = c % 2
        sidx = [c * NSP + i for i in range(NSP)]
        if sq not in xt_cache:
            xT = xp[sq].T
            xt_cache[sq] = np.ascontiguousarray(xT.reshape(KC, 128, NTILE, NT).transpose(2, 1, 0, 3)).reshape(NTILE, 128, KC * NT)
        m = dict(sh)
        m["x_in"] = xt_cache[sq]
        m["xs_in"] = np.stack([np.ascontiguousarray(xs[j].T.reshape(KC, 128, TS).transpose(1, 0, 2)).reshape(128, KC * TS) for j in sidx], 0)
        m["cvec"] = np.stack([_fm(cpv[sq])] + [_fm(csv[j]) for j in sidx], 2).reshape(128, KC * NROW)
        l_ret, l_wkv, l_sh, l_cv = [], [], [], []
        for j in sidx:
            sr = f("state_ret")[0, j]
            l_ret.append(np.ascontiguousarray(sr.reshape(RH, 2, 128, DV).transpose(2, 0, 1, 3)).reshape(128, RH * 2 * DV))
            sw = f("state_rwkv_wkv")[0, j]
            t = np.zeros((16, 2, 64, 2, 64), np.float32)
            swT = sw.transpose(0, 2, 1).reshape(16, 2, 64, 64)
            for hh in range(2):
                t[:, hh, :, hh, :] = swT[:, hh]
            l_wkv.append(np.ascontiguousarray(t.transpose(1, 2, 0, 3, 4)).reshape(128, 16 * 128))
            l_sh.append(_fm(f("state_rwkv_shift")[0, j]))
            sc = f("state_ffn_conv")[:, j]
            l_cv.append(np.ascontiguousarray(sc.reshape(2, 2, FC, 128).transpose(3, 0, 2, 1)).reshape(128, 2 * FC * 2))
        m["st_ret"], m["st_wkv"], m["st_shift"], m["st_conv"] = (np.stack(l, 0) for l in (l_ret, l_wkv, l_sh, l_cv))
        in_maps.append(m)
    res = run_bass_kernel_spmd(nc, in_maps, core_ids=list(range(NCORES)))
    R = res.results
    def unx(a, nt, ntile):
        return a.reshape(ntile, 128, KC, nt).transpose(0, 3, 2, 1).reshape(ntile * nt, D)
    y_prompt = np.stack([unx(R[b]["y_out"], NT, NTILE) for b in range(2)], 0)
    y_sample = np.stack([unx(R[j // NSP]["ys_out"][j % NSP], TS, 1) for j in range(8)], 0)

    def un_ret(a):
        return a.reshape(128, RH, 2, DV).transpose(1, 2, 0, 3).reshape(RH, DK, DV)

    def un_wkv(a):
        t = a.reshape(2, 64, 16, 2, 64)
        o = np.stack([t[hh, :, :, hh, :] for hh in range(2)], 0)
        return o.transpose(2, 0, 3, 1).reshape(32, 64, 64)

    def un_conv(a):
        return a.reshape(128, 2, FC, 2).transpose(1, 3, 2, 0).reshape(2, 2, FF)

    def un_vec(a):
        return a.T.reshape(-1)
    outs_p = [np.stack([fn(R[b][k][0]) for b in range(2)], 0) for k, fn in
              (("o_ret", un_ret), ("o_wkv", un_wkv), ("o_shift", un_vec))]
    outs_s = [np.stack([fn(R[j // NSP][k][1 + j % NSP]) for j in range(8)], 0) for k, fn in
              (("o_ret", un_ret), ("o_wkv", un_wkv), ("o_shift", un_vec))]
    pc = np.stack([un_conv(R[b]["o_conv"][0]) for b in range(2)], 1)
    sc_ = np.stack([un_conv(R[j // NSP]["o_conv"][1 + j % NSP]) for j in range(8)], 1)
    return (y_prompt.astype(np.float32), y_sample.astype(np.float32),
            outs_p[0][None].astype(np.float32), outs_p[1][None].astype(np.float32), outs_p[2][None].astype(np.float32), pc.astype(np.float32),
            outs_s[0][None].astype(np.float32), outs_s[1][None].astype(np.float32), outs_s[2][None].astype(np.float32), sc_.astype(np.float32))
```

```python
import contextlib
import math
import numpy as np
import concourse.bass as bass
import concourse.mybir as mybir
from concourse.bass_utils import run_bass_kernel_spmd

F32 = mybir.dt.float32
BF16 = mybir.dt.bfloat16
ALU = mybir.AluOpType
AF = mybir.ActivationFunctionType
AX = mybir.AxisListType

D = 2048
KC = 16
SEQ = 16384
NT = 256
NTILE = SEQ // NT
TS = 16
PAST = 4096
RH, DK, DV = 8, 256, 512
FF = 5632
FC = 44
HN = 64
GRAN = 512
SAME_ENGINE_SYNC = True
NDMA_SEM = 12
STOP_AFTER = 99
import os
RW_STAGE = int(os.environ.get('RW_STAGE', '9'))
NCORES = 2
NSP = 8 // NCORES
NROW = 1 + NSP


class V:
    def __init__(self, ap, keys):
        self.ap = ap
        self.keys = keys

    def __getitem__(self, idx):
        return V(self.ap[idx], self.keys)


class Sched:
    ENGS = ("pe", "act", "dve", "pool", "sp")

    def __init__(self, nc, stack):
        self.nc = nc
        self.stack = stack
        self.sem = {e: stack.enter_context(nc.semaphore("s_" + e)) for e in ("pe", "act", "dve", "pool")}
        self.dsem = {q: [stack.enter_context(nc.semaphore("d_%s%d" % (q, i))) for i in range(NDMA_SEM)]
                     for q in ("sp", "pool", "act")}
        self.dcnt = {q: 0 for q in ("sp", "pool", "act")}
        self.cnt = {e: 0 for e in ("pe", "act", "dve", "pool")}
        self.waited = {e: {} for e in self.ENGS}
        self.ops = {e: [] for e in self.ENGS}
        self.lastw = {}
        self.readers = {}
        self.nops = 0

    @staticmethod
    def _keys(vs):
        out = []
        for v in vs:
            if isinstance(v, V):
                out.extend(v.keys)
            else:
                out.append(v)
        return out

    def _deps(self, rk, wk):
        deps = []
        lw, rd = self.lastw, self.readers
        for k in rk:
            w = lw.get(k)
            if w is not None:
                deps.append(w)
        for k in wk:
            w = lw.get(k)
            if w is not None:
                deps.append(w)
            r = rd.get(k)
            if r:
                deps.extend(r.values())
        return deps

    def _mark(self, eng, tok, rk, wk):
        rd = self.readers
        for k in rk:
            d = rd.get(k)
            if d is None:
                rd[k] = {eng: tok}
            else:
                d[eng] = tok
        for k in wk:
            self.lastw[k] = tok
            rd[k] = None

    def _waits(self, eng, deps, is_dma):
        waits = []
        wd = self.waited[eng]
        for (sem, val, seng) in deps:
            if seng == eng and not is_dma:
                if eng == "pe" or not SAME_ENGINE_SYNC:
                    continue
            if wd.get(id(sem), 0) >= val:
                continue
            wd[id(sem)] = val
            waits.append((sem, val))
        return waits

    def op(self, eng, fn, reads=(), writes=()):
        rk, wk = self._keys(reads), self._keys(writes)
        waits = self._waits(eng, self._deps(rk, wk), False)
        self.cnt[eng] += 1
        tok = (self.sem[eng], self.cnt[eng], eng)
        self.ops[eng].append((waits, fn, self.sem[eng], 1))
        self._mark(eng, tok, rk, wk)
        self.nops += 1

    def dma(self, q, fn, reads=(), writes=()):
        rk, wk = self._keys(reads), self._keys(writes)
        deps = self._deps(rk, wk)
        m = self.dcnt[q]
        self.dcnt[q] += 1
        slot, rnd = m % NDMA_SEM, m // NDMA_SEM
        sem = self.dsem[q][slot]
        if rnd > 0:
            deps.append((sem, 16 * rnd, "dma"))
        deps = [(s, v, "x") if e == q else (s, v, e) for (s, v, e) in deps]
        waits = self._waits(q, deps, True)
        tok = (sem, 16 * (rnd + 1), "dma_" + q)
        self.ops[q].append((waits, fn, sem, 16))
        self._mark("dma_" + q + str(slot), tok, rk, wk)
        self.nops += 1

    def state(self):
        import copy
        return dict(cnt=dict(self.cnt), dcnt=dict(self.dcnt), waited={e: dict(d) for e, d in self.waited.items()},
                    lastw=dict(self.lastw), readers={k: (dict(v) if v else v) for k, v in self.readers.items()},
                    lens={e: len(v) for e, v in self.ops.items()}, nops=self.nops)

    def restore(self, st):
        self.cnt, self.dcnt = dict(st["cnt"]), dict(st["dcnt"])
        self.waited = {e: dict(d) for e, d in st["waited"].items()}
        self.lastw = dict(st["lastw"])
        self.readers = {k: (dict(v) if v else v) for k, v in st["readers"].items()}
        for e in self.ENGS:
            del self.ops[e][st["lens"][e]:]
        self.nops = st["nops"]

    def deltas(self, st0, st1):
        d = {}
        for e in ("pe", "act", "dve", "pool"):
            d[id(self.sem[e])] = st1["cnt"][e] - st0["cnt"][e]
        for q in ("sp", "pool", "act"):
            m = st1["dcnt"][q] - st0["dcnt"][q]
            assert m % NDMA_SEM == 0, (q, m)
            for sm in self.dsem[q]:
                d[id(sm)] = 16 * (m // NDMA_SEM)
        return d

    def norm_tile(self, st0, st1, k, delta):
        out = {}
        for e in self.ENGS:
            out[e] = [tuple((id(s_), v - k * delta[id(s_)]) for (s_, v) in w[0])
                      for w in self.ops[e][st0["lens"][e]:st1["lens"][e]]]
        return out

    def set_loop(self, st0, st1, k, n_end, delta):
        self.loop = dict(lo=st0["lens"], hi=st1["lens"], k=k, n=n_end, delta=delta)
        extra_iters = n_end - 1 - k
        for e in ("pe", "act", "dve", "pool"):
            self.cnt[e] += extra_iters * (st1["cnt"][e] - st0["cnt"][e])
        for q in ("sp", "pool", "act"):
            self.dcnt[q] += extra_iters * (st1["dcnt"][q] - st0["dcnt"][q])

    loop = None
    loop_var = None

    def emit(self, block):
        ops = self.ops
        self.ops = {e: [] for e in self.ENGS}
        loop = self.loop
        self.loop = None
        sched = self

        def body(ename, lst, extra):
            def run(eng, sub, base, scratch):
                for (waits, fn, sem, inc) in sub:
                    for (s, v) in waits:
                        dl = loop["delta"][id(s)] if base is not None else 0
                        if dl:
                            eng.reg_add(scratch, base[dl], v)
                            eng.wait_ge(s, scratch)
                        else:
                            eng.wait_ge(s, v)
                    fn(eng).then_inc(sem, inc)

            def _f(eng):
                if loop is None:
                    run(eng, lst, None, None)
                else:
                    lo, hi = loop["lo"][ename], loop["hi"][ename]
                    run(eng, lst[:lo], None, None)
                    if hi > lo:
                        with contextlib.ExitStack() as rs:
                            dls = sorted({loop["delta"][id(s_)] for (w, _, _, _) in lst[lo:hi] for (s_, _) in w} - {0})
                            base = {d_: rs.enter_context(eng.register("lb_%s_%d" % (ename, j))) for j, d_ in enumerate(dls)}
                            scratch = rs.enter_context(eng.register("ls_%s" % ename))
                            for d_ in dls:
                                eng.reg_mov(base[d_], 0)
                            with eng.Fori(loop["k"], loop["n"]) as i:
                                sched.loop_var = i
                                run(eng, lst[lo:hi], base, scratch)
                                sched.loop_var = None
                                for d_ in dls:
                                    eng.reg_add(base[d_], base[d_], d_)
                    run(eng, lst[hi:], None, None)
                for (s, v) in extra:
                    eng.wait_ge(s, v)
            return _f

        extra = {e: [] for e in self.ENGS}
        for q in ("sp", "pool", "act"):
            m = self.dcnt[q]
            for slot in range(min(m, NDMA_SEM)):
                n = (m - slot + NDMA_SEM - 1) // NDMA_SEM
                extra[q].append((self.dsem[q][slot], 16 * n))
        block.tensor(body("pe", ops["pe"], extra["pe"]))
        block.scalar(body("act", ops["act"], extra["act"]))
        block.vector(body("dve", ops["dve"], extra["dve"]))
        block.gpsimd(body("pool", ops["pool"], extra["pool"]))
        block.sync(body("sp", ops["sp"], extra["sp"]))
        self.lastw = {}
        self.readers = {}


class Arena:
    def __init__(self, nc, stack, nbytes):
        self.nbytes = nbytes
        self.t = stack.enter_context(nc.sbuf_tensor("arena", [128, nbytes // 4], F32))
        self.top = 0

    def alloc(self, nbytes):
        lo = self.top
        self.top = lo + ((nbytes + GRAN - 1) // GRAN) * GRAN
        assert self.top <= self.nbytes, ("arena overflow", self.top)
        self.peak = max(getattr(self, 'peak', 0), self.top)
        return lo

    def view(self, lo, dt, shape, np_=128):
        esz = 2 if dt == BF16 else 4
        n = int(np.prod(shape))
        nb = n * esz
        ap = self.t[0:np_, lo // 4:(lo + nb + 3) // 4]
        if dt == BF16:
            ap = ap.bitcast(BF16)
        if len(shape) == 2:
            ap = ap.rearrange("p (a b) -> p a b", b=shape[1])
        elif len(shape) == 3:
            ap = ap.rearrange("p (a b c) -> p a b c", b=shape[1], c=shape[2])
        keys = [("A", g) for g in range(lo // GRAN, (lo + nb - 1) // GRAN + 1)]
        return V(ap, keys)

    def tile(self, dt, shape, np_=128):
        esz = 2 if dt == BF16 else 4
        lo = self.alloc(int(np.prod(shape)) * esz)
        return self.view(lo, dt, shape, np_), lo


WSPEC = {
    "ret_in": (16, 256, 48),
    "ret_out": (32, 128, 16),
    "gu0": (16, 256, 44), "gu1": (16, 256, 44),
    "dn0": (22, 128, 32), "dn1": (22, 128, 32),
    "rkv": (32, 128, 48),
    "lora1": (32, 128, 4),
    "w_o": (16, 256, 8),
}


def _pieces(w, fw):
    K, F = w.shape
    return np.ascontiguousarray(w.reshape(K // 128, 128, F // fw, fw).transpose(2, 1, 0, 3)).reshape(F // fw, 128, (K // 128) * fw)


def _fm(v):
    return np.ascontiguousarray(v.reshape(-1, 128).T)


def build_program():
    nc = bass.Bass("TRN2", target_bir_lowering=False)
    dr = lambda n, sh, dt=F32, kind="ExternalInput": nc.dram_tensor(n, sh, dt, kind=kind)
    x_in = dr("x_in", [NTILE, 128, KC * NT])
    xs_in = dr("xs_in", [NSP, 128, KC * TS])
    cvec = dr("cvec", [128, KC * NROW])
    ada_w = dr("ada_w", [2 * 96, 128, KC * 128])
    win = {k: dr("w_" + k, [n, 128, kc * fw]) for k, (kc, fw, n) in WSPEC.items()}
    l2w = dr("l2w", [128, 3 * D])
    g2w = dr("g2w", [128, 2 * D])
    prm = dr("prm", [128, NPRM])
    cst = dr("cst", [128, NCST])
    rope = dr("rope", [NTILE, 128, 2 * NT])
    rope_s = dr("rope_s", [128, 2 * TS])
    st_ret = dr("st_ret", [NSP, 128, RH * 2 * DV])
    st_wkv = dr("st_wkv", [NSP, 128, 16 * 128])
    st_shift = dr("st_shift", [NSP, 128, KC])
    st_conv = dr("st_conv", [NSP, 128, 2 * FC * 2])
    y_out = dr("y_out", [NTILE, 128, KC * NT], kind="ExternalOutput")
    ys_out = dr("ys_out", [NSP, 128, KC * TS], kind="ExternalOutput")
    o_ret = dr("o_ret", [NROW, 128, RH * 2 * DV], kind="ExternalOutput")
    o_wkv = dr("o_wkv", [NROW, 128, 16 * 128], kind="ExternalOutput")
    o_shift = dr("o_shift", [NROW, 128, KC], kind="ExternalOutput")
    o_conv = dr("o_conv", [NROW, 128, 2 * FC * 2], kind="ExternalOutput")
    wscr = {k: nc.dram_tensor("ws_" + k, [n, 128, kc * fw], BF16) for k, (kc, fw, n) in WSPEC.items()}
    l2scr = nc.dram_tensor("ws_l2", [128, 3 * D], BF16)
    g2scr = nc.dram_tensor("ws_g2", [128, 2 * D], BF16)

    with contextlib.ExitStack() as st:
        S = Sched(nc, st)
        AR = Arena(nc, st, 188 * 1024)
        PS = [V(st.enter_context(nc.psum_tensor("ps%d" % i, [128, 512], F32))[:], ["ps%d" % i]) for i in range(8)]
        psi = [0]

        def psum():
            psi[0] = (psi[0] + 1) % 8
            return PS[psi[0]]

        def mm(out, lhsT, rhs, start, stop):
            S.op("pe", lambda e: e.matmul(out.ap, lhsT=lhsT.ap, rhs=rhs.ap, start=start, stop=stop),
                 reads=[lhsT, rhs], writes=[out])

        def tr(out, in_, ident):
            S.op("pe", lambda e: e.transpose(out.ap, in_.ap, ident.ap), reads=[in_, ident], writes=[out])

        def act(out, in_, func, bias=0.0, scale=1.0, extra=()):
            rd = [in_] + [b for b in (bias, scale) if isinstance(b, V)] + list(extra)
            b_ = bias.ap if isinstance(bias, V) else bias
            s_ = scale.ap if isinstance(scale, V) else scale
            S.op("act", lambda e: e.activation(out=out.ap, in_=in_.ap, func=func, bias=b_, scale=s_), reads=rd, writes=[out])

        def tt(out, a, b, op, eng="dve"):
            S.op(eng, lambda e: e.tensor_tensor(out=out.ap, in0=a.ap, in1=b.ap, op=op), reads=[a, b], writes=[out])

        def ts(out, a, s1, s2, op0, op1=None, eng="dve"):
            rd = [a] + [s for s in (s1, s2) if isinstance(s, V)]
            a1 = s1.ap if isinstance(s1, V) else s1
            a2 = s2.ap if isinstance(s2, V) else s2
            if op1 is None:
                S.op(eng, lambda e: e.tensor_scalar(out=out.ap, in0=a.ap, scalar1=a1, scalar2=None, op0=op0), reads=rd, writes=[out])
            else:
                S.op(eng, lambda e: e.tensor_scalar(out=out.ap, in0=a.ap, scalar1=a1, scalar2=a2, op0=op0, op1=op1), reads=rd, writes=[out])

        def stt(out, a, s, b, op0, op1, eng="dve"):
            rd = [a, b] + ([s] if isinstance(s, V) else [])
            s_ = s.ap if isinstance(s, V) else s
            S.op(eng, lambda e: e.scalar_tensor_tensor(out=out.ap, in0=a.ap, scalar=s_, in1=b.ap, op0=op0, op1=op1), reads=rd, writes=[out])

        def cp(out, in_, eng="dve"):
            if eng == "act":
                S.op("act", lambda e: e.copy(out=out.ap, in_=in_.ap), reads=[in_], writes=[out])
            else:
                S.op(eng, lambda e: e.tensor_copy(out=out.ap, in_=in_.ap), reads=[in_], writes=[out])

        def memset(out, val, eng="pool"):
            S.op(eng, lambda e: e.memset(out.ap, val), writes=[out])

        def recip(out, in_):
            S.op("dve", lambda e: e.reciprocal(out=out.ap, in_=in_.ap), reads=[in_], writes=[out])

        def dma(q, out, in_, okeys=None, ikeys=None):
            oa = out.ap if isinstance(out, V) else out
            ia = in_.ap if isinstance(in_, V) else in_
            rd = [in_] if isinstance(in_, V) else list(ikeys or [])
            wr = [out] if isinstance(out, V) else list(okeys or [])
            S.dma(q, lambda e: e.dma_start(out=oa, in_=ia), reads=rd, writes=wr)

        prm_t, _ = AR.tile(F32, (NPRM,))
        cst_t, _ = AR.tile(F32, (NCST,))
        mod_t, _ = AR.tile(F32, (2, 96, NROW))
        sret_t, _ = AR.tile(F32, (RH * 2, DV))
        swkv_t, _ = AR.tile(F32, (16, 128))
        x_t, _ = AR.tile(F32, (KC, NT))
        shift_t, _ = AR.tile(F32, (KC, 1))
        conv_t, _ = AR.tile(F32, (2, FC, 2))
        WB = [AR.tile(BF16, (4096,)) for _ in range(5)]
        wbi = [0]
        base_top = AR.top

        P = lambda name: V(prm_t.ap[:, POFF[name][0]:POFF[name][0] + POFF[name][1]], prm_t.keys)
        C = lambda name: V(cst_t.ap[:, COFF[name][0]:COFF[name][0] + COFF[name][1]], cst_t.keys)
        ident = C("ident")
        ones = C("ones")
        bones = C("bones")

        def wslot():
            wbi[0] = (wbi[0] + 1) % 5
            return WB[wbi[0]]

        def wload(name, piece, dt=BF16, src=None):
            kcn, fw, _ = WSPEC[name] if name in WSPEC else (KC, 128, 0)
            (wv, lo) = wslot()
            v = AR.view(lo, dt, (kcn, fw))
            if src is None:
                dma("sp", v, wscr[name][piece].rearrange("p (a b) -> p a b", b=fw), ikeys=[("ws", name, piece)])
            else:
                dma("sp", v, src.rearrange("p (a b) -> p a b", b=fw))
            return v

        with nc.Block() as block:
            dma("sp", prm_t, prm.ap())
            dma("sp", cst_t, cst.ap())

            m0 = AR.top
            stg = [AR.tile(F32, (6144,)) for _ in range(2)]
            cengs = ["dve", "pool", "act"]
            ci = 0
            mu_v = P("mu")
            for name, (kcn, fw, npc) in WSPEC.items():
                for pc in range(npc):
                    nh = 2 if (kcn * fw * 4 > 24576 or name in ("rkv", "lora1")) else 1
                    hk = kcn // nh
                    for hh in range(nh):
                        (sv, slo) = stg[ci % 2]
                        (bv, blo) = WB[ci % 2]
                        s3 = AR.view(slo, F32, (hk, fw))
                        b3 = AR.view(blo, BF16, (hk, fw))
                        src = win[name][pc].rearrange("p (a b) -> p a b", b=fw)[:, hh * hk:(hh + 1) * hk, :]
                        dst = wscr[name][pc].rearrange("p (a b) -> p a b", b=fw)[:, hh * hk:(hh + 1) * hk, :]
                        dma("sp", s3, src)
                        eng = cengs[ci % 3]
                        mus = None
                        if name == "rkv" and hh == 1:
                            mus = {0: 0, 1: 2, 2: 3}[pc % 3]
                        if name == "lora1" and hh == 1:
                            mus = {0: 1, 1: 4, 2: 5, 3: 5}[pc]
                        if mus is not None:
                            mub = V(mu_v.ap[:, mus * 16:(mus + 1) * 16].unsqueeze(2).to_broadcast([128, 16, fw]), mu_v.keys)
                            tt(b3, s3, mub, ALU.mult, eng="dve" if eng == "act" else eng)
                        else:
                            cp(b3, s3, eng=eng)
                        dma("pool", dst, b3, okeys=[("ws", name, pc)])
                        ci += 1
            for (src_t, dst_t, ncol) in ((l2w, l2scr, 3 * D), (g2w, g2scr, 2 * D)):
                for j in range(ncol // 2048):
                    (sv, slo) = stg[ci % 2]
                    (bv, blo) = WB[ci % 2]
                    s2 = AR.view(slo, F32, (2048,))
                    b2 = AR.view(blo, BF16, (2048,))
                    dma("sp", s2, src_t.ap()[:, j * 2048:(j + 1) * 2048])
                    cp(b2, s2, eng=cengs[ci % 3])
                    dma("pool", dst_t.ap()[:, j * 2048:(j + 1) * 2048], b2, okeys=[("wsl", dst_t.name, j)])
                    ci += 1
            AR.top = m0

            m0 = AR.top
            cv, _ = AR.tile(F32, (KC, NROW))
            dma("sp", cv, cvec.ap().rearrange("p (a b) -> p a b", b=NROW))
            act(cv, cv, AF.Silu)
            for l in range(2):
                for oc in range(96):
                    w = wload("ada", 0, dt=F32, src=ada_w[l * 96 + oc])
                    ps = psum()
                    for kc in range(KC):
                        mm(ps[:, 0:NROW], w[:, kc, :], cv[:, kc, :], kc == 0, kc == KC - 1)
                    bcol = V(prm_t.ap[:, POFF["ada_b"][0] + l * 96 + oc:POFF["ada_b"][0] + l * 96 + oc + 1], prm_t.keys)
                    n = oc // 16
                    if n in (1, 4):
                        ts(mod_t[:, l, oc, :], ps[:, 0:NROW], bcol, 1.0, ALU.add, ALU.add)
                    else:
                        ts(mod_t[:, l, oc, :], ps[:, 0:NROW], bcol, None, ALU.add)
            AR.top = m0

            def modv(l, n, kc, row):
                return mod_t[:, l, n * 16 + kc, row:row + 1]

            def norm_mod(nt, l, nsh, nsc, row, out_bf, last=None):
                m = AR.top
                sqs = [AR.tile(F32, (nt,))[0] for _ in range(3)]
                rstd, _ = AR.tile(F32, (nt,))
                tmps = [AR.tile(F32, (nt,))[0] for _ in range(3)]
                ps = psum()
                for kc in range(KC):
                    sq = sqs[kc % 3]
                    act(sq, x_t[:, kc, 0:nt], AF.Square)
                    mm(ps[:, 0:nt], ones, sq, kc == 0, kc == KC - 1)
                act(rstd, ps[:, 0:nt], AF.Sqrt, bias=C("eps6")[:, 0:1], scale=1.0 / D)
                recip(rstd, rstd)
                for kc in range(KC):
                    tmp = tmps[kc % 3]
                    tt(tmp, x_t[:, kc, 0:nt], rstd, ALU.mult)
                    act(out_bf[:, kc, 0:nt], tmp, AF.Identity, bias=modv(l, nsh, kc, row), scale=modv(l, nsc, kc, row))
                    if last is not None:
                        act(last[:, kc, :], tmp[:, nt - 1:nt], AF.Identity, bias=modv(l, nsh, kc, row), scale=modv(l, nsc, kc, row))
                AR.top = m
                return rstd

            def retention(nt, row, is_sample):
                L = min(128, nt)
                nblk = nt // L
                m = AR.top
                h, _ = AR.tile(BF16, (KC, nt))
                norm_mod(nt, 0, 0, 1, row, h)
                y, _ = AR.tile(BF16, (32, nt))
                qf, _ = AR.tile(F32, (2, nt))
                kf, _ = AR.tile(F32, (2, nt))
                r1, _ = AR.tile(F32, (nt,))
                r2, _ = AR.tile(F32, (nt,))
                r3, _ = AR.tile(F32, (nt,))
                r4, _ = AR.tile(F32, (nt,))
                qT, _ = AR.tile(BF16, (2, nt))
                qcT, _ = AR.tile(F32, (2, nt))
                kT, _ = AR.tile(BF16, (2, nt))
                vT, _ = AR.tile(BF16, (4, nt))
                Vt, _ = AR.tile(BF16, (nblk, DV))
                Kt, _ = AR.tile(BF16, (nblk, DK))
                g, _ = AR.tile(F32, (4, nt))
                oh, _ = AR.tile(F32, (4, nt))
                osq, _ = AR.tile(F32, (nt,))
                mean, _ = AR.tile(F32, (nt,))
                rstd, _ = AR.tile(F32, (nt,))
                sT, _ = AR.tile(BF16, (L,))
                cos = ropet[:, 0, 0:nt]
                sin = ropet[:, 1, 0:nt]
                sfx = "s" if is_sample else ""
                maskT = C("rmask" + sfx)
                cross = C("rcross" + sfx)
                into = C("rinto" + sfx)
                identb = identb_t
                for hd in range(RH):
                    gam_L = (1.0 - 2.0 ** (-5.0 - hd)) ** L
                    for pi in range(6):
                        w = wload("ret_in", hd * 6 + pi)
                        for oc2 in range(2):
                            oc = pi * 2 + oc2
                            ps = psum()
                            for kc in range(KC):
                                mm(ps[:, 0:nt], w[:, kc, oc2 * 128:(oc2 + 1) * 128], h[:, kc, :], kc == 0, kc == KC - 1)
                            if oc < 2:
                                cp(qf[:, oc, :], ps[:, 0:nt], eng="act")
                            elif oc < 4:
                                act(kf[:, oc - 2, :], ps[:, 0:nt], AF.Copy, scale=1.0 / 16.0)
                            elif oc < 8:
                                cp(vT[:, oc - 4, :], ps[:, 0:nt], eng="act")
                            else:
                                act(g[:, oc - 8, :], ps[:, 0:nt], AF.Silu)
                    for (src, dst) in ((qf, qT), (kf, kT)):
                        tt(r1, src[:, 0, :], cos, ALU.mult)
                        tt(r2, src[:, 1, :], sin, ALU.mult)
                        tt(dst[:, 0, :], r1, r2, ALU.subtract)
                        tt(r3, src[:, 0, :], sin, ALU.mult, eng="pool")
                        tt(r4, src[:, 1, :], cos, ALU.mult, eng="pool")
                        tt(dst[:, 1, :], r3, r4, ALU.add, eng="pool")
                    for b in range(nblk):
                        cr = cross[:, hd * L:(hd + 1) * L]
                        for dc in range(2):
                            tt(qcT[:, dc, b * L:(b + 1) * L], qT[:, dc, b * L:(b + 1) * L], cr, ALU.mult)
                    for b in range(nblk):
                        ps = psum()
                        psb = V(ps.ap[:, 0:256].bitcast(BF16), ps.keys)
                        for vc in range(4):
                            tr(psb[0:L, vc * 128:(vc + 1) * 128], vT[:, vc, b * L:(b + 1) * L], identb)
                        cp(Vt[0:L, b, :], psb[0:L, :], eng="act")
                        ps = psum()
                        psb = V(ps.ap[:, 0:256].bitcast(BF16), ps.keys)
                        for dc in range(2):
                            tr(psb[0:L, dc * 128:(dc + 1) * 128], kT[:, dc, b * L:(b + 1) * L], identb)
                        ts(Kt[0:L, b, :], psb[0:L, 0:256], into[0:L, hd:hd + 1], None, ALU.mult)
                    for b in range(nblk):
                        cs = slice(b * L, (b + 1) * L)
                        ps = psum()
                        for dc in range(2):
                            mm(ps[0:L, 0:L], kT[:, dc, cs], qT[:, dc, cs], dc == 0, dc == 1)
                        tt(sT[0:L, :], ps[0:L, 0:L], maskT[0:L, hd * L:(hd + 1) * L], ALU.mult)
                        ps = psum()
                        for vc in range(4):
                            mm(ps[:, vc * L:(vc + 1) * L], Vt[0:L, b, vc * 128:(vc + 1) * 128], sT[0:L, :], True, True)
                        psb2 = psum()
                        for vc in range(4):
                            for dc in range(2):
                                mm(psb2[:, vc * L:(vc + 1) * L], sret_t[:, hd * 2 + dc, vc * 128:(vc + 1) * 128], qcT[:, dc, cs], dc == 0, dc == 1)
                        v3 = lambda p_: V(p_.ap[:, 0:4 * L].rearrange("p (a b) -> p a b", b=L), p_.keys)
                        cp(oh[:, :, cs], v3(ps), eng="act")
                        tt(oh[:, :, cs], oh[:, :, cs], v3(psb2), ALU.add)
                        for dc in range(2):
                            ps = psum()
                            mm(ps[:, 0:DV], Kt[0:L, b, dc * 128:(dc + 1) * 128], Vt[0:L, b, :], True, True)
                            stt(sret_t[:, hd * 2 + dc, :], sret_t[:, hd * 2 + dc, :], gam_L, ps[:, 0:DV], ALU.mult, ALU.add)
                    ps1 = psum()
                    for vc in range(4):
                        mm(ps1[:, 0:nt], ones, oh[:, vc, :], vc == 0, vc == 3)
                    ps2 = psum()
                    for vc in range(4):
                        act(osq, oh[:, vc, :], AF.Square)
                        mm(ps2[:, 0:nt], ones, osq, vc == 0, vc == 3)
                    act(mean, ps1[:, 0:nt], AF.Copy, scale=1.0 / DV)
                    tt(osq, mean, mean, ALU.mult)
                    stt(rstd, ps2[:, 0:nt], 1.0 / DV, osq, ALU.mult, ALU.subtract)
                    act(rstd, rstd, AF.Sqrt, bias=C("eps5")[:, 0:1], scale=1.0)
                    recip(rstd, rstd)
                    gg = P("ret_gn")
                    for vc in range(4):
                        tt(osq, oh[:, vc, :], mean, ALU.subtract)
                        tt(osq, osq, rstd, ALU.mult)
                        stt(y[:, hd * 4 + vc, :], osq, gg[:, hd * 4 + vc:hd * 4 + vc + 1], g[:, vc, :], ALU.mult, ALU.mult)
                for pc in range(16):
                    w = wload("ret_out", pc)
                    ps = psum()
                    for kc in range(32):
                        mm(ps[:, 0:nt], w[:, kc, :], y[:, kc, :], kc == 0, kc == 31)
                    stt(x_t[:, pc, 0:nt], ps[:, 0:nt], modv(0, 2, pc, row), x_t[:, pc, 0:nt], ALU.mult, ALU.add)
                AR.top = m

            def ffn(nt, l, row):
                m = AR.top
                h, _ = AR.tile(BF16, (KC, nt))
                norm_mod(nt, l, 3, 4, row, h)
                a, _ = AR.tile(BF16, (FC, nt))
                ues = [AR.tile(F32, (nt + 2,))[0] for _ in range(2)]
                cvs = [[AR.tile(F32, (nt,))[0] for _ in range(4)] for _ in range(2)]
                cw = P("conv_w")
                cb = P("conv_b")
                for fp in range(22):
                    wg = wload("gu%d" % l, 2 * fp)
                    wu = wload("gu%d" % l, 2 * fp + 1)
                    for oc2 in range(2):
                        fc = fp * 2 + oc2
                        psg = psum()
                        for kc in range(KC):
                            mm(psg[:, 0:nt], wg[:, kc, oc2 * 128:(oc2 + 1) * 128], h[:, kc, :], kc == 0, kc == KC - 1)
                        psu = psum()
                        for kc in range(KC):
                            mm(psu[:, 0:nt], wu[:, kc, oc2 * 128:(oc2 + 1) * 128], h[:, kc, :], kc == 0, kc == KC - 1)
                        ue = ues[fc % 2]
                        cp(ue[:, 0:2], conv_t[:, l, fc, :], eng="pool")
                        cp(ue[:, 2:nt + 2], psg[:, 0:nt], eng="act")
                        cp(conv_t[:, l, fc, :], ue[:, nt:nt + 2], eng="pool")
                        wj = lambda j: cw[:, (l * 3 + j) * FC + fc:(l * 3 + j) * FC + fc + 1]
                        cA, cB, cC, cD = cvs[fc % 2]
                        ts(cA, ue[:, 0:nt], wj(0), cb[:, l * FC + fc:l * FC + fc + 1], ALU.mult, ALU.add)
                        stt(cB, ue[:, 1:nt + 1], wj(1), cA, ALU.mult, ALU.add)
                        stt(cC, ue[:, 2:nt + 2], wj(2), cB, ALU.mult, ALU.add)
                        act(cD, cC, AF.Silu)
                        tt(a[:, fc, :], cD, psu[:, 0:nt], ALU.mult)
                for pc in range(16):
                    ps = psum()
                    for half in range(2):
                        w = wload("dn%d" % l, pc * 2 + half)
                        for kc in range(22):
                            mm(ps[:, 0:nt], w[:, kc, :], a[:, half * 22 + kc, :], half == 0 and kc == 0, half == 1 and kc == 21)
                    stt(x_t[:, pc, 0:nt], ps[:, 0:nt], modv(l, 5, pc, row), x_t[:, pc, 0:nt], ALU.mult, ALU.add)
                AR.top = m

            ropet, _ = AR.tile(F32, (2, NT))
            identb_t, _ = AR.tile(BF16, (128,))
            cp(identb_t, ident)
            base2 = AR.top


            negw0, _ = AR.tile(F32, (16,))
            ts(negw0, P("w0"), -1.0, None, ALU.mult)
            _mo = COFF["mstrict"][0]

            def rwkv(nt, row):
                Cn = min(128, nt)
                nch = nt // Cn
                nlev = int(round(math.log2(Cn)))
                m = AR.top
                h, _ = AR.tile(BF16, (KC, nt))
                xx, _ = AR.tile(BF16, (KC, nt))
                newsh, _ = AR.tile(F32, (KC, 1))
                norm_mod(nt, 1, 0, 1, row, h, last=newsh)
                tt(xx[:, :, 0:1], shift_t, h[:, :, 0:1], ALU.subtract)
                if nt > 1:
                    tt(xx[:, :, 1:nt], h[:, :, 0:nt - 1], h[:, :, 1:nt], ALU.subtract)
                cp(shift_t, newsh, eng="pool")
                y, _ = AR.tile(BF16, (KC, nt))
                lm, _ = AR.tile(BF16, (4, nt))
                l2, _ = AR.tile(BF16, (4, 128))
                F = lambda *sh: AR.tile(F32, sh)[0]
                r_t, k0_t, v_t, e2, asig, g_t, kk, k_t, b_t, bonus, tA, tB = (F(nt) for _ in range(12))
                e2T = F(128)
                cs_, gam, ginv, gprev = F(Cn), F(Cn), F(Cn), F(Cn)
                AR2, BK = F(2, Cn), F(2, Cn)
                AR2f = lambda hs_: V(AR2.ap[hs_].rearrange("p a b -> p (a b)"), AR2.keys)
                Bh, Kh = F(Cn), F(Cn)
                TM = F(4, 128)
                Gb = [F(2 * Cn) for _ in range(2)]
                Gk = [F(2 * Cn) for _ in range(2)]
                g3 = lambda t_: V(t_.ap[0:Cn].rearrange("p (a b) -> p a b", b=Cn), t_.keys)
                Lp = [[F(Cn) for _ in range(2)] for _ in range(2)]
                Pp = [[F(Cn) for _ in range(2)] for _ in range(2)]
                Xp = [[F(128) for _ in range(2)] for _ in range(2)]
                WT, UL, Wfm, UT, Osb, Osq, Yn = F(128), F(128), F(Cn), F(128), F(128), F(128), F(128)
                st1, st2, st3 = F(2), F(2), F(2)
                mask2 = V(cst_t.ap[0:Cn, _mo:_mo + 256].rearrange("p (a b) -> p a b", b=128)[:, :, 0:Cn], cst_t.keys)
                maskT = V(cst_t.ap[0:Cn, COFF["mstrictT"][0]:COFF["mstrictT"][0] + Cn], cst_t.keys)
                mincl = V(cst_t.ap[0:Cn, COFF["mincl"][0]:COFF["mincl"][0] + Cn], cst_t.keys)

                def proj(piece):
                    w = wload("rkv" if piece >= 0 else "lora1", piece if piece >= 0 else -piece - 1)
                    ps = psum()
                    for kc in range(32):
                        rhs = h[:, kc, :] if kc < 16 else xx[:, kc - 16, :]
                        mm(ps[:, 0:nt], w[:, kc, :], rhs, kc == 0, kc == 31)
                    return ps
                ps = proj(-1)
                act(lm[:, 0, :], ps[:, 0:nt], AF.Tanh)
                ps = proj(-2)
                cp(lm[:, 1, :], ps[:, 0:nt], eng="act")
                ps = proj(-3)
                act(lm[:, 2, :], ps[:, 0:nt], AF.Sigmoid)
                ps = proj(-4)
                act(lm[:, 3, :], ps[:, 0:nt], AF.Sigmoid)
                pcol = lambda n, p: V(prm_t.ap[:, POFF[n][0] + p:POFF[n][0] + p + 1], prm_t.keys)
                for p in range(16):
                    psl = slice(p * 128, (p + 1) * 128)
                    dma("sp", l2[:, 0, :], l2scr.ap()[:, p * 128:(p + 1) * 128], ikeys=[("wsl", l2scr.name, 0)])
                    dma("sp", l2[:, 1, :], l2scr.ap()[:, D + p * 128:D + (p + 1) * 128], ikeys=[("wsl", l2scr.name, 1)])
                    dma("sp", l2[:, 2:4, :], g2scr.ap().rearrange("p (a b) -> p a b", b=D)[:, :, p * 128:(p + 1) * 128],
                        ikeys=[("wsl", g2scr.name, 0), ("wsl", g2scr.name, 1)])
                    ps = proj(3 * p + 0)
                    cp(r_t, ps[:, 0:nt], eng="act")
                    ps = proj(3 * p + 1)
                    cp(k0_t, ps[:, 0:nt], eng="act")
                    ps = proj(3 * p + 2)
                    cp(v_t, ps[:, 0:nt], eng="act")
                    ps = psum()
                    mm(ps[:, 0:nt], l2[:, 0, :], lm[:, 0, :], True, True)
                    act(e2, ps[:, 0:nt], AF.Exp, bias=negw0[:, p:p + 1], scale=-1.0)
                    act(e2, e2, AF.Ln, bias=1.0)
                    act(e2, e2, AF.Exp, bias=C("mhalf")[:, 0:1], scale=-1.0)
                    ps = psum()
                    mm(ps[:, 0:nt], l2[:, 1, :], lm[:, 1, :], True, True)
                    act(asig, ps[:, 0:nt], AF.Sigmoid, bias=pcol("a0", p))
                    ps = psum()
                    mm(ps[:, 0:nt], l2[:, 2, :], lm[:, 2, :], True, False)
                    mm(ps[:, 0:nt], l2[:, 3, :], lm[:, 3, :], False, True)
                    cp(g_t, ps[:, 0:nt], eng="act")
                    ts(kk, k0_t, pcol("k_k", p), None, ALU.mult)
                    act(tA, kk, AF.Square)
                    ps = psum()
                    mm(ps[:, 0:nt], bones, tA, True, True)
                    act(tA, ps[:, 0:nt], AF.Sqrt)
                    ts(tA, tA, 1e-12, None, ALU.max)
                    recip(tA, tA)
                    tt(kk, kk, tA, ALU.mult)
                    ts(tA, asig, pcol("k_a", p), pcol("k_a", p), ALU.mult, ALU.subtract)
                    stt(k_t, tA, 1.0, k0_t, ALU.add, ALU.mult)
                    tt(b_t, kk, asig, ALU.mult)
                    stt(tA, r_t, pcol("r_k", p), k_t, ALU.mult, ALU.mult)
                    ps = psum()
                    mm(ps[:, 0:nt], bones, tA, True, True)
                    tt(bonus, ps[:, 0:nt], v_t, ALU.mult)
                    for c in range(nch if RW_STAGE >= 2 else 0):
                        cs = slice(c * Cn, (c + 1) * Cn)
                        ps = psum()
                        tr(ps[0:Cn, 0:128], e2[:, cs], ident)
                        cp(e2T[0:Cn, :], ps[0:Cn, 0:128])
                        ps = psum()
                        mm(ps[:, 0:Cn], e2T[0:Cn, :], mincl, True, True)
                        cp(cs_, ps[:, 0:Cn])
                        act(gam, cs_, AF.Exp, scale=-1.0)
                        act(ginv, cs_, AF.Exp)
                        tt(gprev, cs_, e2[:, cs], ALU.subtract)
                        act(gprev, gprev, AF.Exp, scale=-1.0)
                        stt(AR2[:, 0, :], kk[:, cs], -1.0, gprev, ALU.mult, ALU.mult)
                        tt(AR2[:, 1, :], r_t[:, cs], gam, ALU.mult)
                        tt(BK[:, 0, :], b_t[:, cs], ginv, ALU.mult)
                        tt(BK[:, 1, :], k_t[:, cs], ginv, ALU.mult)
                        gC = gam[:, Cn - 1:Cn]
                        ts(Bh, BK[:, 0, :], gC, None, ALU.mult)
                        ts(Kh, BK[:, 1, :], gC, None, ALU.mult)
                        ps = psum()
                        tr(ps[0:Cn, 0:128], v_t[:, cs], ident)
                        tr(ps[0:Cn, 128:256], Bh, ident)
                        tr(ps[0:Cn, 256:384], Kh, ident)
                        tr(ps[0:Cn, 384:512], AR2[:, 0, :], ident)
                        cp(TM[0:Cn, 0:2, :], V(ps.ap[0:Cn, 0:256].rearrange("p (a b) -> p a b", b=128), ps.keys), eng="act")
                        cp(TM[0:Cn, 2:4, :], V(ps.ap[0:Cn, 256:512].rearrange("p (a b) -> p a b", b=128), ps.keys))
                        Vtm, Bhtm, Khtm, Attm = TM[0:Cn, 0, :], TM[0:Cn, 1, :], TM[0:Cn, 2, :], TM[0:Cn, 3, :]
                        if RW_STAGE < 3:
                            continue
                        hst = []
                        for hh in range(2):
                            hs = slice(64 * hh, 64 * hh + 64)
                            ar_h = AR2f(hs)
                            ps = psum()
                            mm(ps[0:Cn, 0:2 * Cn], BK[hs, 0, :], ar_h, True, True)
                            tt(g3(Gb[hh]), V(ps.ap[0:Cn, 0:2 * Cn].rearrange("p (a b) -> p a b", b=Cn), ps.keys), mask2, ALU.mult)
                            ps = psum()
                            mm(ps[0:Cn, 0:2 * Cn], BK[hs, 1, :], ar_h, True, True)
                            tt(g3(Gk[hh]), V(ps.ap[0:Cn, 0:2 * Cn].rearrange("p (a b) -> p a b", b=Cn), ps.keys), mask2, ALU.mult)
                            ps = psum()
                            mm(ps[0:Cn, 0:Cn], AR2[hs, 0, :], BK[hs, 0, :], True, True)
                            tt(Lp[hh][0][0:Cn], ps[0:Cn, 0:Cn], maskT, ALU.mult)
                            hc = hs
                            ps = psum()
                            mm(ps[0:Cn, 0:64], Gk[hh][0:Cn, 0:Cn], Vtm[:, hc], True, True)
                            X = Xp[hh][0]
                            cp(X[0:Cn, 0:64], Attm[:, hc], eng="pool")
                            cp(X[0:Cn, 64:128], ps[0:Cn, 0:64], eng="act")
                            hst.append([Gb[hh][0:Cn, 0:Cn], Lp[hh][0][0:Cn], X])
                        for lv in range(nlev):
                            for hh in range(2):
                                hc = slice(64 * hh, 64 * hh + 64)
                                Pc, Lc, X = hst[hh]
                                ps = psum()
                                mm(ps[0:Cn, 0:128], Pc, X[0:Cn], True, True)
                                if lv == nlev - 1:
                                    tt(WT[0:Cn, hc], X[0:Cn, 0:64], ps[0:Cn, 0:64], ALU.add)
                                    tt(UL[0:Cn, hc], X[0:Cn, 64:128], ps[0:Cn, 64:128], ALU.add)
                                else:
                                    Xn = Xp[hh][(lv + 1) % 2]
                                    tt(Xn[0:Cn], X[0:Cn], ps[0:Cn, 0:128], ALU.add)
                                    ps1 = psum()
                                    mm(ps1[0:Cn, 0:Cn], Lc, Pc, True, True)
                                    ps2 = psum()
                                    mm(ps2[0:Cn, 0:Cn], Pc, Lc, True, True)
                                    Pn = Pp[hh][lv % 2][0:Cn]
                                    Ln = Lp[hh][(lv + 1) % 2][0:Cn]
                                    cp(Pn, ps1[0:Cn, 0:Cn], eng="act")
                                    cp(Ln, ps2[0:Cn, 0:Cn])
                                    hst[hh] = [Pn, Ln, Xn]
                        ps = psum()
                        tr(ps[:, 0:Cn], WT[0:Cn], ident[0:Cn, 0:Cn])
                        cp(Wfm, ps[:, 0:Cn], eng="act")
                        if RW_STAGE < 4:
                            continue
                        for hh in range(2):
                            hs = slice(64 * hh, 64 * hh + 64)
                            ps = psum()
                            mm(ps[0:Cn, 0:64], Wfm[hs, :], swkv_t[hs, p, hs], True, True)
                            tt(UT[0:Cn, hs], ps[0:Cn, 0:64], UL[0:Cn, hs], ALU.add)
                        for hh in range(2):
                            hs = slice(64 * hh, 64 * hh + 64)
                            psa = psum()
                            mm(psa[0:Cn, 0:64], AR2[hs, 1, :], swkv_t[hs, p, hs], True, True)
                            cp(Osb[0:Cn, hs], psa[0:Cn, 0:64], eng="act")
                        ps = psum()
                        for hh in range(2):
                            hs = slice(64 * hh, 64 * hh + 64)
                            mm(ps[0:Cn, hs], Gb[hh][0:Cn, Cn:2 * Cn], UT[0:Cn, hs], True, False)
                            mm(ps[0:Cn, hs], Gk[hh][0:Cn, Cn:2 * Cn], Vtm[:, hs], False, True)
                        tt(Osb[0:Cn], Osb[0:Cn], ps[0:Cn, 0:128], ALU.add)
                        ps = psum()
                        mm(ps[:, 0:128], Bhtm, UT[0:Cn], True, False)
                        mm(ps[:, 0:128], Khtm, Vtm, False, True)
                        stt(swkv_t[:, p, :], swkv_t[:, p, :], gC, ps[:, 0:128], ALU.mult, ALU.add)
                        if RW_STAGE < 5:
                            continue
                        O3 = V(Osb.ap[0:Cn].rearrange("p (a b) -> p a b", b=64), Osb.keys)
                        Q3 = V(Osq.ap[0:Cn].rearrange("p (a b) -> p a b", b=64), Osq.keys)
                        Y3 = V(Yn.ap[0:Cn].rearrange("p (a b) -> p a b", b=64), Yn.keys)
                        S.op("dve", lambda e, o=st1, i=O3: e.tensor_reduce(out=o.ap[0:Cn], in_=i.ap, axis=AX.X, op=ALU.add), reads=[Osb], writes=[st1])
                        act(Osq[0:Cn], Osb[0:Cn], AF.Square)
                        S.op("dve", lambda e, o=st2, i=Q3: e.tensor_reduce(out=o.ap[0:Cn], in_=i.ap, axis=AX.X, op=ALU.add), reads=[Osq], writes=[st2])
                        ts(st1[0:Cn], st1[0:Cn], 1.0 / 64, None, ALU.mult)
                        tt(st3[0:Cn], st1[0:Cn], st1[0:Cn], ALU.mult)
                        stt(st2[0:Cn], st2[0:Cn], 1.0 / 64, st3[0:Cn], ALU.mult, ALU.subtract)
                        act(st2[0:Cn], st2[0:Cn], AF.Sqrt, bias=C("epsw")[0:Cn, 0:1])
                        recip(st2[0:Cn], st2[0:Cn])
                        bc = lambda t_: V(t_.ap[0:Cn].unsqueeze(2).to_broadcast([Cn, 2, 64]), t_.keys)
                        tt(Y3, O3, bc(st1), ALU.subtract)
                        tt(Y3, Y3, bc(st2), ALU.mult)
                        ps = psum()
                        tr(ps[:, 0:Cn], Yn[0:Cn], ident[0:Cn, 0:Cn])
                        stt(tA[:, 0:Cn], ps[:, 0:Cn], pcol("rw_gn", p), bonus[:, cs], ALU.mult, ALU.add)
                        tt(y[:, p, cs], tA[:, 0:Cn], g_t[:, cs], ALU.mult)
                for pc in range(8):
                    w = wload("w_o", pc)
                    for oc2 in range(2):
                        oc = pc * 2 + oc2
                        ps = psum()
                        for kc in range(KC):
                            mm(ps[:, 0:nt], w[:, kc, oc2 * 128:(oc2 + 1) * 128], y[:, kc, :], kc == 0, kc == KC - 1)
                        stt(x_t[:, oc, 0:nt], ps[:, 0:nt], modv(1, 2, oc, row), x_t[:, oc, 0:nt], ALU.mult, ALU.add)
                AR.top = m

            def final_out(nt, dst):
                m = AR.top
                sq, _ = AR.tile(F32, (nt,))
                rstd, _ = AR.tile(F32, (nt,))
                o, _ = AR.tile(F32, (KC, nt))
                ps = psum()
                for kc in range(KC):
                    act(sq, x_t[:, kc, 0:nt], AF.Square)
                    mm(ps[:, 0:nt], ones, sq, kc == 0, kc == KC - 1)
                act(rstd, ps[:, 0:nt], AF.Sqrt, bias=C("eps6")[:, 0:1], scale=1.0 / D)
                recip(rstd, rstd)
                fin = P("fin")
                for kc in range(KC):
                    stt(o[:, kc, :], x_t[:, kc, 0:nt], fin[:, kc:kc + 1], rstd, ALU.mult, ALU.mult)
                S.dma("pool", lambda e: e.dma_start(out=dst(), in_=o.ap), reads=S._keys([o]))
                AR.top = m

            dmy = nc.dram_tensor("dmy", [2, 64], F32)

            def pad_dmas(st0):
                for q in ("sp", "pool"):
                    m = S.dcnt[q] - st0["dcnt"][q]
                    for _ in range((-m) % NDMA_SEM):
                        S.dma(q, lambda e: e.dma_start(out=dmy.ap()[0:1, 0:16], in_=dmy.ap()[1:2, 0:16]))

            def tidx(ti):
                return lambda: (S.loop_var if S.loop_var is not None else ti)

            def init_states(is_sample, si):
                if is_sample:
                    dma("sp", sret_t, st_ret[si].rearrange("p (a b) -> p a b", b=DV))
                    dma("sp", swkv_t, st_wkv[si].rearrange("p (a b) -> p a b", b=128))
                    dma("sp", shift_t, st_shift[si].rearrange("p (a b) -> p a b", b=1))
                    dma("sp", conv_t, st_conv[si].rearrange("p (a b c) -> p a b c", b=FC, c=2))
                else:
                    memset(sret_t, 0.0)
                    memset(swkv_t, 0.0)
                    memset(shift_t, 0.0)
                    memset(conv_t, 0.0)

            def run_tile(is_sample, si, ti):
                nt = TS if is_sample else NT
                row = 1 + si if is_sample else 0
                psi[0] = 0
                wbi[0] = 0
                if is_sample:
                    dma("sp", x_t[:, :, 0:nt], xs_in[si].rearrange("p (a b) -> p a b", b=nt))
                    dma("sp", ropet[:, :, 0:nt], rope_s.ap().rearrange("p (a b) -> p a b", b=nt))
                else:
                    tv = tidx(ti)
                    S.dma("sp", lambda e: e.dma_start(out=x_t.ap, in_=x_in.ap()[tv()].rearrange("p (a b) -> p a b", b=NT)), writes=S._keys([x_t]))
                    S.dma("sp", lambda e: e.dma_start(out=ropet.ap, in_=rope.ap()[tv()].rearrange("p (a b) -> p a b", b=NT)), writes=S._keys([ropet]))
                retention(nt, row, is_sample)
                ffn(nt, 0, row)
                if STOP_AFTER >= 3:
                    rwkv(nt, row)
                if STOP_AFTER >= 4:
                    ffn(nt, 1, row)
                if is_sample:
                    final_out(nt, lambda: ys_out[si].rearrange("p (a b) -> p a b", b=nt))
                else:
                    tv = tidx(ti)
                    final_out(nt, lambda: y_out.ap()[tv()].rearrange("p (a b) -> p a b", b=NT))

            def write_states(oi):
                dma("pool", o_ret[oi].rearrange("p (a b) -> p a b", b=DV), sret_t)
                dma("pool", o_wkv[oi].rearrange("p (a b) -> p a b", b=128), swkv_t)
                dma("pool", o_shift[oi].rearrange("p (a b) -> p a b", b=1), shift_t)
                dma("pool", o_conv[oi].rearrange("p (a b c) -> p a b c", b=FC, c=2), conv_t)

            for si in range(NSP):
                init_states(True, si)
                run_tile(True, si, 0)
                write_states(1 + si)
            init_states(False, 0)
            n_run = NTILE_RUN
            if n_run <= 4:
                for ti in range(n_run):
                    st0 = S.state()
                    run_tile(False, 0, ti)
                    pad_dmas(st0)
            else:
                k = 0
                prev = None
                while True:
                    st0 = S.state()
                    run_tile(False, 0, k)
                    pad_dmas(st0)
                    st1 = S.state()
                    delta = S.deltas(st0, st1)
                    cur = S.norm_tile(st0, st1, k, delta)
                    if prev is not None and cur == prev[0] and delta == prev[1]:
                        S.restore(st0)
                        S.set_loop(prev[2], st0, k - 1, n_run, delta)
                        print("steady state at tile", k - 1)
                        break
                    prev = (cur, delta, st0)
                    k += 1
                    assert k < 8, "no steady state"
            S.emit(block)
        with nc.Block() as block2:
            write_states(0)
            S.emit(block2)
        print("ops recorded:", S.nops, "arena top", AR.top, "peak", AR.peak)
    return nc


POFF = {}
COFF = {}
_o = 0
for _n, _w in (("ada_b", 192), ("ret_gn", 32), ("mu", 96), ("w0", 16), ("a0", 16), ("k_k", 16), ("k_a", 16),
               ("r_k", 16), ("rw_gn", 16), ("conv_w", 2 * 3 * FC), ("conv_b", 2 * FC), ("fin", 16)):
    POFF[_n] = (_o, _w)
    _o += _w
NPRM = _o
_o = 0
for _n, _w in (("ident", 128), ("ones", 128), ("bones", 128), ("eps6", 1), ("eps5", 1), ("epsw", 1), ("mhalf", 1),
               ("rmask", 8 * 128), ("rcross", 8 * 128), ("rinto", 8),
               ("rmasks", 8 * 16), ("rcrosss", 8 * 16), ("rintos", 8),
               ("mstrict", 128), ("mincl", 128), ("mstrictT", 128)):
    COFF[_n] = (_o, _w)
    _o += _w
NCST = _o
NTILE_RUN = NTILE


def _consts():
    c = np.zeros((128, NCST), np.float32)

    def put(n, a):
        o, w = COFF[n]
        c[:a.shape[0], o:o + w] = a
    put("ident", np.eye(128, dtype=np.float32))
    put("ones", np.ones((128, 128), np.float32))
    bo = np.zeros((128, 128), np.float32)
    bo[:64, :64] = 1
    bo[64:, 64:] = 1
    put("bones", bo)
    put("eps6", np.full((128, 1), 1e-6, np.float32))
    put("eps5", np.full((128, 1), 1e-5, np.float32))
    put("epsw", np.full((128, 1), 64e-5, np.float32))
    put("mhalf", np.full((128, 1), -0.5, np.float32))
    gam = 1.0 - 2.0 ** (-5.0 - np.arange(8, dtype=np.float64))
    for sfx, L in (("", 128), ("s", 16)):
        idx = np.arange(L, dtype=np.float64)
        diff = idx[None, :] - idx[:, None]
        mk = np.where(diff >= 0, gam[:, None, None] ** np.maximum(diff, 0)[None], 0.0)
        put("rmask" + sfx, mk.transpose(1, 0, 2).reshape(L, 8 * L).astype(np.float32))
        cr = gam[:, None] ** (idx[None, :] + 1.0)
        put("rcross" + sfx, np.broadcast_to(cr.reshape(1, 8 * L), (128, 8 * L)).astype(np.float32))
        it = gam[:, None] ** (L - 1.0 - idx[None, :])
        put("rinto" + sfx, it.T.astype(np.float32))
    s = np.arange(128)
    put("mstrict", (s[:, None] < s[None, :]).astype(np.float32))
    put("mincl", (s[:, None] <= s[None, :]).astype(np.float32))
    put("mstrictT", (s[:, None] > s[None, :]).astype(np.float32))
    return c


def _rope(pos):
    half = 128
    inv = 10000.0 ** (-np.arange(half, dtype=np.float32) / half)
    ang = pos.astype(np.float32)[None, :] * inv[:, None]
    return np.cos(ang).astype(np.float32), np.sin(ang).astype(np.float32)


_NC_CACHE = {}


def kernel(**inp):
    f = lambda k: np.asarray(inp[k], np.float32)
    if "nc" not in _NC_CACHE:
        _NC_CACHE["nc"] = build_program()
    nc = _NC_CACHE["nc"]
    sh = {}
    aw = f("ada_w")
    sh["ada_w"] = np.concatenate([_pieces(aw[l], 128) for l in range(2)], 0)
    wi = f("ret_w_in")[0]
    cols = []
    for h in range(RH):
        cols += [wi[:, h * DK:(h + 1) * DK], wi[:, 2048 + h * DK:2048 + (h + 1) * DK],
                 wi[:, 4096 + h * DV:4096 + (h + 1) * DV], wi[:, 8192 + h * DV:8192 + (h + 1) * DV]]
    sh["w_ret_in"] = _pieces(np.concatenate(cols, 1), 256)
    sh["w_ret_out"] = _pieces(f("ret_w_out")[0], 128)
    for l in range(2):
        g_ = _pieces(f("ffn_w_gate")[l], 256)
        u_ = _pieces(f("ffn_w_up")[l], 256)
        gu = np.empty((44,) + g_.shape[1:], np.float32)
        gu[0::2] = g_
        gu[1::2] = u_
        sh["w_gu%d" % l] = gu
        dn = _pieces(f("ffn_w_down")[l], 128)
        sh["w_dn%d" % l] = np.ascontiguousarray(dn.reshape(16, 128, 2, 22 * 128).transpose(0, 2, 1, 3)).reshape(32, 128, 22 * 128)
    st2 = lambda w: np.concatenate([w, w], 0)
    r_, k_, v_ = (_pieces(st2(f(n)[0]), 128) for n in ("rwkv_w_r", "rwkv_w_k", "rwkv_w_v"))
    rkv = np.empty((48,) + r_.shape[1:], np.float32)
    rkv[0::3], rkv[1::3], rkv[2::3] = r_, k_, v_
    sh["w_rkv"] = rkv
    pad = lambda w: np.concatenate([w, np.zeros((w.shape[0], 128 - w.shape[1]), np.float32)], 1)
    sh["w_lora1"] = np.concatenate([_pieces(st2(pad(f("rwkv_w1")[0])), 128), _pieces(st2(pad(f("rwkv_a1")[0])), 128),
                                    _pieces(st2(f("rwkv_g1")[0]), 128)], 0)
    sh["w_w_o"] = _pieces(f("rwkv_w_o")[0], 256)
    l2 = np.zeros((128, 3 * D), np.float32)
    l2[:96, 0:D] = f("rwkv_w2")[0]
    l2[:96, D:2 * D] = f("rwkv_a2")[0]
    sh["l2w"] = l2
    g2 = f("rwkv_g2")[0]
    sh["g2w"] = np.concatenate([g2[0:128], g2[128:256]], 1)
    prm = np.zeros((128, NPRM), np.float32)

    def putp(n, a):
        o, w = POFF[n]
        prm[:, o:o + w] = a
    ab = f("ada_b")
    putp("ada_b", np.concatenate([_fm(ab[0]), _fm(ab[1])], 1))
    putp("ret_gn", _fm(f("ret_gn_gain")[0]))
    putp("mu", np.concatenate([_fm(f("rwkv_mu")[0][i]) for i in range(6)], 1))
    for n, k in (("w0", "rwkv_w0"), ("a0", "rwkv_a0"), ("k_k", "rwkv_k_k"), ("k_a", "rwkv_k_a"), ("rw_gn", "rwkv_gn_gain")):
        putp(n, _fm(f(k)[0]))
    putp("r_k", _fm(f("rwkv_r_k")[0].reshape(-1)))
    cw = f("ffn_conv_w")
    putp("conv_w", np.concatenate([_fm(cw[l, j]) for l in range(2) for j in range(3)], 1))
    cb = f("ffn_conv_b")
    putp("conv_b", np.concatenate([_fm(cb[l]) for l in range(2)], 1))
    putp("fin", _fm(f("final_gain")))
    sh["prm"] = prm
    sh["cst"] = _consts()
    cosp, sinp = _rope(np.arange(SEQ))
    rp = np.stack([cosp, sinp], 1)
    sh["rope"] = np.ascontiguousarray(rp.reshape(128, 2, NTILE, NT).transpose(2, 0, 1, 3)).reshape(NTILE, 128, 2 * NT)
    coss, sins = _rope(PAST + np.arange(TS))
    sh["rope_s"] = np.stack([coss, sins], 1).reshape(128, 2 * TS)
    xp, xs = f("x_prompt"), f("x_sample")
    cpv, csv = f("c_prompt"), f("c_sample")
    xt_cache = {}
    in_maps = []
    for c in range(NCORES):
        sq = c % 2
        sidx = [c * NSP + i for i in range(NSP)]
        if sq not in xt_cache:
            xT = xp[sq].T
            xt_cache[sq] = np.ascontiguousarray(xT.reshape(KC, 128, NTILE, NT).transpose(2, 1, 0, 3)).reshape(NTILE, 128, KC * NT)
        m = dict(sh)
        m["x_in"] = xt_cache[sq]
        m["xs_in"] = np.stack([np.ascontiguousarray(xs[j].T.reshape(KC, 128, TS).transpose(1, 0, 2)).reshape(128, KC * TS) for j in sidx], 0)
        m["cvec"] = np.stack([_fm(cpv[sq])] + [_fm(csv[j]) for j in sidx], 2).reshape(128, KC * NROW)
        l_ret, l_wkv, l_sh, l_cv = [], [], [], []
        for j in sidx:
            sr = f("state_ret")[0, j]
            l_ret.append(np.ascontiguousarray(sr.reshape(RH, 2, 128, DV).transpose(2, 0, 1, 3)).reshape(128, RH * 2 * DV))
            sw = f("state_rwkv_wkv")[0, j]
            t = np.zeros((16, 2, 64, 2, 64), np.float32)
            swT = sw.transpose(0, 2, 1).reshape(16, 2, 64, 64)
            for hh in range(2):
                t[:, hh, :, hh, :] = swT[:, hh]
            l_wkv.append(np.ascontiguousarray(t.transpose(1, 2, 0, 3, 4)).reshape(128, 16 * 128))
            l_sh.append(_fm(f("state_rwkv_shift")[0, j]))
            sc = f("state_ffn_conv")[:, j]
            l_cv.append(np.ascontiguousarray(sc.reshape(2, 2, FC, 128).transpose(3, 0, 2, 1)).reshape(128, 2 * FC * 2))
        m["st_ret"], m["st_wkv"], m["st_shift"], m["st_conv"] = (np.stack(l, 0) for l in (l_ret, l_wkv, l_sh, l_cv))
        in_maps.append(m)
    res = run_bass_kernel_spmd(nc, in_maps, core_ids=list(range(NCORES)))
    R = res.results
    def unx(a, nt, ntile):
        return a.reshape(ntile, 128, KC, nt).transpose(0, 3, 2, 1).reshape(ntile * nt, D)
    y_prompt = np.stack([unx(R[b]["y_out"], NT, NTILE) for b in range(2)], 0)
    y_sample = np.stack([unx(R[j // NSP]["ys_out"][j % NSP], TS, 1) for j in range(8)], 0)

    def un_ret(a):
        return a.reshape(128, RH, 2, DV).transpose(1, 2, 0, 3).reshape(RH, DK, DV)

    def un_wkv(a):
        t = a.reshape(2, 64, 16, 2, 64)
        o = np.stack([t[hh, :, :, hh, :] for hh in range(2)], 0)
        return o.transpose(2, 0, 3, 1).reshape(32, 64, 64)

    def un_conv(a):
        return a.reshape(128, 2, FC, 2).transpose(1, 3, 2, 0).reshape(2, 2, FF)

    def un_vec(a):
        return a.T.reshape(-1)
    outs_p = [np.stack([fn(R[b][k][0]) for b in range(2)], 0) for k, fn in
              (("o_ret", un_ret), ("o_wkv", un_wkv), ("o_shift", un_vec))]
    outs_s = [np.stack([fn(R[j // NSP][k][1 + j % NSP]) for j in range(8)], 0) for k, fn in
              (("o_ret", un_ret), ("o_wkv", un_wkv), ("o_shift", un_vec))]
    pc = np.stack([un_conv(R[b]["o_conv"][0]) for b in range(2)], 1)
    sc_ = np.stack([un_conv(R[j // NSP]["o_conv"][1 + j % NSP]) for j in range(8)], 1)
    return (y_prompt.astype(np.float32), y_sample.astype(np.float32),
            outs_p[0][None].astype(np.float32), outs_p[1][None].astype(np.float32), outs_p[2][None].astype(np.float32), pc.astype(np.float32),
            outs_s[0][None].astype(np.float32), outs_s[1][None].astype(np.float32), outs_s[2][None].astype(np.float32), sc_.astype(np.float32))
```
